# Optimizing a Trainium2 kernel written in Bass

```python
import math
import jax, jax.numpy as jnp
from jax import lax
import numpy as np

D_MODEL = 1024
BATCH = 8
SEQ = 2048
DEPTH = 1
DEC_BATCH = 32
DEC_SEQ = 8
PAST_LEN = 16384
PAGE_SIZE = 128

N_HEADS_A = 12
HEAD_DIM_A = 64
D_ATTN = N_HEADS_A * HEAD_DIM_A
DILATED_GROUPS = ((128, 1), (512, 4), (2048, 16))
WINDOW_MAX = 2048
QUERY_BLOCK = 64
N_BUCKETS = 32
MAX_DISTANCE = WINDOW_MAX
D_SSM = 2 * D_MODEL
SSM_HEAD_DIM = 64
N_SSM_HEADS = D_SSM // SSM_HEAD_DIM
SSM_GROUPS = 8
D_STATE = 128
CONV_WIDTH = 4
D_CONV = D_SSM + 2 * SSM_GROUPS * D_STATE
SSD_CHUNK = 128
DT_MIN = 0.001
DT_MAX = 0.1
SPLIT_SIZES = (D_ATTN, D_ATTN, D_ATTN, D_ATTN, D_SSM, D_CONV, N_SSM_HEADS, D_MODEL, D_MODEL)
D_IN_PROJ = sum(SPLIT_SIZES)
SPLIT_POINTS = tuple(int(s) for s in np.cumsum(SPLIT_SIZES)[:-1])

kernel_name = 'hybrid_dilated_attn_mamba2_gated_decode_step'


def _rmsnorm(x, g, eps=1e-6):
    xf = x.astype(jnp.float32)
    y = xf * lax.rsqrt(jnp.mean(xf * xf, axis=-1, keepdims=True) + eps)
    return (y * g.astype(jnp.float32)).astype(x.dtype)


def _group_rmsnorm(y, g, eps=1e-5):
    shp = y.shape
    yg = y.reshape(shp[:-1] + (SSM_GROUPS, shp[-1] // SSM_GROUPS))
    yg = yg * lax.rsqrt(jnp.mean(yg * yg, axis=-1, keepdims=True) + eps)
    return yg.reshape(shp) * g.astype(jnp.float32)


def _t5_bucket(dist):
    max_exact = N_BUCKETS // 2
    d = np.maximum(dist, 1).astype(np.float32)
    large = max_exact + (np.log(d / max_exact) / math.log(MAX_DISTANCE / max_exact)
                         * (N_BUCKETS - max_exact)).astype(np.int32)
    large = np.minimum(large, N_BUCKETS - 1)
    return np.where(dist < max_exact, dist, large).astype(np.int32)


def _dilated_attention(q, k_all, v_all, rel_bias, q_pos0, k_pos0):
    bsz, lq, n_heads, head_dim = q.shape
    lk = k_all.shape[1]
    qb = QUERY_BLOCK if lq % QUERY_BLOCK == 0 else lq
    n_blocks = lq // qb
    scale = 1.0 / math.sqrt(head_dim)
    offsets = [np.arange(0, w + 1, d, dtype=np.int32) for (w, d) in DILATED_GROUPS]
    biases = [rel_bias[_t5_bucket(o)].T.astype(jnp.float32) for o in offsets]

    def one_block(blk):
        start = blk * qb
        qs = lax.dynamic_slice_in_dim(q, start, qb, axis=1).astype(jnp.float32) * scale
        qpos = q_pos0 + start + jnp.arange(qb, dtype=jnp.int32)
        lses, outs = [], []
        for off, bias in zip(offsets, biases):
            kpos = qpos[:, None] - off[None, :]
            idx = kpos - k_pos0
            valid = (kpos >= 0) & (idx >= 0)
            idx = jnp.clip(idx, 0, lk - 1)
            kg = jnp.take(k_all, idx, axis=1).astype(jnp.float32)
            vg = jnp.take(v_all, idx, axis=1).astype(jnp.float32)
            s = jnp.einsum('bqhe,bqjhe->bhqj', qs, kg) + bias[None, :, None, :]
            s = jnp.where(valid[None, None], s, -jnp.inf)
            lse = jax.nn.logsumexp(s, axis=-1)
            p = jnp.exp(s - lse[..., None])
            outs.append(jnp.einsum('bhqj,bqjhe->bqhe', p, vg))
            lses.append(lse)
        mix = jax.nn.softmax(jnp.stack(lses, axis=0), axis=0)
        o = jnp.einsum('gbhq,gbqhe->bqhe', mix, jnp.stack(outs, axis=0))
        return o.astype(q.dtype)

    out = lax.map(one_block, jnp.arange(n_blocks))
    return jnp.moveaxis(out, 0, 1).reshape(bsz, lq, n_heads, head_dim)


def _causal_dwconv(xbc, conv_past, conv_w, conv_b):
    xp = jnp.concatenate([conv_past.astype(xbc.dtype), xbc], axis=1)
    length = xbc.shape[1]
    y = conv_b
    for tap in range(CONV_WIDTH):
        y = y + xp[:, tap:tap + length] * conv_w[tap]
    return y, xp[:, -(CONV_WIDTH - 1):]


def _ssd(x, dt, a, bm, cm, h0):
    f32 = jnp.float32
    bsz, length, n_heads, p_dim = x.shape
    n_groups, n_state = bm.shape[2], bm.shape[3]
    r = n_heads // n_groups
    qc = SSD_CHUNK if length % SSD_CHUNK == 0 else length
    nc = length // qc
    xc = x.astype(f32).reshape(bsz, nc, qc, n_groups, r, p_dim)
    dtc = dt.astype(f32).reshape(bsz, nc, qc, n_groups, r)
    bc = bm.astype(f32).reshape(bsz, nc, qc, n_groups, n_state)
    cc = cm.astype(f32).reshape(bsz, nc, qc, n_groups, n_state)
    acum = jnp.cumsum(dtc * a.astype(f32).reshape(n_groups, r), axis=2)
    xdt = xc * dtc[..., None]
    causal = np.tril(np.ones((qc, qc), dtype=bool))[:, :, None, None]
    seg = acum[:, :, :, None] - acum[:, :, None, :]
    decay = jnp.exp(jnp.where(causal, seg, -jnp.inf))
    cb = jnp.einsum('bclgn,bcsgn->bclsg', cc, bc)
    y_diag = jnp.einsum('bclsgr,bcsgrp->bclgrp', cb[..., None] * decay, xdt)
    wx = xdt * jnp.exp(acum[:, :, -1:] - acum)[..., None]
    states = jnp.einsum('bclgn,bclgrp->bcgrpn', bc, wx)
    chunk_decay = jnp.exp(acum[:, :, -1])

    def step(h, inp):
        s_c, d_c = inp
        return d_c[..., None, None] * h + s_c, h

    h_init = h0.astype(f32).reshape(bsz, n_groups, r, p_dim, n_state)
    h_final, h_prev = lax.scan(step, h_init, (jnp.moveaxis(states, 1, 0), jnp.moveaxis(chunk_decay, 1, 0)))
    h_prev = jnp.moveaxis(h_prev, 0, 1)
    y_off = jnp.einsum('bclgn,bcgrpn->bclgrp', cc, h_prev) * jnp.exp(acum)[..., None]
    y = (y_diag + y_off).reshape(bsz, length, n_heads, p_dim)
    return y, h_final.reshape(bsz, n_heads, p_dim, n_state)


def _layer(x, k_past, v_past, conv_past, ssm_past, pos0,
           norm_g, w_in, conv_w, conv_b, dt_bias, a_log, d_skip, ssm_norm,
           w_branch_a, w_branch_b, w_out, rel_bias):
    f32 = jnp.float32
    bsz, length, _ = x.shape
    h = _rmsnorm(x, norm_g)
    proj = jnp.einsum('bld,dn->bln', h, w_in)
    q, k, v, g_attn, z, xbc, dt_raw, gate_a, gate_b = jnp.split(proj, SPLIT_POINTS, axis=-1)

    q = q.reshape(bsz, length, N_HEADS_A, HEAD_DIM_A)
    k = k.reshape(bsz, length, N_HEADS_A, HEAD_DIM_A)
    v = v.reshape(bsz, length, N_HEADS_A, HEAD_DIM_A)
    k_all = jnp.concatenate([k_past.astype(k.dtype), k], axis=1)
    v_all = jnp.concatenate([v_past.astype(v.dtype), v], axis=1)
    k_pos0 = pos0 - k_past.shape[1]
    y_a = _dilated_attention(q, k_all, v_all, rel_bias, pos0, k_pos0).reshape(bsz, length, D_ATTN)
    y_a = y_a * jax.nn.silu(g_attn)

    xbc_c, conv_new = _causal_dwconv(xbc, conv_past, conv_w, conv_b)
    xbc_c = jax.nn.silu(xbc_c)
    xs, bm, cm = jnp.split(xbc_c, [D_SSM, D_SSM + SSM_GROUPS * D_STATE], axis=-1)
    xs = xs.reshape(bsz, length, N_SSM_HEADS, SSM_HEAD_DIM)
    bm = bm.reshape(bsz, length, SSM_GROUPS, D_STATE)
    cm = cm.reshape(bsz, length, SSM_GROUPS, D_STATE)
    dt = jax.nn.softplus(dt_raw.astype(f32) + dt_bias.astype(f32))
    a = -jnp.exp(a_log.astype(f32))
    y_s, ssm_new = _ssd(xs, dt, a, bm, cm, ssm_past)
    y_s = y_s + d_skip.astype(f32)[:, None] * xs.astype(f32)
    y_s = y_s.reshape(bsz, length, D_SSM) * jax.nn.silu(z.astype(f32))
    y_s = _group_rmsnorm(y_s, ssm_norm).astype(x.dtype)

    merged = (jax.nn.sigmoid(gate_a) * jnp.einsum('blc,cd->bld', y_a, w_branch_a)
              + jax.nn.sigmoid(gate_b) * jnp.einsum('blc,cd->bld', y_s, w_branch_b))
    out = x + jnp.einsum('bld,de->ble', merged, w_out)
    return out, k, v, conv_new, ssm_new


def setup_inputs(seed: int = 0) -> dict:
    key = jax.random.key(seed)
    ks = jax.random.split(key, 20)
    f32 = jnp.float32
    kv_buf = min(WINDOW_MAX, PAST_LEN)

    def nrm(k, shape, s):
        return jax.random.normal(k, shape, f32) * s

    dt0 = jnp.exp(jax.random.uniform(ks[11], (DEPTH, N_SSM_HEADS), f32, math.log(DT_MIN), math.log(DT_MAX)))
    return {
        'x_prompt': nrm(ks[0], (BATCH, SEQ, D_MODEL), 1.0),
        'x_sample': nrm(ks[1], (DEC_BATCH, DEC_SEQ, D_MODEL), 1.0),
        'cache_k': nrm(ks[2], (DEPTH, DEC_BATCH, kv_buf, N_HEADS_A, HEAD_DIM_A), 1.0),
        'cache_v': nrm(ks[3], (DEPTH, DEC_BATCH, kv_buf, N_HEADS_A, HEAD_DIM_A), 1.0),
        'state_conv': nrm(ks[4], (DEPTH, DEC_BATCH, CONV_WIDTH - 1, D_CONV), 1.0),
        'state_ssm': nrm(ks[5], (DEPTH, DEC_BATCH, N_SSM_HEADS, SSM_HEAD_DIM, D_STATE), 0.3),
        'norm_g': 1.0 + nrm(ks[6], (DEPTH, D_MODEL), 0.05),
        'w_in': nrm(ks[7], (DEPTH, D_MODEL, D_IN_PROJ), D_MODEL ** -0.5),
        'conv_w': nrm(ks[8], (DEPTH, CONV_WIDTH, D_CONV), CONV_WIDTH ** -0.5),
        'conv_b': nrm(ks[9], (DEPTH, D_CONV), 0.02),
        'dt_bias': dt0 + jnp.log(-jnp.expm1(-dt0)),
        'a_log': jnp.log(jax.random.uniform(ks[10], (DEPTH, N_SSM_HEADS), f32, 1.0, 16.0)),
        'd_skip': 1.0 + nrm(ks[12], (DEPTH, N_SSM_HEADS), 0.1),
        'ssm_norm': 1.0 + nrm(ks[13], (DEPTH, D_SSM), 0.05),
        'w_branch_a': nrm(ks[14], (DEPTH, D_ATTN, D_MODEL), D_ATTN ** -0.5),
        'w_branch_b': nrm(ks[15], (DEPTH, D_SSM, D_MODEL), D_SSM ** -0.5),
        'w_out': nrm(ks[16], (DEPTH, D_MODEL, D_MODEL), D_MODEL ** -0.5),
        'rel_bias': nrm(ks[17], (N_BUCKETS, N_HEADS_A), 0.5),
        'final_norm': 1.0 + nrm(ks[18], (D_MODEL,), 0.05),
    }


def reference(x_prompt, x_sample, cache_k, cache_v, state_conv, state_ssm,
              norm_g, w_in, conv_w, conv_b, dt_bias, a_log, d_skip, ssm_norm,
              w_branch_a, w_branch_b, w_out, rel_bias, final_norm):
    bp, lp = x_prompt.shape[0], x_prompt.shape[1]
    keep = min(WINDOW_MAX, lp)
    yp, ys = x_prompt, x_sample
    nk_p, nv_p, nc_p, ns_p = [], [], [], []
    nk_s, nv_s, nc_s, ns_s = [], [], [], []
    for layer in range(DEPTH):
        wts = (norm_g[layer], w_in[layer], conv_w[layer], conv_b[layer], dt_bias[layer],
               a_log[layer], d_skip[layer], ssm_norm[layer], w_branch_a[layer],
               w_branch_b[layer], w_out[layer], rel_bias)
        k0 = jnp.zeros((bp, 0, N_HEADS_A, HEAD_DIM_A), x_prompt.dtype)
        c0 = jnp.zeros((bp, CONV_WIDTH - 1, D_CONV), x_prompt.dtype)
        s0 = jnp.zeros((bp, N_SSM_HEADS, SSM_HEAD_DIM, D_STATE), jnp.float32)
        yp, kp, vp, cp, sp = _layer(yp, k0, k0, c0, s0, 0, *wts)
        nk_p.append(kp[:, lp - keep:])
        nv_p.append(vp[:, lp - keep:])
        nc_p.append(cp)
        ns_p.append(sp)
        ys, ks_, vs_, cs_, ss_ = _layer(ys, cache_k[layer], cache_v[layer], state_conv[layer],
                                        state_ssm[layer], PAST_LEN, *wts)
        nk_s.append(ks_)
        nv_s.append(vs_)
        nc_s.append(cs_)
        ns_s.append(ss_)
    y_prompt = _rmsnorm(yp, final_norm)
    y_sample = _rmsnorm(ys, final_norm)
    return (y_prompt, y_sample,
            jnp.stack(nk_p), jnp.stack(nv_p), jnp.stack(nc_p), jnp.stack(ns_p),
            jnp.stack(nk_s), jnp.stack(nv_s), jnp.stack(nc_s), jnp.stack(ns_s))
```

```python
import math
import numpy as np
from contextlib import ExitStack
import concourse.bass as bass
import concourse.mybir as mybir
from concourse.bass_utils import run_bass_kernel_spmd

F32 = mybir.dt.float32
BF16 = mybir.dt.bfloat16
ALU = mybir.AluOpType
AF = mybir.ActivationFunctionType

NCORES = 8
D = 1024
LP = 2048
NS = 32
NT = LP + NS
NIN = 11296
NEG = -30000.0


class Buf:
    __slots__ = ("name", "w", "r", "excl")

    def __init__(self, name="", excl=False):
        self.name = name
        self.w = None
        self.r = []
        self.excl = excl


class Op:
    __slots__ = ("eng", "fn", "deps", "hard", "idx", "signal", "token", "is_dma", "prev_token")

    def __init__(self, eng, fn, idx, is_dma):
        self.eng = eng
        self.fn = fn
        self.idx = idx
        self.deps = set()
        self.hard = set()
        self.signal = False
        self.token = None
        self.is_dma = is_dma
        self.prev_token = None


class Sched:
    ENGS = ("pe", "act", "dve", "pool", "sp")

    def __init__(self, nc, n_dma_sems=40):
        self.nc = nc
        self.ops = []
        self.n_dma_sems = n_dma_sems
        self.dma_since_barrier = []

    def add(self, eng, fn, reads=(), writes=(), is_dma=False):
        op = Op(eng, fn, len(self.ops), is_dma)
        self.ops.append(op)
        writes = list(writes) + [b for b in reads if b.excl]
        wset = set(id(b) for b in writes)
        for b in reads:
            if id(b) in wset:
                continue
            if b.w is not None:
                op.deps.add(b.w)
                op.hard.add(b.w)
            b.r.append(op.idx)
        done = set()
        for b in writes:
            if id(b) in done:
                continue
            done.add(id(b))
            if b.w is not None:
                op.deps.add(b.w)
                op.hard.add(b.w)
            op.deps.update(b.r)
            b.w = op.idx
            b.r = []
        op.deps.discard(op.idx)
        op.hard.discard(op.idx)
        if is_dma:
            self.dma_since_barrier.append(op.idx)
        return op

    def dma(self, fn, reads=(), writes=(), eng="sp"):
        return self.add(eng, fn, reads, writes, is_dma=True)

    def barrier(self, scratch):
        bars = {}
        firsts = []
        for i, e in enumerate(self.ENGS):
            b = Buf("bar")
            bars[e] = b
            if e in ("pe", "sp"):
                fn = lambda en: en.nop()
            elif e == "act":
                fn = (lambda en, i=i: en.copy(out=scratch[0:1, i:i + 1], in_=scratch[0:1, i:i + 1]))
            else:
                fn = (lambda en, i=i: en.memset(scratch[0:1, i:i + 1], 0.0))
            op = self.add(e, fn, writes=[b])
            firsts.append(op)
        for op in firsts:
            op.deps.update(self.dma_since_barrier)
            op.deps.discard(op.idx)
        self.dma_since_barrier = []
        for i, e in enumerate(self.ENGS):
            if e in ("pe", "sp"):
                fn = lambda en: en.nop()
            elif e == "act":
                fn = (lambda en, i=i: en.copy(out=scratch[0:1, 8 + i:9 + i], in_=scratch[0:1, 8 + i:9 + i]))
            else:
                fn = (lambda en, i=i: en.memset(scratch[0:1, 8 + i:9 + i], 0.0))
            self.add(e, fn, reads=list(bars.values()))

    def emit(self, stack):
        nc = self.nc
        ops = self.ops
        def needs(op, dop):
            if dop.is_dma or dop.eng != op.eng:
                return True
            if op.eng in ("pe", "sp"):
                return False
            return dop.idx in op.hard

        for op in ops:
            for d in op.deps:
                dop = ops[d]
                if needs(op, dop):
                    dop.signal = True
        esem = {e: stack.enter_context(nc.semaphore("s_" + e)) for e in self.ENGS}
        dsem = [stack.enter_context(nc.semaphore("d%d" % i)) for i in range(self.n_dma_sems)]
        cnt = {e: 0 for e in self.ENGS}
        duse = [0] * self.n_dma_sems
        ndma = 0
        for op in ops:
            if op.is_dma:
                k = ndma % self.n_dma_sems
                ndma += 1
                if duse[k] > 0:
                    op.prev_token = (dsem[k], 16 * duse[k])
                duse[k] += 1
                op.token = (dsem[k], 16 * duse[k])
            elif op.signal:
                cnt[op.eng] += 1
                op.token = (esem[op.eng], cnt[op.eng])
        per_eng = {e: [op for op in ops if op.eng == e] for e in self.ENGS}
        final_dma = [(dsem[k], 16 * duse[k]) for k in range(self.n_dma_sems) if duse[k] > 0]

        def run(ename, e):
            waited = {}
            for op in per_eng[ename]:
                need = {}
                for d in op.deps:
                    dop = ops[d]
                    if not needs(op, dop):
                        continue
                    s, v = dop.token
                    if need.get(s.num, (None, 0))[1] < v:
                        need[s.num] = (s, v)
                if op.prev_token is not None:
                    s, v = op.prev_token
                    if need.get(s.num, (None, 0))[1] < v:
                        need[s.num] = (s, v)
                for sn, (s, v) in need.items():
                    if waited.get(sn, 0) < v:
                        e.wait_ge(s, v)
                        waited[sn] = v
                ins = op.fn(e)
                if op.is_dma:
                    ins.then_inc(op.token[0], 16)
                elif op.signal:
                    ins.then_inc(op.token[0], 1)
            if ename == "sp":
                for s, v in final_dma:
                    if waited.get(s.num, 0) < v:
                        e.wait_ge(s, v)

        block = stack.enter_context(nc.Block())

        @block.tensor
        def _(e):
            run("pe", e)

        @block.scalar
        def _(e):
            run("act", e)

        @block.vector
        def _(e):
            run("dve", e)

        @block.gpsimd
        def _(e):
            run("pool", e)

        @block.sync
        def _(e):
            run("sp", e)


def pipeline(stage_lists):
    n = len(stage_lists)
    ns = max(len(x) for x in stage_lists) if n else 0
    for t in range(n + ns - 1):
        for k in reversed(range(ns)):
            i = t - k
            if 0 <= i < n and k < len(stage_lists[i]) and stage_lists[i][k] is not None:
                stage_lists[i][k]()


def _t5_bucket(dist):
    n_buckets, max_distance = 32, 2048
    max_exact = n_buckets // 2
    d = np.maximum(dist, 1).astype(np.float32)
    large = max_exact + (np.log(d / max_exact) / math.log(max_distance / max_exact)
                         * (n_buckets - max_exact)).astype(np.int32)
    large = np.minimum(large, n_buckets - 1)
    return np.where(dist < max_exact, dist, large).astype(np.int32)


GROUP_D = (1, 4, 16)


def _static_index_tables():
    bk = [_t5_bucket(np.arange(0, 129, dtype=np.int32) * d) for d in GROUP_D]
    k = np.arange(128)[:, None]
    q = np.arange(128)[None, :]
    tb = np.full((128, 640), 32, np.int32)
    for g in range(2):
        dist = q + 128 - k
        tb[:, 256 * g:256 * g + 128] = np.where(dist <= 128, bk[g][np.clip(dist, 0, 128)], 32)
        dist = q - k
        tb[:, 256 * g + 128:256 * g + 256] = np.where(dist >= 0, bk[g][np.clip(dist, 0, 128)], 32)
    dist = q - k
    tb[:, 512:640] = np.where(dist >= 0, bk[2][np.clip(dist, 0, 128)], 32)
    fb = np.full((128, 8, 3), 32, np.int32)
    m = np.arange(128)
    for i in range(8):
        fb[:, i, 2] = bk[2][128 - m]
        j = 128 + (i // 4) - m
        fb[:, i, 1] = np.where((j >= 1) & (j <= 128), bk[1][np.clip(j, 0, 128)], 32)
        j = 128 + i - m
        fb[:, i, 0] = np.where((j >= 1) & (j <= 128), bk[0][np.clip(j, 0, 128)], 32)
    fn = np.full((32, 32, 3), 32, np.int32)
    for s in range(4):
        for i in range(8):
            for ip in range(i + 1):
                dlt = i - ip
                p = s * 8 + ip
                fn[p, s * 8 + i, 0] = bk[0][dlt]
                if dlt % 4 == 0:
                    fn[p, s * 8 + i, 1] = bk[1][dlt // 4]
                if dlt == 0:
                    fn[p, s * 8 + i, 2] = bk[2][0]
    return tb, fb, fn


def _const_pack():
    c = np.zeros((128, 776), np.float32)
    c[:, 0:128] = np.eye(128)
    s = np.arange(128)[:, None]
    l = np.arange(128)[None, :]
    c[:, 128:256] = (s <= l)
    same = (s // 8 == l // 8) & (s < 32) & (l < 32)
    c[:, 256:384] = same & (s <= l)
    c[:, 384:512] = 1.0
    c[:, 512:640] = same
    for sq in range(4):
        c[sq * 8:(sq + 1) * 8, 640 + sq] = 1.0
    for sq in range(4):
        c[:, 648 + sq * 32 + sq * 8: 648 + sq * 32 + sq * 8 + 8] = 1.0
    return c


def build_program():
    nc = bass.Bass("TRN2", target_bir_lowering=False)

    def din(name, shape, dt=F32):
        return nc.dram_tensor(name, list(shape), dt, kind="ExternalInput").ap()

    def dout(name, shape):
        return nc.dram_tensor(name, list(shape), F32, kind="ExternalOutput").ap()

    x_all = din("x_all", [NT, D])
    w_in = din("w_in", [D, NIN])
    w_a = din("w_a", [768, D])
    w_b = din("w_b", [2048, D])
    w_o = din("w_o", [D, D])
    norm_g = din("norm_g", [1, D])
    final_g = din("final_g", [1, D])
    ssm_norm = din("ssm_norm", [1, 2048])
    cwT = din("cwT", [128, 32, 4])
    cbT = din("cbT", [128, 32])
    cb_row = din("cb_row", [1, 4096])
    hpar = din("hpar", [1, 96])
    tbraw = din("tbraw", [128, 12, 640])
    fbraw = din("fbraw", [128, 8, 3, 12])
    fnraw = din("fnraw", [32, 32, 3, 12])
    cst = din("cst", [128, 776])
    cache_k = din("cache_k", [4, 2048, 768])
    cache_v = din("cache_v", [4, 2048, 768])
    scT = din("scT", [128, 32, 4, 3])
    st_nat = din("st_nat", [4, 32, 64, 128])
    st_T = din("st_T", [128, 4, 2048])

    y_out = dout("y_out", [NT, D])
    k_out = dout("k_out", [NT, 768])
    v_out = dout("v_out", [NT, 768])
    conv_out = dout("conv_out", [15, 4096])
    ssm_p = dout("ssm_p", [32, 64, 128])
    ssm_s = dout("ssm_s", [4, 32, 64, 128])
    acT_d = nc.dram_tensor("acT_d", [32, NT], F32, kind="Internal").ap()

    w_in_v = w_in.rearrange("(kc p) n -> p kc n", p=128)

    TT = [(i * 128, 128) for i in range(16)] + [(LP, NS)]
    TB = [(i * 512, 512) for i in range(4)] + [(LP, NS)]

    with ExitStack() as top:
        S = Sched(nc)

        def SB(st, name, shape, dt):
            return st.enter_context(nc.sbuf_tensor(name, list(shape), dt))

        PS = [top.enter_context(nc.psum_tensor("ps%d" % i, [128, 512], F32)) for i in range(8)]
        PB = [Buf("ps%d" % i, excl=True) for i in range(8)]

        XN = SB(top, "XN", [128, 8, NT], BF16); bXN = Buf()
        ysT = SB(top, "ysT", [128, 16, NT], BF16); bys = Buf()
        tabs = ExitStack()
        wbuf = [SB(top, "wbuf0", [128, 8, 512], BF16)]
        bw = [Buf(), Buf()]
        nwb = [2]
        C = SB(top, "C", [128, 776], F32); bC = Buf()
        Cb = SB(top, "Cb", [128, 776], BF16); bCb = Buf()
        scr = SB(top, "scr", [128, 16], F32)
        hp_bc = SB(top, "hp_bc", [128, 96], F32); bhp = Buf()
        IDf = C[:, 0:128]; TRIp = C[:, 128:256]; TRIs = C[:, 256:384]; ONESf = C[:, 384:512]; ONESs = C[:, 512:640]
        SEG = C[0:32, 640:644]
        IDb = Cb[:, 0:128]; TRIpb = Cb[:, 128:256]; TRIsb = Cb[:, 256:384]; ONESb = Cb[:, 384:512]
        SEGXb = Cb[:, 648:776]

        S.dma(lambda e: e.dma_start(out=C[:], in_=cst), writes=[bC])
        S.add("dve", lambda e: e.tensor_copy(out=Cb[:], in_=C[:]), reads=[bC], writes=[bCb])
        S.dma(lambda e: e.dma_start(out=hp_bc[:], in_=hpar.to_broadcast([128, 96])), writes=[bhp])

        wcount = [0]

        def load_w(segs):
            i = wcount[0] % nwb[0]
            wcount[0] += 1
            t, b = wbuf[i], bw[i]
            for (src, nk, c0, n) in segs:
                S.dma(lambda e, src=src, nk=nk, c0=c0, n=n: e.dma_start(out=t[:, 0:nk, c0:c0 + n], in_=src),
                      writes=[b], eng="pool")
            return t, b

        def win_seg(col0, n, c0):
            return (w_in_v[:, :, col0:col0 + n], 8, c0, n)

        with ExitStack() as p0:
            gbc = SB(p0, "gbc", [128, D], F32); bg = Buf()
            xin = [SB(p0, "xin%d" % i, [128, D], F32) for i in range(2)]; bxin = [Buf(), Buf()]
            junk = SB(p0, "junk", [128, D], BF16); bjunk = Buf()
            hb = [SB(p0, "hb%d" % i, [128, D], BF16) for i in range(2)]; bhb = [Buf(), Buf()]
            ssq = SB(p0, "ssq", [128, 51], F32); bss = Buf()
            S.dma(lambda e: e.dma_start(out=gbc[:], in_=norm_g.to_broadcast([128, D])), writes=[bg])
            S.add("dve", lambda e: e.memset(ssq[:], 0.0), writes=[bss])
            for ti, (t0, rows) in enumerate(TT):
                xi, bxi = xin[ti % 2], bxin[ti % 2]
                S.dma(lambda e, xi=xi, t0=t0, rows=rows: e.dma_start(out=xi[0:rows, :], in_=x_all[t0:t0 + rows, :]), writes=[bxi])
                S.add("act", lambda e, xi=xi, rows=rows, ti=ti: e.activation(out=junk[0:rows, :], in_=xi[0:rows, :], func=AF.Square,
                                                                             accum_out=ssq[0:rows, ti:ti + 1]), reads=[bxi], writes=[bjunk, bss])
            S.add("act", lambda e: e.activation(out=ssq[:, 17:34], in_=ssq[:, 0:17], func=AF.Sqrt, scale=1.0 / D, bias=1e-6), reads=[bss], writes=[bss])
            S.add("dve", lambda e: e.reciprocal(out=ssq[:, 34:51], in_=ssq[:, 17:34]), reads=[bss], writes=[bss])
            for ti, (t0, rows) in enumerate(TT):
                xi, bxi = xin[ti % 2], bxin[ti % 2]
                h_, bh_ = hb[ti % 2], bhb[ti % 2]
                pb, bpb = PS[ti % 2], PB[ti % 2]
                S.dma(lambda e, xi=xi, t0=t0, rows=rows: e.dma_start(out=xi[0:rows, :], in_=x_all[t0:t0 + rows, :]), writes=[bxi])
                S.add("dve", lambda e, xi=xi, h_=h_, rows=rows, ti=ti: e.scalar_tensor_tensor(
                    out=h_[0:rows, :], in0=xi[0:rows, :], scalar=ssq[0:rows, 34 + ti:35 + ti], in1=gbc[0:rows, :],
                    op0=ALU.mult, op1=ALU.mult), reads=[bxi, bss, bg], writes=[bh_])
                pv = pb[:].bitcast(BF16)
                for kc in range(8):
                    S.add("pe", lambda e, pv=pv, h_=h_, kc=kc, rows=rows: e.transpose(
                        out=pv[:, kc * 128:kc * 128 + rows], in_=h_[0:rows, kc * 128:(kc + 1) * 128],
                        identity=IDb[0:rows, 0:rows]), reads=[bh_, bCb], writes=[bpb])
                S.add("act", lambda e, pv=pv, t0=t0, rows=rows: e.copy(
                    out=XN[:, :, t0:t0 + rows], in_=pv.rearrange("p (k t) -> p k t", k=8)[:, :, 0:rows]),
                    reads=[bpb], writes=[bXN, bpb])
        S.barrier(scr)

        wbuf.append(SB(tabs, "wbuf1", [128, 8, 512], BF16))
        dtt = SB(tabs, "dtt", [128, 17, 32], F32)
        acum = SB(tabs, "acum", [128, 17, 32], F32)
        eacum = SB(tabs, "eacum", [128, 17, 32], F32)
        dtw = SB(tabs, "dtw", [128, 17, 32], F32)
        decay = SB(tabs, "decay", [128, 17, 32], F32)
        totb = SB(tabs, "totb", [128, 17, 32], F32)
        nacum = SB(tabs, "nacum", [128, 17, 32], F32)
        bT = Buf()
        with ExitStack() as p1:
            dta = SB(p1, "dta", [128, 17, 32], F32)
            acTs = SB(p1, "acTs", [32, NT], F32); bacT = Buf()
            wt, bwt = load_w([win_seg(9216, 32, 0)])
            S.add("dve", lambda e: e.memset(dtt[:], 0.0), writes=[bT])
            for ti, (t0, rows) in enumerate(TT):
                pb, bpb = (PS[2], PB[2]) if ti < 16 else (PS[3], PB[3])
                c0 = (ti % 16) * 32
                for kc in range(8):
                    S.add("pe", lambda e, pb=pb, kc=kc, t0=t0, rows=rows, c0=c0: e.matmul(
                        pb[0:rows, c0:c0 + 32], lhsT=XN[:, kc, t0:t0 + rows], rhs=wt[:, kc, 0:32],
                        start=(kc == 0), stop=(kc == 7)), reads=[bXN, bwt], writes=[bpb])
            S.add("dve", lambda e: e.tensor_tensor(out=dtt[:, 0:16, :], in0=PS[2][:].rearrange("p (c h) -> p c h", h=32),
                                                   in1=hp_bc[:, 0:32].unsqueeze(1).to_broadcast([128, 16, 32]), op=ALU.add),
                  reads=[PB[2], bhp], writes=[bT, PB[2]])
            S.add("dve", lambda e: e.tensor_tensor(out=dtt[0:32, 16, :], in0=PS[3][0:32, 0:32], in1=hp_bc[0:32, 0:32], op=ALU.add),
                  reads=[PB[3], bhp], writes=[bT, PB[3]])
            S.add("act", lambda e: e.activation(out=dtt[:], in_=dtt[:], func=AF.Exp), reads=[bT], writes=[bT])
            S.add("act", lambda e: e.activation(out=dtt[:], in_=dtt[:], func=AF.Ln, bias=1.0), reads=[bT], writes=[bT])
            S.add("dve", lambda e: e.memset(dtt[32:64, 16, :], 0.0), writes=[bT])
            S.add("dve", lambda e: e.memset(dtt[64:128, 16, :], 0.0), writes=[bT])
            S.add("act", lambda e: e.activation(out=hp_bc[:, 32:64], in_=hp_bc[:, 32:64], func=AF.Exp), reads=[bhp], writes=[bhp])
            S.add("dve", lambda e: e.scalar_tensor_tensor(out=dta[:], in0=dtt[:], scalar=-1.0,
                                                          in1=hp_bc[:, 32:64].unsqueeze(1).to_broadcast([128, 17, 32]),
                                                          op0=ALU.mult, op1=ALU.mult), reads=[bT, bhp], writes=[bT])
            for c in range(17):
                tri = TRIp if c < 16 else TRIs
                ones = ONESf if c < 16 else ONESs
                pa, bpa = (PS[4], PB[4]) if c < 16 else (PS[5], PB[5])
                pt, bpt = (PS[6], PB[6]) if c < 16 else (PS[7], PB[7])
                c0 = (c % 16) * 32
                S.add("pe", lambda e, pa=pa, tri=tri, c=c, c0=c0: e.matmul(pa[:, c0:c0 + 32], lhsT=tri, rhs=dta[:, c, :],
                                                                         start=True, stop=True), reads=[bT, bC], writes=[bpa])
                S.add("pe", lambda e, pt=pt, ones=ones, c=c, c0=c0: e.matmul(pt[:, c0:c0 + 32], lhsT=ones, rhs=dta[:, c, :],
                                                                           start=True, stop=True), reads=[bT, bC], writes=[bpt])
            S.add("dve", lambda e: e.tensor_copy(out=acum[:, 0:16, :], in_=PS[4][:].rearrange("p (c h) -> p c h", h=32)),
                  reads=[PB[4]], writes=[bT, PB[4]])
            S.add("dve", lambda e: e.tensor_copy(out=acum[:, 16, :], in_=PS[5][:, 0:32]), reads=[PB[5]], writes=[bT, PB[5]])
            S.add("dve", lambda e: e.tensor_copy(out=totb[:, 0:16, :], in_=PS[6][:].rearrange("p (c h) -> p c h", h=32)),
                  reads=[PB[6]], writes=[bT, PB[6]])
            S.add("dve", lambda e: e.tensor_copy(out=totb[:, 16, :], in_=PS[7][:, 0:32]), reads=[PB[7]], writes=[bT, PB[7]])
            S.add("act", lambda e: e.activation(out=eacum[:], in_=acum[:], func=AF.Exp), reads=[bT], writes=[bT])
            S.add("act", lambda e: e.activation(out=decay[:], in_=totb[:], func=AF.Exp), reads=[bT], writes=[bT])
            S.add("dve", lambda e: e.tensor_sub(out=dtw[:], in0=totb[:], in1=acum[:]), reads=[bT], writes=[bT])
            S.add("act", lambda e: e.activation(out=dtw[:], in_=dtw[:], func=AF.Exp), reads=[bT], writes=[bT])
            S.add("dve", lambda e: e.tensor_mul(out=dtw[:], in0=dtw[:], in1=dtt[:]), reads=[bT], writes=[bT])
            S.add("dve", lambda e: e.tensor_scalar(out=nacum[:], in0=acum[:], scalar1=-1.0, scalar2=None, op0=ALU.mult), reads=[bT], writes=[bT])
            for c in range(17):
                tri = TRIp if c < 16 else TRIs
                pb, bpb = PS[c % 2], PB[c % 2]
                rows = 128 if c < 16 else 32
                S.add("pe", lambda e, pb=pb, tri=tri, c=c, rows=rows: e.matmul(pb[0:32, 0:rows], lhsT=dta[:, c, :], rhs=tri[:, 0:rows],
                                                                             start=True, stop=True), reads=[bT, bC], writes=[bpb])
                S.add("dve", lambda e, pb=pb, c=c, rows=rows: e.tensor_copy(out=acTs[:, c * 128:c * 128 + rows], in_=pb[0:32, 0:rows]),
                      reads=[bpb], writes=[bacT, bpb])
            bscr = Buf()
            S.dma(lambda e: e.dma_start(out=acT_d, in_=acTs[:]), reads=[bacT], writes=[bscr])
        S.barrier(scr)

        with ExitStack() as p2:
            cw = SB(p2, "cw", [128, 32, 4], F32); bcw = Buf()
            cb = SB(p2, "cb", [128, 32], F32)
            sct = SB(p2, "sct", [128, 32, 4, 3], F32)
            S.dma(lambda e: e.dma_start(out=cw[:], in_=cwT), writes=[bcw])
            S.dma(lambda e: e.dma_start(out=cb[:], in_=cbT), writes=[bcw])
            S.dma(lambda e: e.dma_start(out=sct[:], in_=scT), writes=[bcw])
            S.add("pool", lambda e: e.tensor_scalar(out=cw[:], in0=cw[:], scalar1=0.5, scalar2=None, op0=ALU.mult), reads=[bcw], writes=[bcw])
            S.add("pool", lambda e: e.tensor_scalar(out=cb[:], in0=cb[:], scalar1=0.5, scalar2=None, op0=ALU.mult), reads=[bcw], writes=[bcw])
            hsel = SB(p2, "hsel", [128, 8, 15], BF16); bhsel = Buf()
            S.add("pool", lambda e: e.tensor_copy(out=hsel[:, :, 0:3], in_=XN[:, :, 2045:2048]), reads=[bXN], writes=[bhsel])
            for s in range(4):
                S.add("pool", lambda e, s=s: e.tensor_copy(out=hsel[:, :, 3 + 3 * s:6 + 3 * s], in_=XN[:, :, LP + 8 * s + 5:LP + 8 * s + 8]),
                      reads=[bXN], writes=[bhsel])
            wz = [SB(p2, "wz%d" % i, [128, 8, 256], BF16) for i in range(1)] * 2; bwz = [Buf()] * 2
            xwb = SB(p2, "xwb", [128, 3 + LP], BF16); bxw = Buf()
            xwsb = SB(p2, "xwsb", [128, 4, 11], BF16); bxws = Buf()
            tnh2 = [SB(p2, "tnh2_%d" % i, [128, 512], BF16) for i in range(2)]; btnh2 = [Buf(), Buf()]
            dg = SB(p2, "dg", [128, 4, 128], BF16); bdg = Buf()
            cbrf = SB(p2, "cbrf", [1, 128], F32); bcbrf = Buf()
            cbrb = SB(p2, "cbrb", [1, 128], BF16); bcbrb = Buf()
            onesrow = SB(p2, "onesrow", [1, 512], BF16); bones = Buf()
            S.add("pool", lambda e: e.memset(onesrow[:], 1.0), writes=[bones])
            xdt2 = [SB(p2, "xdt%d" % i, [128, 256], BF16) for i in range(2)]; bxdt2 = [Buf(), Buf()]
            xdb2 = [SB(p2, "xdb%d" % i, [128, 256], BF16) for i in range(2)]; bxdb2 = [Buf(), Buf()]
            XT = SB(p2, "XT", [128, 4, NT], BF16); bXT = [Buf() for _ in range(4)]
            xtok = SB(p2, "xtok", [128, 17, 256], BF16); bxtk = [Buf() for _ in range(17)]
            btok = SB(p2, "btok", [128, 17, 128], BF16); bbtok = Buf()
            CBm = SB(p2, "CBm", [128, 17, 128], BF16); bCBm = Buf()
            CTs = SB(p2, "CTs", [128, 4, 32], BF16); bCTs = Buf()
            abc = [SB(p2, "abc%d" % i, [128, 4, 128], F32) for i in range(2)]; babc = [Buf() for _ in range(2)]
            Dt = [SB(p2, "Dt%d" % i, [128, 2, 128], F32) for i in range(1)] * 2; bDt = [Buf()] * 2
            Et = [SB(p2, "Et%d" % i, [128, 4, 128], BF16) for i in range(2)]; bEt = [Buf(), Buf()]
            Mt = [SB(p2, "Mt%d" % i, [128, 4, 128], BF16) for i in range(2)]; bMt = [Buf(), Buf()]
            Bw = [SB(p2, "Bw%d" % i, [128, 256], BF16) for i in range(2)]; bBw = [Buf(), Buf()]
            yt = [SB(p2, "yt%d" % i, [128, 256], F32) for i in range(2)]; byt = [Buf(), Buf()]
            STt = SB(p2, "STt", [128, 256], F32); bST = Buf()
            STb = SB(p2, "STb", [128, 256], BF16); bSTb = Buf()
            tz = SB(p2, "tz", [128, 256], F32); btz = Buf()
            uz = tz; buz = btz
            yg = tz; byg = btz
            ynb2 = [SB(p2, "ynb%d" % i, [128, 256], BF16) for i in range(2)]; bynb2 = [Buf(), Buf()]; ynb = ynb2[0]; bynb = bynb2[0]
            gs = SB(p2, "gs", [128, 51], F32); bgs = Buf()
            nrm = SB(p2, "nrm", [128, 256], F32); bnrm = Buf()
            cvo = abc[0][:, :, :].rearrange("p a b -> p (a b)"); bcvo = babc[0]
            h0T = SB(p2, "h0T", [128, 4, 256], BF16); bh0T = Buf()
            h0n = [SB(p2, "h0n%d" % i, [128, 2, 128], F32) for i in range(1)] * 2; bh0n = [Buf()] * 2
            wxm = SB(p2, "wxm", [32, 256], BF16); bwxm = Buf(); xsr = SB(p2, "xsr", [32, 256], BF16); bxsr = Buf()
            dsg = SB(p2, "dsg", [32, 4, 4], F32); bdsg = Buf()
            dtaE = SB(p2, "dtaE", [32, 256], F32); bdtaE = Buf()
            dcol = SB(p2, "dcol", [128, 2, 4], F32); bdcol = Buf()
            nst = [SB(p2, "nst%d" % i, [128, 2, 128], F32) for i in range(1)] * 2; bnst = [Buf()] * 2
            stp = nst[0]; bstp = bnst[0]
            jk = ynb; bjk = bynb

            WT = {}
            wzt, bwzt = wz[0], bwz[0]

            def g_loadw(g):
                WT[g] = load_w([win_seg(5120 + 256 * g, 256, 0), win_seg(7168 + 128 * g, 128, 256), win_seg(8192 + 128 * g, 128, 384)])

            def g_pre(g):
                wt, bwt = WT[g]
                S.dma(lambda e: e.dma_start(out=wzt[:], in_=w_in_v[:, :, 3072 + 256 * g:3072 + 256 * (g + 1)]), writes=[bwzt], eng="pool")
                S.dma(lambda e: e.dma_start(out=nrm[:], in_=ssm_norm[:, 256 * g:256 * (g + 1)].to_broadcast([128, 256])), writes=[bnrm])
                S.dma(lambda e: e.dma_start(out=h0T[:], in_=st_T[:, :, 256 * g:256 * (g + 1)]), writes=[bh0T], eng="pool")
                for kc in range(8):
                    S.add("pe", lambda e, kc=kc, wt=wt: e.matmul(PS[2][0:15, :], lhsT=hsel[:, kc, :], rhs=wt[:, kc, :],
                                                               start=(kc == 0), stop=(kc == 7)), reads=[bhsel, bwt], writes=[PB[2]])
                S.add("act", lambda e: e.copy(out=cvo[0:15, :], in_=PS[2][0:15, :]), reads=[PB[2]], writes=[bcvo, PB[2]])
                S.dma(lambda e, g=g: e.dma_start(out=conv_out[:, 256 * g:256 * (g + 1)], in_=cvo[0:15, 0:256]), reads=[bcvo])
                S.dma(lambda e, g=g: e.dma_start(out=conv_out[:, 2048 + 128 * g:2048 + 128 * (g + 1)], in_=cvo[0:15, 256:384]), reads=[bcvo])
                S.dma(lambda e, g=g: e.dma_start(out=conv_out[:, 3072 + 128 * g:3072 + 128 * (g + 1)], in_=cvo[0:15, 384:512]), reads=[bcvo])

            def g_conv_items(g, tiles):
                wt, bwt = WT[g]
                conv_items = []
                for ct in tiles:
                    ctile = (2 * g + ct) if ct < 2 else (16 + g if ct == 2 else 24 + g)
                    for bi, (t0, n) in enumerate(TB):
                        def c0(ct=ct, ctile=ctile, bi=bi, t0=t0, n=n, wt=wt, bwt=bwt):
                            pb, bpb = PS[bi % 2], PB[bi % 2]
                            if bi == 0:
                                S.add("pool", lambda e: e.memset(xwb[:, 0:3], 0.0), writes=[bxw])
                                S.add("pool", lambda e: e.tensor_copy(out=xwsb[:, :, 0:3], in_=sct[:, ctile, :, :]), reads=[bcw], writes=[bxws])
                                S.add("pool", lambda e: e.tensor_tensor(out=dg[:], in0=IDb.unsqueeze(1).to_broadcast([128, 4, 128]),
                                                                        in1=cw[:, ctile, :].unsqueeze(2).to_broadcast([128, 4, 128]), op=ALU.mult),
                                      reads=[bCb, bcw], writes=[bdg])
                                S.dma(lambda e: e.dma_start(out=cbrf[:], in_=cb_row[:, ctile * 128:(ctile + 1) * 128]), writes=[bcbrf])
                                S.add("pool", lambda e: e.tensor_scalar(out=cbrb[:], in0=cbrf[:], scalar1=0.5, scalar2=None, op0=ALU.mult), reads=[bcbrf], writes=[bcbrb])
                            for kc in range(8):
                                S.add("pe", lambda e, kc=kc: e.matmul(pb[:, 0:n], lhsT=wt[:, kc, ct * 128:(ct + 1) * 128], rhs=XN[:, kc, t0:t0 + n],
                                                                     start=(kc == 0), stop=(kc == 7)), reads=[bXN, bwt], writes=[bpb])
                            if bi < 4:
                                S.add("act", lambda e: e.copy(out=xwb[:, 3 + 512 * bi:3 + 512 * (bi + 1)], in_=pb[:, :]), reads=[bpb], writes=[bxw, bpb])
                            else:
                                S.add("act", lambda e: e.copy(out=xwsb[:, :, 3:11], in_=pb[:, 0:32].rearrange("p (s t) -> p s t", s=4)),
                                      reads=[bpb], writes=[bxws, bpb])

                        def c1(ct=ct, bi=bi):
                            cps, bcps = PS[2 + (bi % 2)], PB[2 + (bi % 2)]
                            tn_, btn_ = tnh2[bi % 2], btnh2[bi % 2]
                            if bi < 4:
                                S.add("pe", lambda e: e.matmul(cps[:, :], lhsT=cbrb[0:1, :], rhs=onesrow[0:1, :], start=True, stop=False),
                                      reads=[bcbrb, bones], writes=[bcps])
                                for tap in range(4):
                                    S.add("pe", lambda e, tap=tap: e.matmul(cps[:, :], lhsT=dg[:, tap, :], rhs=xwb[:, 512 * bi + tap:512 * bi + tap + 512],
                                                                           start=False, stop=(tap == 3)), reads=[bdg, bxw], writes=[bcps])
                                S.add("act", lambda e: e.activation(out=tn_[:], in_=cps[:, :], func=AF.Tanh), reads=[bcps], writes=[btn_])
                                S.add("dve", lambda e: e.scalar_tensor_tensor(out=XT[:, ct, 512 * bi:512 * (bi + 1)], in0=tn_[:], scalar=1.0, in1=cps[:, :],
                                                                              op0=ALU.add, op1=ALU.mult), reads=[btn_, bcps], writes=[bXT[ct]])
                            else:
                                S.add("pe", lambda e: e.matmul(cps[:, 0:32], lhsT=cbrb[0:1, :], rhs=onesrow[0:1, 0:32], start=True, stop=False),
                                      reads=[bcbrb, bones], writes=[bcps])
                                for tap in range(4):
                                    S.add("pe", lambda e, tap=tap: e.matmul(cps[:, 0:32].rearrange("p (s t) -> p s t", s=4), lhsT=dg[:, tap, :],
                                                                           rhs=xwsb[:, :, tap:tap + 8], start=False, stop=(tap == 3)),
                                          reads=[bdg, bxws], writes=[bcps])
                                S.add("act", lambda e: e.activation(out=tn_[:, 0:32], in_=cps[:, 0:32], func=AF.Tanh), reads=[bcps], writes=[btn_])
                                S.add("dve", lambda e: e.scalar_tensor_tensor(out=XT[:, ct, LP:NT], in0=tn_[:, 0:32], scalar=1.0, in1=cps[:, 0:32],
                                                                              op0=ALU.add, op1=ALU.mult), reads=[btn_, bcps], writes=[bXT[ct]])
                        conv_items.append([c0, c1])
                return conv_items

            def g_mid(g):
                for c4 in range(5):
                    chunks = list(range(c4 * 4, min(c4 * 4 + 4, 17)))
                    pv = PS[3][:].bitcast(BF16)
                    for j, c in enumerate(chunks):
                        t0, rows = TT[c]
                        for k in range(2):
                            S.add("pe", lambda e, pv=pv, j=j, k=k, t0=t0, rows=rows: e.transpose(
                                out=pv[0:rows, j * 256 + k * 128:j * 256 + (k + 1) * 128], in_=XT[:, k, t0:t0 + rows], identity=IDb),
                                reads=[bXT[k], bCb], writes=[PB[3]])
                    nchk = len(chunks)
                    rws = 128 if chunks[-1] < 16 else 32
                    S.add("act", lambda e, pv=pv, c4=c4, nchk=nchk, rws=rws: e.copy(
                        out=xtok[0:rws, c4 * 4:c4 * 4 + nchk, :], in_=pv[0:rws, 0:nchk * 256].rearrange("p (c x) -> p c x", x=256)),
                        reads=[PB[3]], writes=[bxtk[c_] for c_ in chunks] + [PB[3]])
                    for j, c in enumerate(chunks):
                        t0, rows = TT[c]
                        S.add("pe", lambda e, pv=pv, j=j, t0=t0, rows=rows: e.transpose(
                            out=pv[0:rows, j * 128:(j + 1) * 128], in_=XT[:, 2, t0:t0 + rows], identity=IDb),
                            reads=[bXT[2], bCb], writes=[PB[3]])
                    S.add("act", lambda e, pv=pv, c4=c4, nchk=nchk, rws=rws: e.copy(
                        out=btok[0:rws, c4 * 4:c4 * 4 + nchk, :], in_=pv[0:rws, 0:nchk * 128].rearrange("p (c x) -> p c x", x=128)),
                        reads=[PB[3]], writes=[bbtok, PB[3]])
                    for j, c in enumerate(chunks):
                        t0, rows = TT[c]
                        S.add("pe", lambda e, j=j, t0=t0, rows=rows: e.matmul(
                            PS[2][0:rows, j * 128:j * 128 + rows], lhsT=XT[:, 2, t0:t0 + rows], rhs=XT[:, 3, t0:t0 + rows],
                            start=True, stop=True), reads=[bXT[2], bXT[3]], writes=[PB[2]])
                    if rws == 128:
                        S.add("dve", lambda e, c4=c4, nchk=nchk: e.tensor_tensor(
                            out=CBm[:, c4 * 4:c4 * 4 + nchk, :], in0=PS[2][:, 0:nchk * 128].rearrange("p (c x) -> p c x", x=128),
                            in1=TRIpb.unsqueeze(1).to_broadcast([128, nchk, 128]), op=ALU.mult),
                            reads=[PB[2], bCb], writes=[bCBm, PB[2]])
                    else:
                        S.add("dve", lambda e: e.tensor_tensor(out=CBm[0:32, 16, 0:32], in0=PS[2][0:32, 0:32], in1=TRIsb[0:32, 0:32], op=ALU.mult),
                              reads=[PB[2], bCb], writes=[bCBm, PB[2]])
                S.add("pool", lambda e: e.tensor_tensor(out=CTs[:], in0=XT[:, 3, LP:NT].unsqueeze(1).to_broadcast([128, 4, 32]),
                                                        in1=SEGXb.rearrange("p (s t) -> p s t", s=4), op=ALU.mult),
                      reads=[bXT[3], bCb], writes=[bCTs])
                S.add("dve", lambda e, g=g: e.tensor_tensor(out=dsg[:], in0=dtw[0:32, 16, 4 * g:4 * g + 4].unsqueeze(1).to_broadcast([32, 4, 4]),
                                                           in1=SEG.unsqueeze(2).to_broadcast([32, 4, 4]), op=ALU.mult), reads=[bT, bC], writes=[bdsg])
                S.add("pool", lambda e: e.tensor_copy(out=xsr[:], in_=xtok[0:32, 16, :]), reads=[bxtk[16]], writes=[bxsr])
                S.add("dve", lambda e: e.memset(gs[:], 0.0), writes=[bgs])
                S.add("dve", lambda e: e.memset(STt[:], 0.0), writes=[bST])
                S.add("dve", lambda e: e.memset(STb[:], 0.0), writes=[bSTb])

            def g_chunk_items(g):
                chunk_items = []
                for c in range(17):
                    def s0(c=c, g=g):
                        t0, rows = TT[c]
                        a_, ba_ = abc[c % 2], babc[c % 2]
                        d_, bd_ = Dt[c % 2], bDt[c % 2]
                        e_, be_ = Et[c % 2], bEt[c % 2]
                        w_, bw_ = Bw[c % 2], bBw[c % 2]
                        xdt, bxdt = xdt2[c % 2], bxdt2[c % 2]
                        xdb, bxdb = xdb2[c % 2], bxdb2[c % 2]
                        S.dma(lambda e: e.dma_start(out=a_[:, :, 0:rows], in_=acT_d[4 * g:4 * g + 4, t0:t0 + rows].partition_broadcast(128)),
                              reads=[bscr], writes=[ba_])
                        for j in (2, 3):
                            hh = 4 * g + j
                            S.add("dve", lambda e, j=j, hh=hh: e.tensor_scalar(
                                out=d_[0:rows, j - 2, 0:rows], in0=a_[0:rows, j, 0:rows], scalar1=acum[0:rows, c, hh:hh + 1], scalar2=0.0,
                                op0=ALU.subtract, op1=ALU.min), reads=[ba_, bT], writes=[bd_])
                        for j in (0, 1):
                            hh = 4 * g + j
                            S.add("act", lambda e, j=j, hh=hh: e.activation(out=e_[0:rows, j, 0:rows], in_=a_[0:rows, j, 0:rows], func=AF.Exp,
                                                                           bias=nacum[0:rows, c, hh:hh + 1]), reads=[ba_, bT], writes=[be_])
                        S.add("act", lambda e: e.activation(out=e_[0:rows, 2:4, 0:rows], in_=d_[0:rows, 0:2, 0:rows], func=AF.Exp), reads=[bd_], writes=[be_])
                        S.add("pool", lambda e: e.tensor_tensor(
                            out=xdt[0:rows, :].rearrange("p (h x) -> p h x", h=4), in0=xtok[0:rows, c, :].rearrange("p (h x) -> p h x", h=4),
                            in1=dtt[0:rows, c, 4 * g:4 * g + 4].unsqueeze(2).to_broadcast([rows, 4, 64]), op=ALU.mult), reads=[bxtk[c], bT], writes=[bxdt])
                        S.add("pool", lambda e: e.tensor_tensor(
                            out=xdb[0:rows, :].rearrange("p (h x) -> p h x", h=4), in0=xtok[0:rows, c, :].rearrange("p (h x) -> p h x", h=4),
                            in1=hp_bc[0:rows, 64 + 4 * g:68 + 4 * g].unsqueeze(2).to_broadcast([rows, 4, 64]), op=ALU.mult), reads=[bxtk[c], bhp], writes=[bxdb])
                        if c < 16:
                            S.add("pool", lambda e: e.tensor_tensor(
                                out=w_[:, :].rearrange("p (h x) -> p h x", h=4), in0=xtok[:, c, :].rearrange("p (h x) -> p h x", h=4),
                                in1=dtw[:, c, 4 * g:4 * g + 4].unsqueeze(2).to_broadcast([128, 4, 64]), op=ALU.mult), reads=[bxtk[c], bT], writes=[bw_])

                    def s1(c=c, g=g, wzt=wzt, bwzt=bwzt):
                        t0, rows = TT[c]
                        e_, be_ = Et[c % 2], bEt[c % 2]
                        m_, bm_ = Mt[c % 2], bMt[c % 2]
                        w_, bw_ = Bw[c % 2], bBw[c % 2]
                        xdt, bxdt = xdt2[c % 2], bxdt2[c % 2]
                        xdb, bxdb = xdb2[c % 2], bxdb2[c % 2]
                        yb, byb = PS[4 + (c % 2)], PB[4 + (c % 2)]
                        sbk, bsbk = PS[6], PB[6]
                        zb, bzb = PS[7], PB[7]
                        S.add("dve", lambda e: e.scalar_tensor_tensor(
                            out=m_[0:rows, :, 0:rows], in0=e_[0:rows, :, 0:rows], scalar=1.0, in1=CBm[0:rows, c, 0:rows].unsqueeze(1).to_broadcast([rows, 4, rows]),
                            op0=ALU.min, op1=ALU.mult), reads=[be_, bCBm], writes=[bm_])
                        for kc in range(8):
                            S.add("pe", lambda e, kc=kc: e.matmul(zb[0:rows, 0:256], lhsT=XN[:, kc, t0:t0 + rows], rhs=wzt[:, kc, :],
                                                                 start=(kc == 0), stop=(kc == 7)), reads=[bXN, bwzt], writes=[bzb])
                        if c < 16:
                            S.add("pe", lambda e: e.matmul(sbk[:, 0:256], lhsT=btok[:, c, :], rhs=w_[:, :], start=True, stop=True),
                                  reads=[bw_, bbtok], writes=[bsbk])
                            S.add("pe", lambda e: e.matmul(yb[:, 256:512], lhsT=XT[:, 3, t0:t0 + 128], rhs=STb[:],
                                                           start=True, stop=True, skip_group_check=True), reads=[bXT[3], bSTb], writes=[byb])
                        else:
                            for sq in range(4):
                                S.add("pe", lambda e, sq=sq: e.matmul(yb[0:32, 256:512], lhsT=CTs[:, sq, :], rhs=h0T[:, sq, :],
                                                                     start=(sq == 0), stop=(sq == 3), skip_group_check=True), reads=[bCTs, bh0T], writes=[byb])
                        S.add("pe", lambda e: e.matmul(yb[0:rows, 0:256], lhsT=IDb[0:rows, 0:rows], rhs=xdb[0:rows, :],
                                                       start=False, stop=False, skip_group_check=True), reads=[bxdb, bCb], writes=[byb])
                        for j in range(4):
                            S.add("pe", lambda e, j=j: e.matmul(yb[0:rows, 64 * j:64 * j + 64], lhsT=m_[0:rows, j, 0:rows], rhs=xdt[0:rows, 64 * j:64 * j + 64],
                                                               start=False, stop=True, skip_group_check=True), reads=[bm_, bxdt], writes=[byb])

                    def s2(c=c, g=g):
                        t0, rows = TT[c]
                        y_, by_ = yt[c % 2], byt[c % 2]
                        yb, byb = PS[4 + (c % 2)], PB[4 + (c % 2)]
                        sbk, bsbk = PS[6], PB[6]
                        zb, bzb = PS[7], PB[7]
                        if c < 16:
                            S.add("pool", lambda e: e.tensor_tensor(
                                out=STt[:].rearrange("p (h x) -> p h x", h=4), in0=STt[:].rearrange("p (h x) -> p h x", h=4),
                                in1=decay[:, c, 4 * g:4 * g + 4].unsqueeze(2).to_broadcast([128, 4, 64]), op=ALU.mult), reads=[bT, bSTb], writes=[bST])
                            S.add("dve", lambda e: e.tensor_tensor(out=STt[:], in0=STt[:], in1=sbk[:, 0:256], op=ALU.add), reads=[bsbk], writes=[bST, bsbk])
                            S.add("act", lambda e: e.copy(out=STb[:], in_=STt[:]), reads=[bST], writes=[bSTb])
                        S.add("dve", lambda e: e.tensor_tensor(
                            out=y_[0:rows, :].rearrange("p (h x) -> p h x", h=4), in0=yb[0:rows, 256:512].rearrange("p (h x) -> p h x", h=4),
                            in1=eacum[0:rows, c, 4 * g:4 * g + 4].unsqueeze(2).to_broadcast([rows, 4, 64]), op=ALU.mult), reads=[byb, bT], writes=[by_])
                        S.add("dve", lambda e: e.tensor_tensor(out=y_[0:rows, :], in0=y_[0:rows, :], in1=yb[0:rows, 0:256], op=ALU.add), reads=[byb], writes=[by_, byb])
                        S.add("act", lambda e: e.activation(out=tz[0:rows, :], in_=zb[0:rows, 0:256], func=AF.Tanh, scale=0.5), reads=[bzb], writes=[btz])
                        S.add("dve", lambda e: e.scalar_tensor_tensor(out=tz[0:rows, :], in0=tz[0:rows, :], scalar=1.0, in1=zb[0:rows, 0:256],
                                                                      op0=ALU.add, op1=ALU.mult), reads=[bzb], writes=[btz, bzb])
                        S.add("dve", lambda e: e.tensor_tensor(out=tz[0:rows, :], in0=tz[0:rows, :], in1=y_[0:rows, :], op=ALU.mult), reads=[by_], writes=[btz])
                        S.add("act", lambda e: e.activation(out=ynb[0:rows, :], in_=tz[0:rows, :], func=AF.Square, accum_out=gs[0:rows, c:c + 1]),
                              reads=[btz], writes=[bynb, bgs])
                        S.add("act", lambda e: e.copy(out=xtok[0:rows, c, :], in_=tz[0:rows, :]), reads=[btz], writes=[bxtk[c]])
                    chunk_items.append([s0, s1, s2])
                return chunk_items

            def g_post(g):
                S.add("act", lambda e: e.activation(out=gs[:, 17:34], in_=gs[:, 0:17], func=AF.Sqrt, scale=1.0 / 256, bias=4e-5), reads=[bgs], writes=[bgs])
                S.add("dve", lambda e: e.reciprocal(out=gs[:, 34:51], in_=gs[:, 17:34]), reads=[bgs], writes=[bgs])
                norm_items = []
                for c in range(17):
                    def n0(c=c):
                        t0, rows = TT[c]
                        yn_, byn_ = ynb2[c % 2], bynb2[c % 2]
                        S.add("dve", lambda e: e.scalar_tensor_tensor(out=yn_[0:rows, :], in0=xtok[0:rows, c, :], scalar=gs[0:rows, 34 + c:35 + c], in1=nrm[0:rows, :],
                                                                      op0=ALU.mult, op1=ALU.mult), reads=[bxtk[c], bgs, bnrm], writes=[byn_])

                    def n1(c=c, g=g):
                        t0, rows = TT[c]
                        yn_, byn_ = ynb2[c % 2], bynb2[c % 2]
                        pv = PS[3][:].bitcast(BF16)
                        for k in range(2):
                            S.add("pe", lambda e, k=k: e.transpose(out=pv[:, k * 128:k * 128 + rows], in_=yn_[0:rows, k * 128:(k + 1) * 128],
                                                                  identity=IDb[0:rows, 0:rows]), reads=[byn_, bCb], writes=[PB[3]])
                        S.add("act", lambda e: e.copy(out=ysT[:, 2 * g:2 * g + 2, t0:t0 + rows], in_=pv[:, 0:256].rearrange("p (k t) -> p k t", k=2)[:, :, 0:rows]),
                              reads=[PB[3]], writes=[bys, PB[3]])
                    norm_items.append([n0, n1])
                pipeline(norm_items)
                for k in range(2):
                    S.add("pe", lambda e, k=k: e.transpose(out=PS[2][:, k * 128:(k + 1) * 128], in_=STt[:, k * 128:(k + 1) * 128], identity=IDf),
                          reads=[bST, bC], writes=[PB[2]])
                S.add("act", lambda e: e.copy(out=stp[:], in_=PS[2][:, 0:256].rearrange("p (k n) -> p k n", k=2)), reads=[PB[2]], writes=[bstp, PB[2]])
                S.dma(lambda e, g=g: e.dma_start(out=ssm_p[4 * g:4 * g + 4, :, :].rearrange("(a b) p n -> (b p) a n", b=2), in_=stp[:]), reads=[bstp])
                S.add("dve", lambda e, g=g: e.tensor_scalar(out=dtaE[:].rearrange("p (h x) -> p h x", h=4),
                                                           in0=totb[0:32, 16, 4 * g:4 * g + 4].unsqueeze(2).to_broadcast([32, 4, 64]),
                                                           scalar1=0.125, scalar2=None, op0=ALU.mult), reads=[bT], writes=[bdtaE])
                for k in range(2):
                    S.add("pe", lambda e, k=k: e.matmul(PS[2][:, 256 + 4 * k:260 + 4 * k], lhsT=dtaE[:, k * 128:(k + 1) * 128], rhs=SEG,
                                                       start=True, stop=True), reads=[bdtaE, bC], writes=[PB[2]])
                S.add("act", lambda e: e.activation(out=dcol[:], in_=PS[2][:, 256:264].rearrange("p (k s) -> p k s", k=2), func=AF.Exp),
                      reads=[PB[2]], writes=[bdcol, PB[2]])
                for s in range(4):
                    S.dma(lambda e, g=g, s=s: e.dma_start(out=h0n[s % 2][:], in_=st_nat[s, 4 * g:4 * g + 4, :, :].rearrange("(a b) p n -> (b p) a n", b=2)),
                          writes=[bh0n[s % 2]])
                    S.add("dve", lambda e, s=s: e.tensor_tensor(out=wxm[:, :].rearrange("p (h x) -> p h x", h=4),
                                                               in0=xsr[:, :].rearrange("p (h x) -> p h x", h=4),
                                                               in1=dsg[:, s, :].unsqueeze(2).to_broadcast([32, 4, 64]), op=ALU.mult),
                          reads=[bxsr, bdsg], writes=[bwxm])
                    for k in range(2):
                        S.add("pe", lambda e, s=s, k=k: e.matmul(PS[6 + (s % 2)][:, k * 128:(k + 1) * 128],
                                                                lhsT=wxm[:, k * 128:(k + 1) * 128], rhs=btok[0:32, 16, :], start=True, stop=True),
                              reads=[bwxm, bbtok], writes=[PB[6 + (s % 2)]])
                    for k in range(2):
                        S.add("dve", lambda e, s=s, k=k: e.scalar_tensor_tensor(out=nst[s % 2][:, k, :], in0=h0n[s % 2][:, k, :], scalar=dcol[:, k, s:s + 1],
                                                                              in1=PS[6 + (s % 2)][:, k * 128:(k + 1) * 128], op0=ALU.mult, op1=ALU.add),
                              reads=[bh0n[s % 2], bdcol, PB[6 + (s % 2)]], writes=[bnst[s % 2], PB[6 + (s % 2)]])
                    S.dma(lambda e, g=g, s=s: e.dma_start(out=ssm_s[s, 4 * g:4 * g + 4, :, :].rearrange("(a b) p n -> (b p) a n", b=2), in_=nst[s % 2][:]), reads=[bnst[s % 2]])

            def interleave(a, b):
                out = []
                nb = len(b)
                for i, x in enumerate(a):
                    out.append(x)
                    if i < nb:
                        out.append(b[i])
                out.extend(b[len(a):])
                return out

            g_loadw(0)
            for g in range(8):
                g_pre(g)
                pipeline(g_conv_items(g, range(4)))
                g_mid(g)
                if g + 1 < 8:
                    g_loadw(g + 1)
                pipeline(g_chunk_items(g))
                g_post(g)
        S.barrier(scr)
        tabs.close()
        nwb[0] = 1

        build_attention_and_tail(nc, S, top, SB, PS, PB, XN, bXN, ysT, bys, load_w, win_seg, w_in_v, C, Cb, bC, bCb, scr, hp_bc, bhp,
                                 dict(x_all=x_all, w_a=w_a, w_b=w_b, w_o=w_o, final_g=final_g, tbraw=tbraw, fbraw=fbraw, fnraw=fnraw,
                                      cache_k=cache_k, cache_v=cache_v, y_out=y_out, k_out=k_out, v_out=v_out), TT, TB)
        S.emit(top)
    return nc


def build_attention_and_tail(nc, S, top, SB, PS, PB, XN, bXN, ysT, bys, load_w, win_seg, w_in_v, C, Cb, bC, bCb, scr, hp_bc, bhp, T, TT, TB):
    IDf = C[:, 0:128]; IDb = Cb[:, 0:128]; ONESb = Cb[:, 384:512]
    x_all, w_a, w_b, w_o, final_g = T["x_all"], T["w_a"], T["w_b"], T["w_o"], T["final_g"]
    k_out, v_out, y_out = T["k_out"], T["v_out"], T["y_out"]
    yaT = SB(top, "yaT", [128, 6, NT], BF16); bya = Buf()
    Qs = SB(top, "Qs", [32, 768], BF16); bQs = Buf()
    Ks = SB(top, "Ks", [32, 768], F32); bKs = Buf()
    Vs = SB(top, "Vs", [32, 768], BF16); bVs = Buf()

    with ExitStack() as p3:
        TM = SB(p3, "TM", [128, 2, 640], BF16); bTM = Buf()
        tstage = SB(p3, "tstage", [128, 2, 640], F32); bts = Buf()
        QT = SB(p3, "QT", [128, NT], BF16); bQT = Buf()
        KT = SB(p3, "KT", [128, NT], BF16); bKT = Buf()
        Kf = SB(p3, "Kf", [128, NT], F32); bKf = Buf()
        Vf = Kf; bVf = bKf
        VTb = SB(p3, "VTb", [128, NT], BF16); bVTb = Buf()
        SG = SB(p3, "SG", [128, NT], BF16); bSG = Buf()
        tg = SB(p3, "tg", [128, 512], F32); btg = Buf()
        Vx = [SB(p3, "Vx%d" % i, [128, 16, 2, 128], BF16) for i in range(3)]; bVx = [Buf(), Buf(), Buf()]
        ost = [SB(p3, "ost%d" % i, [128, 4, 128], F32) for i in range(1)] * 2; bost = [Buf()] * 2
        Pt = [SB(p3, "Pt%d" % i, [128, 512], BF16) for i in range(3)]; bPt = [Buf(), Buf(), Buf()]
        rec = SB(p3, "rec", [128, 512], F32); brec = Buf()
        tmpo = rec; btmpo = Buf()
        for i in range(3):
            S.add("pool", lambda e, i=i: e.memset(Vx[i][:], 1.0), writes=[bVx[i]])

        for hp in range(6):
            S.dma(lambda e, hp=hp: e.dma_start(out=tstage[:], in_=T["tbraw"][:, 2 * hp:2 * hp + 2, :]), writes=[bts])
            S.add("act", lambda e: e.copy(out=TM[:], in_=tstage[:]), reads=[bts], writes=[bTM])
            wt, bwt = load_w([win_seg(128 * hp, 128, 0), win_seg(768 + 128 * hp, 128, 128),
                              win_seg(1536 + 128 * hp, 128, 256), win_seg(2304 + 128 * hp, 128, 384)])
            def emit_kv_out(which, src, bsrc, dst, hp=hp):
                for c4 in range(5):
                    tiles = list(range(c4 * 4, min(c4 * 4 + 4, 17)))
                    o_, bo_ = ost[(which * 5 + c4) % 2], bost[(which * 5 + c4) % 2]
                    for j, ti in enumerate(tiles):
                        t0, rows = TT[ti]
                        S.add("pe", lambda e, j=j, t0=t0, rows=rows, src=src: e.transpose(out=PS[2][0:rows, j * 128:(j + 1) * 128], in_=src[:, t0:t0 + rows], identity=IDf),
                              reads=[bsrc, bC], writes=[PB[2]])
                    nt_ = len(tiles)
                    rws = 128 if tiles[-1] < 16 else 32
                    S.add("act", lambda e, o_=o_, nt_=nt_, rws=rws: e.copy(out=o_[0:rws, 0:nt_, :], in_=PS[2][0:rws, 0:nt_ * 128].rearrange("p (c x) -> p c x", x=128)),
                          reads=[PB[2]], writes=[bo_])
                    if which == 1 and rws == 128:
                        S.add("dve", lambda e, c4=c4: e.tensor_copy(out=Vx[0][:, c4 * 4:c4 * 4 + 4, 0, 0:64], in_=PS[2][:, :].rearrange("p (c x) -> p c x", x=128)[:, :, 0:64]),
                              reads=[PB[2]], writes=[bVx[0]])
                        S.add("dve", lambda e, c4=c4: e.tensor_copy(out=Vx[0][:, c4 * 4:c4 * 4 + 4, 1, 64:128], in_=PS[2][:, :].rearrange("p (c x) -> p c x", x=128)[:, :, 64:128]),
                              reads=[PB[2]], writes=[bVx[0], PB[2]])
                    if rws == 32:
                        tgt, btgt = (Ks, bKs) if which == 0 else (Vs, bVs)
                        S.add("dve", lambda e, tgt=tgt, hp=hp: e.tensor_copy(out=tgt[:, 128 * hp:128 * (hp + 1)], in_=PS[2][0:32, 0:128]), reads=[PB[2]], writes=[btgt, PB[2]])
                    S.add("dve", lambda e: e.memset(scr[0:1, 15:16], 0.0), reads=[PB[2]], writes=[PB[2]])
                    if rws == 128:
                        S.dma(lambda e, o_=o_, c4=c4, hp=hp, dst=dst: e.dma_start(
                            out=dst[c4 * 512:(c4 + 1) * 512, 128 * hp:128 * (hp + 1)].rearrange("(c p) x -> p c x", p=128), in_=o_[:, :, :]), reads=[bo_])
                    else:
                        S.dma(lambda e, o_=o_, hp=hp, dst=dst: e.dma_start(out=dst[LP:NT, 128 * hp:128 * (hp + 1)], in_=o_[0:32, 0, :]), reads=[bo_])
            for seg in range(4):
                for bi, (t0, n) in enumerate(TB):
                    pb, bpb = PS[bi % 2], PB[bi % 2]
                    for kc in range(8):
                        S.add("pe", lambda e, pb=pb, kc=kc, t0=t0, n=n, seg=seg, wt=wt: e.matmul(
                            pb[:, 0:n], lhsT=wt[:, kc, seg * 128:(seg + 1) * 128], rhs=XN[:, kc, t0:t0 + n],
                            start=(kc == 0), stop=(kc == 7)), reads=[bXN, bwt], writes=[bpb])
                    if seg == 0:
                        S.add("act", lambda e, pb=pb, t0=t0, n=n: e.activation(out=QT[:, t0:t0 + n], in_=pb[:, 0:n], func=AF.Identity, scale=0.125),
                              reads=[bpb], writes=[bQT, bpb])
                    elif seg == 1:
                        S.add("act", lambda e, pb=pb, t0=t0, n=n: e.copy(out=KT[:, t0:t0 + n], in_=pb[:, 0:n]), reads=[bpb], writes=[bKT])
                        S.add("dve", lambda e, pb=pb, t0=t0, n=n: e.tensor_copy(out=Kf[:, t0:t0 + n], in_=pb[:, 0:n]), reads=[bpb], writes=[bKf, bpb])
                    elif seg == 2:
                        S.add("act", lambda e, pb=pb, t0=t0, n=n: e.copy(out=VTb[:, t0:t0 + n], in_=pb[:, 0:n]), reads=[bpb], writes=[bVTb])
                        S.add("dve", lambda e, pb=pb, t0=t0, n=n: e.tensor_copy(out=Vf[:, t0:t0 + n], in_=pb[:, 0:n]), reads=[bpb], writes=[bVf, bpb])
                    else:
                        S.add("act", lambda e, pb=pb, n=n: e.activation(out=tg[:, 0:n], in_=pb[:, 0:n], func=AF.Tanh, scale=0.5), reads=[bpb], writes=[btg])
                        S.add("dve", lambda e, pb=pb, t0=t0, n=n: e.scalar_tensor_tensor(out=SG[:, t0:t0 + n], in0=tg[:, 0:n], scalar=1.0, in1=pb[:, 0:n],
                                                                                       op0=ALU.add, op1=ALU.mult), reads=[btg, bpb], writes=[bSG, bpb])
                if seg == 1:
                    emit_kv_out(0, Kf, bKf, k_out)
                elif seg == 2:
                    emit_kv_out(1, Vf, bVf, v_out)
            pv = PS[3][:].bitcast(BF16)
            S.add("pe", lambda e, pv=pv: e.transpose(out=pv[0:32, 0:128], in_=QT[:, LP:NT], identity=IDb), reads=[bQT, bCb], writes=[PB[3]])
            S.add("act", lambda e, pv=pv, hp=hp: e.copy(out=Qs[:, 128 * hp:128 * (hp + 1)], in_=pv[0:32, 0:128]), reads=[PB[3]], writes=[bQs, PB[3]])
            for pi, (mod, vx, bvx) in enumerate(((4, Vx[1], bVx[1]), (16, Vx[2], bVx[2]))):
                for half in range(2):
                    for j in range(8):
                        tl = half * 8 + j
                        if mod == 4:
                            r, cbk = tl // 4, tl % 4
                            src = VTb[:, r + 512 * cbk:r + 512 * cbk + 512:4]
                        else:
                            src = VTb[:, tl:LP:16]
                        S.add("pe", lambda e, pv=pv, j=j, src=src: e.transpose(out=pv[:, j * 128:(j + 1) * 128], in_=src, identity=IDb),
                              reads=[bVTb, bCb], writes=[PB[3]])
                    S.add("act", lambda e, pv=pv, vx=vx, half=half: e.copy(out=vx[:, half * 8:half * 8 + 8, 0, 0:64],
                                                                        in_=pv[:, :].rearrange("p (c x) -> p c x", x=128)[:, :, 0:64]), reads=[PB[3]], writes=[bvx])
                    S.add("dve", lambda e, pv=pv, vx=vx, half=half: e.tensor_copy(out=vx[:, half * 8:half * 8 + 8, 1, 64:128],
                                                                               in_=pv[:, :].rearrange("p (c x) -> p c x", x=128)[:, :, 64:128]),
                          reads=[PB[3]], writes=[bvx, PB[3]])
            bank_items = []
            bctr = [0]
            for hd in range(2):
                hs = slice(64 * hd, 64 * hd + 64)
                for R in range(4):
                    accb, baccb = PS[6 + ((hd * 4 + R) % 2)], PB[6 + ((hd * 4 + R) % 2)]
                    banks = []
                    for half in range(2):
                        s_mm, pv_mm = [], []
                        qb0 = 4 * R + 2 * half
                        lc = (qb0 - 4 * R) * 128
                        if qb0 > 0:
                            s_mm.append((0, 128, KT[hs, (qb0 - 1) * 128:qb0 * 128], QT[hs, qb0 * 128:(qb0 + 1) * 128]))
                            pv_mm.append((accb[:, lc:lc + 128], 0, 128, Vx[0][:, qb0 - 1, hd, :], bVx[0]))
                        s_mm.append((128, 256, KT[hs, qb0 * 128:(qb0 + 1) * 128], QT[hs, qb0 * 128:(qb0 + 2) * 128]))
                        pv_mm.append((accb[:, lc:lc + 256], 128, 256, Vx[0][:, qb0, hd, :], bVx[0]))
                        s_mm.append((384, 128, KT[hs, (qb0 + 1) * 128:(qb0 + 2) * 128], QT[hs, (qb0 + 1) * 128:(qb0 + 2) * 128]))
                        pv_mm.append((accb[:, lc + 128:lc + 256], 384, 128, Vx[0][:, qb0 + 1, hd, :], bVx[0]))
                        banks.append((s_mm, TM[:, hd, 0:256].unsqueeze(1).to_broadcast([128, 2, 256]), pv_mm, 256))
                    for half in range(2):
                        s_mm, pv_mm = [], []
                        for rl in range(2):
                            r = 2 * half + rl
                            qcols = QT[hs, r + 512 * R:r + 512 * R + 512:4]
                            oc = accb[:, r:512:4]
                            if R > 0:
                                s_mm.append((rl * 256, 128, KT[hs, r + 512 * (R - 1):r + 512 * R:4], qcols))
                                pv_mm.append((oc, rl * 256, 128, Vx[1][:, r * 4 + R - 1, hd, :], bVx[1]))
                            s_mm.append((rl * 256 + 128, 128, KT[hs, r + 512 * R:r + 512 * R + 512:4], qcols))
                            pv_mm.append((oc, rl * 256 + 128, 128, Vx[1][:, r * 4 + R, hd, :], bVx[1]))
                        banks.append((s_mm, TM[:, hd, 256:512].unsqueeze(1).to_broadcast([128, 2, 256]), pv_mm, 256))
                    s_mm, pv_mm = [], []
                    for r in range(16):
                        s_mm.append((r * 32, 32, KT[hs, r:LP:16], QT[hs, r + 512 * R:r + 512 * R + 512:16]))
                        pv_mm.append((accb[:, r:512:16], r * 32, 32, Vx[2][:, r, hd, :], bVx[2]))
                    banks.append((s_mm, TM[:, hd, 512 + 32 * R:544 + 32 * R].unsqueeze(1).to_broadcast([128, 16, 32]), pv_mm, 32))

                    for bk, (s_mm, mask_ap, pv_mm, bw_) in enumerate(banks):
                        i = bctr[0] % 3
                        bctr[0] += 1
                        meng = "dve" if bk == 4 else "pool"

                        def sA(i=i, s_mm=s_mm, mask_ap=mask_ap, bw_=bw_, meng=meng, bk=bk, hd=hd):
                            sb_, bsb_ = PS[3 + i], PB[3 + i]
                            p_, bp_ = Pt[i], bPt[i]
                            if bk < 4:
                                c0g = 0 if bk < 2 else 256
                                for half in range(2):
                                    S.add("pe", lambda e, half=half, c0g=c0g: e.matmul(sb_[:, 256 * half:256 * (half + 1)], lhsT=IDb, rhs=TM[:, hd, c0g:c0g + 256],
                                                                                    start=(half == 0), stop=False, skip_group_check=True),
                                          reads=[bTM, bCb], writes=[bsb_])
                                for (c0, n, lhsT, rhs) in s_mm:
                                    S.add("pe", lambda e, c0=c0, n=n, lhsT=lhsT, rhs=rhs: e.matmul(sb_[:, c0:c0 + n], lhsT=lhsT, rhs=rhs, start=False, stop=True,
                                                                                                skip_group_check=True), reads=[bKT, bQT], writes=[bsb_])
                                S.add("act", lambda e: e.activation(out=p_[:], in_=sb_[:], func=AF.Exp), reads=[bsb_], writes=[bp_, bsb_])
                            else:
                                S.add("pe", lambda e: e.matmul(sb_[:].rearrange("p (a b) -> p a b", b=bw_), lhsT=IDb, rhs=mask_ap,
                                                               start=True, stop=False, skip_group_check=True), reads=[bTM, bCb], writes=[bsb_])
                                for (c0, n, lhsT, rhs) in s_mm:
                                    S.add("pe", lambda e, c0=c0, n=n, lhsT=lhsT, rhs=rhs: e.matmul(sb_[:, c0:c0 + n], lhsT=lhsT, rhs=rhs, start=False, stop=True,
                                                                                                skip_group_check=True), reads=[bKT, bQT], writes=[bsb_])
                                S.add("act", lambda e: e.activation(out=p_[:], in_=sb_[:], func=AF.Exp), reads=[bsb_], writes=[bp_, bsb_])

                        def sB(i=i, pv_mm=pv_mm, bk=bk, accb=accb, baccb=baccb, hd=hd, R=R, hp=hp, last=(bk == len(banks) - 1)):
                            p_, bp_ = Pt[i], bPt[i]
                            for k, (oc, c0, n, lhsT, bl) in enumerate(pv_mm):
                                st = (bk == 0 and k == 0)
                                S.add("pe", lambda e, oc=oc, c0=c0, n=n, lhsT=lhsT, st=st: e.matmul(
                                    oc, lhsT=lhsT, rhs=p_[:, c0:c0 + n], start=st, stop=True, skip_group_check=True),
                                    reads=[bp_, bl], writes=[baccb])
                            if last:
                                ns = slice(64 * hd, 64 * hd + 64)
                                ds_ = slice(64 * (1 - hd), 64 * (1 - hd) + 64)
                                S.add("dve", lambda e: e.reciprocal(out=rec[ds_, :], in_=accb[ds_, :]), reads=[baccb], writes=[brec])
                                S.add("dve", lambda e: e.scalar_tensor_tensor(out=tmpo[ns, :], in0=accb[ns, :], scalar=0.5, in1=rec[ds_, :],
                                                                              op0=ALU.mult, op1=ALU.mult), reads=[baccb, brec], writes=[btmpo, baccb])
                                S.add("dve", lambda e: e.tensor_tensor(out=yaT[ns, hp, 512 * R:512 * (R + 1)], in0=tmpo[ns, :], in1=SG[ns, 512 * R:512 * (R + 1)], op=ALU.mult),
                                      reads=[btmpo, bSG], writes=[bya])
                        bank_items.append([sA, None, sB])
            pipeline(bank_items)
            S.add("pool", lambda e, hp=hp: e.tensor_copy(out=yaT[:, hp, LP:NT], in_=SG[:, LP:NT]), reads=[bSG], writes=[bya])
    S.barrier(scr)

    with ExitStack() as p4:
        FB = SB(p4, "FB", [128, 8, 36], F32); bFB = Buf()
        FN = SB(p4, "FN", [32, 32, 36], F32); bFN = Buf()
        S.dma(lambda e: e.dma_start(out=FB[:], in_=T["fbraw"].rearrange("p i g h -> p i (g h)")), writes=[bFB])
        S.dma(lambda e: e.dma_start(out=FN[:], in_=T["fnraw"].rearrange("p q g h -> p q (g h)")), writes=[bFN])
        SEL = SB(p4, "SEL", [32, 32, 128], BF16); bSEL = Buf()
        S.add("pool", lambda e: e.tensor_copy(out=SEL[:], in_=Cb[0:32, 0:32].unsqueeze(2).to_broadcast([32, 32, 128])), reads=[bCb], writes=[bSEL])
        K1 = [SB(p4, "K1_%d" % i, [128, 768], F32) for i in range(2)]; bK1 = [Buf(), Buf()]
        V1 = [SB(p4, "V1_%d" % i, [128, 768], BF16) for i in range(2)]; bV1 = [Buf(), Buf()]
        K2 = [SB(p4, "K2_%d" % i, [128, 768], F32) for i in range(2)]; bK2 = [Buf(), Buf()]
        V2 = [SB(p4, "V2_%d" % i, [128, 768], BF16) for i in range(2)]; bV2 = [Buf(), Buf()]
        K3 = [SB(p4, "K3_%d" % i, [128, 768], F32) for i in range(2)]; bK3 = [Buf(), Buf()]
        V3 = [SB(p4, "V3_%d" % i, [128, 768], BF16) for i in range(2)]; bV3 = [Buf(), Buf()]
        prod = SB(p4, "prod", [128, 3, 768], F32); bprod = Buf()
        Sc = [SB(p4, "Sc%d" % i, [128, 48], F32) for i in range(2)]; bSc = [Buf(), Buf()]
        Pc = [SB(p4, "Pc%d" % i, [128, 36], BF16) for i in range(2)]; bPc = [Buf(), Buf()]
        Pn = SB(p4, "Pn", [32, 36], F32); bPn = Buf()
        Pnb = [SB(p4, "Pnb%d" % i, [32, 36], BF16) for i in range(2)]; bPnb = [Buf(), Buf()]
        osb = SB(p4, "osb", [128, 12], F32); bosb = Buf()
        rcs = SB(p4, "rcs", [128, 12], F32); brcs = Buf()
        it = 0
        n3 = 0
        n2 = 0
        for s in range(4):
            k1, bk1, v1, bv1 = K1[s % 2], bK1[s % 2], V1[s % 2], bV1[s % 2]
            S.dma(lambda e, k1=k1, s=s: e.dma_start(out=k1[:], in_=T["cache_k"][s, 1920:2048, :]), writes=[bk1])
            S.dma(lambda e, v1=v1, s=s: e.dma_start(out=v1[:], in_=T["cache_v"][s, 1920:2048, :]), writes=[bv1], eng="pool")
            for r in range(4):
                k2, bk2, v2, bv2 = K2[n2 % 2], bK2[n2 % 2], V2[n2 % 2], bV2[n2 % 2]
                n2 += 1
                S.dma(lambda e, k2=k2, s=s, r=r: e.dma_start(out=k2[:], in_=T["cache_k"][s, 1536 + r:2048:4, :]), writes=[bk2])
                S.dma(lambda e, v2=v2, s=s, r=r: e.dma_start(out=v2[:], in_=T["cache_v"][s, 1536 + r:2048:4, :]), writes=[bv2], eng="pool")
                for i in (r, r + 4):
                    k3, bk3, v3, bv3 = K3[n3 % 2], bK3[n3 % 2], V3[n3 % 2], bV3[n3 % 2]
                    n3 += 1
                    S.dma(lambda e, k3=k3, s=s, i=i: e.dma_start(out=k3[:], in_=T["cache_k"][s, i:2048:16, :]), writes=[bk3])
                    S.dma(lambda e, v3=v3, s=s, i=i: e.dma_start(out=v3[:], in_=T["cache_v"][s, i:2048:16, :]), writes=[bv3], eng="pool")
                    col = s * 8 + i
                    sc_, bsc_ = Sc[it % 2], bSc[it % 2]
                    pc_, bpc_ = Pc[it % 2], bPc[it % 2]
                    pnb_, bpnb_ = Pnb[it % 2], bPnb[it % 2]
                    ob, bob = PS[2 + 2 * (it % 2)], PB[2 + 2 * (it % 2)]
                    db, bdb = PS[3 + 2 * (it % 2)], PB[3 + 2 * (it % 2)]
                    it += 1
                    ktiles = ((0, k1, bk1), (1, k2, bk2), (2, k3, bk3))
                    vtiles = ((0, v1, bv1), (1, v2, bv2), (2, v3, bv3))
                    S.add("pe", lambda e, col=col: e.matmul(PS[0][:, 0:512], lhsT=SEL[:, col, :], rhs=Qs[:, 0:512], start=True, stop=True), reads=[bSEL, bQs], writes=[PB[0]])
                    S.add("pe", lambda e, col=col: e.matmul(PS[1][:, 0:256], lhsT=SEL[:, col, :], rhs=Qs[:, 512:768], start=True, stop=True), reads=[bSEL, bQs], writes=[PB[1]])
                    for (gi, kt_, bkt_) in ktiles:
                        S.add("dve", lambda e, gi=gi, kt_=kt_: e.tensor_tensor(out=prod[:, gi, 0:512], in0=kt_[:, 0:512], in1=PS[0][:, 0:512], op=ALU.mult),
                              reads=[bkt_, PB[0]], writes=[bprod])
                        S.add("dve", lambda e, gi=gi, kt_=kt_: e.tensor_tensor(out=prod[:, gi, 512:768], in0=kt_[:, 512:768], in1=PS[1][:, 0:256], op=ALU.mult),
                              reads=[bkt_, PB[1]], writes=[bprod])
                    S.add("dve", lambda e, sc_=sc_: e.tensor_reduce(out=sc_[:, 0:36], in_=prod[:].rearrange("p g (h x) -> p (g h) x", x=64), axis=mybir.AxisListType.X, op=ALU.add),
                          reads=[bprod], writes=[bsc_])
                    S.add("dve", lambda e: e.tensor_tensor(out=prod[0:32, 0, 0:512], in0=Ks[:, 0:512], in1=PS[0][0:32, 0:512], op=ALU.mult), reads=[bKs, PB[0]], writes=[bprod, PB[0]])
                    S.add("dve", lambda e: e.tensor_tensor(out=prod[0:32, 0, 512:768], in0=Ks[:, 512:768], in1=PS[1][0:32, 0:256], op=ALU.mult), reads=[bKs, PB[1]], writes=[bprod, PB[1]])
                    S.add("dve", lambda e, sc_=sc_: e.tensor_reduce(out=sc_[0:32, 36:48], in_=prod[0:32, 0, :].rearrange("p (h x) -> p h x", x=64), axis=mybir.AxisListType.X, op=ALU.add),
                          reads=[bprod], writes=[bsc_])
                    S.add("dve", lambda e, sc_=sc_, i=i: e.tensor_tensor(out=sc_[:, 0:36], in0=sc_[:, 0:36], in1=FB[:, i, :], op=ALU.add), reads=[bFB], writes=[bsc_])
                    S.add("dve", lambda e, sc_=sc_, col=col: e.tensor_tensor(out=Pn[:].rearrange("p (g h) -> p g h", g=3), in0=sc_[0:32, 36:48].unsqueeze(1).to_broadcast([32, 3, 12]),
                                                                           in1=FN[:, col, :].rearrange("p (g h) -> p g h", g=3), op=ALU.add), reads=[bsc_, bFN], writes=[bPn])
                    S.add("act", lambda e, sc_=sc_, pc_=pc_: e.activation(out=pc_[:], in_=sc_[:, 0:36], func=AF.Exp), reads=[bsc_], writes=[bpc_])
                    S.add("act", lambda e, pnb_=pnb_: e.activation(out=pnb_[:], in_=Pn[:], func=AF.Exp), reads=[bPn], writes=[bpnb_])
                    for hp in range(6):
                        for (gi, vt_, bvt_) in vtiles:
                            S.add("pe", lambda e, ob=ob, vt_=vt_, pc_=pc_, hp=hp, gi=gi: e.matmul(ob[:, 2 * hp:2 * hp + 2], lhsT=vt_[:, 128 * hp:128 * (hp + 1)],
                                                                                          rhs=pc_[:, gi * 12 + 2 * hp:gi * 12 + 2 * hp + 2], start=(gi == 0), stop=False),
                                  reads=[bvt_, bpc_], writes=[bob])
                        for gi in range(3):
                            S.add("pe", lambda e, ob=ob, pnb_=pnb_, hp=hp, gi=gi: e.matmul(ob[:, 2 * hp:2 * hp + 2], lhsT=Vs[:, 128 * hp:128 * (hp + 1)],
                                                                                        rhs=pnb_[:, gi * 12 + 2 * hp:gi * 12 + 2 * hp + 2], start=False, stop=(gi == 2)),
                                  reads=[bVs, bpnb_], writes=[bob])
                    for gi in range(3):
                        S.add("pe", lambda e, db=db, pc_=pc_, gi=gi: e.matmul(db[:, 0:12], lhsT=ONESb, rhs=pc_[:, gi * 12:gi * 12 + 12], start=(gi == 0), stop=False),
                              reads=[bCb, bpc_], writes=[bdb])
                    for gi in range(3):
                        S.add("pe", lambda e, db=db, pnb_=pnb_, gi=gi: e.matmul(db[:, 0:12], lhsT=ONESb[0:32, :], rhs=pnb_[:, gi * 12:gi * 12 + 12], start=False, stop=(gi == 2)),
                              reads=[bCb, bpnb_], writes=[bdb])
                    S.add("dve", lambda e, db=db: e.reciprocal(out=rcs[:], in_=db[:, 0:12]), reads=[bdb], writes=[brcs, bdb])
                    S.add("dve", lambda e, ob=ob: e.scalar_tensor_tensor(out=osb[:], in0=ob[:, 0:12], scalar=0.5, in1=rcs[:], op0=ALU.mult, op1=ALU.mult),
                          reads=[bob, brcs], writes=[bosb, bob])
                    cc = LP + col
                    S.add("dve", lambda e, cc=cc: e.tensor_tensor(out=yaT[0:64, :, cc:cc + 1], in0=yaT[0:64, :, cc:cc + 1], in1=osb[0:64, 0:12:2].unsqueeze(2), op=ALU.mult),
                          reads=[bosb], writes=[bya])
                    S.add("dve", lambda e, cc=cc: e.tensor_tensor(out=yaT[64:128, :, cc:cc + 1], in0=yaT[64:128, :, cc:cc + 1], in1=osb[64:128, 1:12:2].unsqueeze(2), op=ALU.mult),
                          reads=[bosb], writes=[bya])
    S.barrier(scr)

    with ExitStack() as p5:
        mT = SB(p5, "mT", [128, 8, NT], BF16); bmT = Buf()
        p5a = ExitStack()
        wd = [SB(p5a, "wd%d" % i, [128, 38, 128], BF16) for i in range(2)]; bwd = [Buf(), Buf()]
        sa = SB(p5a, "sa", [128, 512], F32); bsa = Buf()
        sbb = SB(p5a, "sbb", [128, 512], F32); bsbb = Buf()
        m1 = SB(p5a, "m1", [128, 512], F32); bm1 = Buf()
        m2 = SB(p5a, "m2", [128, 512], F32); bm2 = Buf()
        w_a_v = w_a.rearrange("(kc p) n -> p kc n", p=128)
        w_b_v = w_b.rearrange("(kc p) n -> p kc n", p=128)
        w_o_v = w_o.rearrange("(kc p) n -> p kc n", p=128)
        for dti in range(8):
            w_, bw_ = wd[dti % 2], bwd[dti % 2]
            cs = slice(128 * dti, 128 * (dti + 1))
            S.dma(lambda e, w_=w_, cs=cs: e.dma_start(out=w_[:, 0:6, :], in_=w_a_v[:, :, cs]), writes=[bw_], eng="pool")
            S.dma(lambda e, w_=w_, cs=cs: e.dma_start(out=w_[:, 6:22, :], in_=w_b_v[:, :, cs]), writes=[bw_], eng="pool")
            S.dma(lambda e, w_=w_, dti=dti: e.dma_start(out=w_[:, 22:30, :], in_=w_in_v[:, :, 9248 + 128 * dti:9248 + 128 * (dti + 1)]), writes=[bw_], eng="pool")
            S.dma(lambda e, w_=w_, dti=dti: e.dma_start(out=w_[:, 30:38, :], in_=w_in_v[:, :, 10272 + 128 * dti:10272 + 128 * (dti + 1)]), writes=[bw_], eng="pool")
            for bi, (t0, n) in enumerate(TB):
                o4 = 4 * (bi % 2)
                pA, pBk, pga, pgb = PS[o4], PS[o4 + 1], PS[o4 + 2], PS[o4 + 3]
                bA, bBk, bga, bgb = PB[o4], PB[o4 + 1], PB[o4 + 2], PB[o4 + 3]
                for kc in range(6):
                    S.add("pe", lambda e, pA=pA, w_=w_, kc=kc, t0=t0, n=n: e.matmul(pA[:, 0:n], lhsT=w_[:, kc, :], rhs=yaT[:, kc, t0:t0 + n], start=(kc == 0), stop=(kc == 5)),
                          reads=[bw_, bya], writes=[bA])
                for kc in range(16):
                    S.add("pe", lambda e, pBk=pBk, w_=w_, kc=kc, t0=t0, n=n: e.matmul(pBk[:, 0:n], lhsT=w_[:, 6 + kc, :], rhs=ysT[:, kc, t0:t0 + n], start=(kc == 0), stop=(kc == 15)),
                          reads=[bw_, bys], writes=[bBk])
                for kc in range(8):
                    S.add("pe", lambda e, pga=pga, w_=w_, kc=kc, t0=t0, n=n: e.matmul(pga[:, 0:n], lhsT=w_[:, 22 + kc, :], rhs=XN[:, kc, t0:t0 + n], start=(kc == 0), stop=(kc == 7)),
                          reads=[bw_, bXN], writes=[bga])
                for kc in range(8):
                    S.add("pe", lambda e, pgb=pgb, w_=w_, kc=kc, t0=t0, n=n: e.matmul(pgb[:, 0:n], lhsT=w_[:, 30 + kc, :], rhs=XN[:, kc, t0:t0 + n], start=(kc == 0), stop=(kc == 7)),
                          reads=[bw_, bXN], writes=[bgb])
                S.add("act", lambda e, pga=pga, n=n: e.activation(out=sa[:, 0:n], in_=pga[:, 0:n], func=AF.Tanh, scale=0.5), reads=[bga], writes=[bsa, bga])
                S.add("act", lambda e, pgb=pgb, n=n: e.activation(out=sbb[:, 0:n], in_=pgb[:, 0:n], func=AF.Tanh, scale=0.5), reads=[bgb], writes=[bsbb, bgb])
                S.add("dve", lambda e, pA=pA, n=n: e.scalar_tensor_tensor(out=m1[:, 0:n], in0=sa[:, 0:n], scalar=1.0, in1=pA[:, 0:n], op0=ALU.add, op1=ALU.mult),
                      reads=[bsa, bA], writes=[bm1, bA])
                S.add("dve", lambda e, pBk=pBk, n=n: e.scalar_tensor_tensor(out=m2[:, 0:n], in0=sbb[:, 0:n], scalar=1.0, in1=pBk[:, 0:n], op0=ALU.add, op1=ALU.mult),
                      reads=[bsbb, bBk], writes=[bm2, bBk])
                S.add("dve", lambda e, n=n: e.tensor_tensor(out=m1[:, 0:n], in0=m1[:, 0:n], in1=m2[:, 0:n], op=ALU.add), reads=[bm2], writes=[bm1])
                S.add("act", lambda e, dti=dti, t0=t0, n=n: e.activation(out=mT[:, dti, t0:t0 + n], in_=m1[:, 0:n], func=AF.Identity, scale=0.5),
                      reads=[bm1], writes=[bmT])
        S.barrier(scr)
        p5a.close()
        wo = SB(p5, "wo", [128, 8, D], BF16); bwo = Buf()
        S.dma(lambda e: e.dma_start(out=wo[:, :, 0:512], in_=w_o_v[:, :, 0:512]), writes=[bwo], eng="pool")
        S.dma(lambda e: e.dma_start(out=wo[:, :, 512:1024], in_=w_o_v[:, :, 512:1024]), writes=[bwo], eng="pool")
        fg = SB(p5, "fg", [128, D], F32); bfg = Buf()
        S.dma(lambda e: e.dma_start(out=fg[:], in_=final_g.to_broadcast([128, D])), writes=[bfg])
        xr = [SB(p5, "xr%d" % i, [128, D], F32) for i in range(2)]; bxr = [Buf(), Buf()]
        yo = xr; byo = bxr
        jk2 = SB(p5, "jk2", [128, D], BF16); bjk2 = Buf()
        fs = SB(p5, "fs", [128, 4], F32); bfs = Buf()
        for ti, (t0, rows) in enumerate(TT):
            x_, bx_ = xr[ti % 2], bxr[ti % 2]
            y_, by_ = yo[ti % 2], byo[ti % 2]
            S.dma(lambda e, x_=x_, t0=t0, rows=rows: e.dma_start(out=x_[0:rows, :], in_=x_all[t0:t0 + rows, :]), writes=[bx_])
            for hf in range(2):
                pb, bpb = PS[2 * (ti % 2) + hf], PB[2 * (ti % 2) + hf]
                for kc in range(8):
                    S.add("pe", lambda e, pb=pb, kc=kc, t0=t0, rows=rows, hf=hf: e.matmul(pb[0:rows, :], lhsT=mT[:, kc, t0:t0 + rows], rhs=wo[:, kc, 512 * hf:512 * (hf + 1)],
                                                                                      start=(kc == 0), stop=(kc == 7)), reads=[bmT, bwo], writes=[bpb])
                S.add("dve", lambda e, pb=pb, x_=x_, hf=hf, rows=rows: e.tensor_tensor(out=x_[0:rows, 512 * hf:512 * (hf + 1)], in0=x_[0:rows, 512 * hf:512 * (hf + 1)], in1=pb[0:rows, :], op=ALU.add),
                      reads=[bpb], writes=[bx_, bpb])
            S.add("dve", lambda e: e.memset(fs[:, 0:1], 0.0), writes=[bfs])
            S.add("act", lambda e, x_=x_, rows=rows: e.activation(out=jk2[0:rows, :], in_=x_[0:rows, :], func=AF.Square, accum_out=fs[0:rows, 0:1]), reads=[bx_], writes=[bjk2, bfs])
            S.add("act", lambda e, rows=rows: e.activation(out=fs[0:rows, 1:2], in_=fs[0:rows, 0:1], func=AF.Sqrt, scale=1.0 / D, bias=1e-6), reads=[bfs], writes=[bfs])
            S.add("dve", lambda e, rows=rows: e.reciprocal(out=fs[0:rows, 2:3], in_=fs[0:rows, 1:2]), reads=[bfs], writes=[bfs])
            S.add("dve", lambda e, x_=x_, y_=y_, rows=rows: e.scalar_tensor_tensor(out=y_[0:rows, :], in0=x_[0:rows, :], scalar=fs[0:rows, 2:3], in1=fg[0:rows, :], op0=ALU.mult, op1=ALU.mult),
                  reads=[bx_, bfs, bfg], writes=[by_])
            S.dma(lambda e, y_=y_, t0=t0, rows=rows: e.dma_start(out=y_out[t0:t0 + rows, :], in_=y_[0:rows, :]), reads=[by_])


_PROG = None


def kernel(x_prompt, x_sample, cache_k, cache_v, state_conv, state_ssm,
           norm_g, w_in, conv_w, conv_b, dt_bias, a_log, d_skip, ssm_norm,
           w_branch_a, w_branch_b, w_out, rel_bias, final_norm):
    global _PROG
    f = np.float32
    asf = lambda a: np.ascontiguousarray(np.asarray(a, dtype=f))
    x_prompt, x_sample = asf(x_prompt), asf(x_sample)
    cache_k, cache_v = np.asarray(cache_k, dtype=f), np.asarray(cache_v, dtype=f)
    state_conv, state_ssm = asf(state_conv), asf(state_ssm)
    rel_bias = asf(rel_bias)
    tb_i, fb_i, fn_i = _static_index_tables()
    rb_ext = np.concatenate([rel_bias, np.full((1, 12), NEG, f)], axis=0)
    tbraw = np.ascontiguousarray(rb_ext[tb_i].transpose(0, 2, 1))
    fbraw = np.ascontiguousarray(rb_ext[fb_i])
    fnraw = np.ascontiguousarray(rb_ext[fn_i])
    cst = _const_pack()
    cwT = np.ascontiguousarray(asf(conv_w)[0].reshape(4, 32, 128).transpose(2, 1, 0))
    cbT = np.ascontiguousarray(asf(conv_b)[0].reshape(32, 128).T)
    hpar = np.concatenate([asf(dt_bias)[0], asf(a_log)[0], asf(d_skip)[0]])[None, :]
    common = dict(w_in=asf(w_in)[0], w_a=asf(w_branch_a)[0], w_b=asf(w_branch_b)[0], w_o=asf(w_out)[0],
                  norm_g=asf(norm_g), final_g=asf(final_norm)[None, :], ssm_norm=asf(ssm_norm),
                  cwT=cwT, cbT=cbT, cb_row=asf(conv_b), hpar=np.ascontiguousarray(hpar), tbraw=tbraw, fbraw=fbraw, fnraw=fnraw, cst=cst)
    in_maps = []
    for c in range(NCORES):
        sl = slice(4 * c, 4 * c + 4)
        m = dict(common)
        m["x_all"] = np.ascontiguousarray(np.concatenate([x_prompt[c], x_sample[sl].reshape(NS, D)], axis=0))
        m["cache_k"] = np.ascontiguousarray(cache_k[0, sl].reshape(4, 2048, 768))
        m["cache_v"] = np.ascontiguousarray(cache_v[0, sl].reshape(4, 2048, 768))
        sc = state_conv[0, sl]
        m["scT"] = np.ascontiguousarray(sc.reshape(4, 3, 32, 128).transpose(3, 2, 0, 1))
        st = state_ssm[0, sl]
        m["st_nat"] = np.ascontiguousarray(st)
        m["st_T"] = np.ascontiguousarray(st.reshape(4, 2048, 128).transpose(2, 0, 1))
        in_maps.append(m)
    if _PROG is None:
        _PROG = build_program()
    res = run_bass_kernel_spmd(_PROG, in_maps, core_ids=list(range(NCORES)))
    R = res.results
    y_p = np.stack([R[c]["y_out"][:LP] for c in range(NCORES)])
    y_s = np.concatenate([R[c]["y_out"][LP:].reshape(4, 8, D) for c in range(NCORES)])
    k_p = np.stack([R[c]["k_out"][:LP].reshape(LP, 12, 64) for c in range(NCORES)])[None]
    v_p = np.stack([R[c]["v_out"][:LP].reshape(LP, 12, 64) for c in range(NCORES)])[None]
    k_s = np.concatenate([R[c]["k_out"][LP:].reshape(4, 8, 12, 64) for c in range(NCORES)])[None]
    v_s = np.concatenate([R[c]["v_out"][LP:].reshape(4, 8, 12, 64) for c in range(NCORES)])[None]
    c_p = np.stack([R[c]["conv_out"][0:3] for c in range(NCORES)])[None]
    c_s = np.concatenate([R[c]["conv_out"][3:15].reshape(4, 3, 4096) for c in range(NCORES)])[None]
    s_p = np.stack([R[c]["ssm_p"] for c in range(NCORES)])[None]
    s_s = np.concatenate([R[c]["ssm_s"] for c in range(NCORES)])[None]
    outs = (y_p, y_s, k_p, v_p, c_p, s_p, k_s, v_s, c_s, s_s)
    return tuple(np.ascontiguousarray(o.astype(np.float32)) for o in outs)
```

```python
import math
import numpy as np
from contextlib import ExitStack
import concourse.bass as bass
import concourse.mybir as mybir
from concourse.bass_utils import run_bass_kernel_spmd

F32 = mybir.dt.float32
BF16 = mybir.dt.bfloat16
ALU = mybir.AluOpType
AF = mybir.ActivationFunctionType

NCORES = 8
D = 1024
LP = 2048
NS = 32
NT = LP + NS
NIN = 11296
NEG = -30000.0


class Buf:
    __slots__ = ("name", "w", "r", "excl")

    def __init__(self, name="", excl=False):
        self.name = name
        self.w = None
        self.r = []
        self.excl = excl


class Op:
    __slots__ = ("eng", "fn", "deps", "hard", "idx", "signal", "token", "is_dma", "prev_token")

    def __init__(self, eng, fn, idx, is_dma):
        self.eng = eng
        self.fn = fn
        self.idx = idx
        self.deps = set()
        self.hard = set()
        self.signal = False
        self.token = None
        self.is_dma = is_dma
        self.prev_token = None


class Sched:
    ENGS = ("pe", "act", "dve", "pool", "sp")

    def __init__(self, nc, n_dma_sems=40):
        self.nc = nc
        self.ops = []
        self.n_dma_sems = n_dma_sems
        self.dma_since_barrier = []

    def add(self, eng, fn, reads=(), writes=(), is_dma=False):
        op = Op(eng, fn, len(self.ops), is_dma)
        self.ops.append(op)
        writes = list(writes) + [b for b in reads if b.excl]
        wset = set(id(b) for b in writes)
        for b in reads:
            if id(b) in wset:
                continue
            if b.w is not None:
                op.deps.add(b.w)
                op.hard.add(b.w)
            b.r.append(op.idx)
        done = set()
        for b in writes:
            if id(b) in done:
                continue
            done.add(id(b))
            if b.w is not None:
                op.deps.add(b.w)
                op.hard.add(b.w)
            op.deps.update(b.r)
            b.w = op.idx
            b.r = []
        op.deps.discard(op.idx)
        op.hard.discard(op.idx)
        if is_dma:
            self.dma_since_barrier.append(op.idx)
        return op

    def dma(self, fn, reads=(), writes=(), eng="sp"):
        return self.add(eng, fn, reads, writes, is_dma=True)

    def barrier(self, scratch):
        bars = {}
        firsts = []
        for i, e in enumerate(self.ENGS):
            b = Buf("bar")
            bars[e] = b
            if e in ("pe", "sp"):
                fn = lambda en: en.nop()
            elif e == "act":
                fn = (lambda en, i=i: en.copy(out=scratch[0:1, i:i + 1], in_=scratch[0:1, i:i + 1]))
            else:
                fn = (lambda en, i=i: en.memset(scratch[0:1, i:i + 1], 0.0))
            op = self.add(e, fn, writes=[b])
            firsts.append(op)
        for op in firsts:
            op.deps.update(self.dma_since_barrier)
            op.deps.discard(op.idx)
        self.dma_since_barrier = []
        for i, e in enumerate(self.ENGS):
            if e in ("pe", "sp"):
                fn = lambda en: en.nop()
            elif e == "act":
                fn = (lambda en, i=i: en.copy(out=scratch[0:1, 8 + i:9 + i], in_=scratch[0:1, 8 + i:9 + i]))
            else:
                fn = (lambda en, i=i: en.memset(scratch[0:1, 8 + i:9 + i], 0.0))
            self.add(e, fn, reads=list(bars.values()))

    def emit(self, stack):
        nc = self.nc
        ops = self.ops
        def needs(op, dop):
            if dop.is_dma or dop.eng != op.eng:
                return True
            if op.eng in ("pe", "sp"):
                return False
            return dop.idx in op.hard

        for op in ops:
            for d in op.deps:
                dop = ops[d]
                if needs(op, dop):
                    dop.signal = True
        esem = {e: stack.enter_context(nc.semaphore("s_" + e)) for e in self.ENGS}
        dsem = [stack.enter_context(nc.semaphore("d%d" % i)) for i in range(self.n_dma_sems)]
        cnt = {e: 0 for e in self.ENGS}
        duse = [0] * self.n_dma_sems
        ndma = 0
        for op in ops:
            if op.is_dma:
                k = ndma % self.n_dma_sems
                ndma += 1
                if duse[k] > 0:
                    op.prev_token = (dsem[k], 16 * duse[k])
                duse[k] += 1
                op.token = (dsem[k], 16 * duse[k])
            elif op.signal:
                cnt[op.eng] += 1
                op.token = (esem[op.eng], cnt[op.eng])
        per_eng = {e: [op for op in ops if op.eng == e] for e in self.ENGS}
        final_dma = [(dsem[k], 16 * duse[k]) for k in range(self.n_dma_sems) if duse[k] > 0]

        def run(ename, e):
            waited = {}
            for op in per_eng[ename]:
                need = {}
                for d in op.deps:
                    dop = ops[d]
                    if not needs(op, dop):
                        continue
                    s, v = dop.token
                    if need.get(s.num, (None, 0))[1] < v:
                        need[s.num] = (s, v)
                if op.prev_token is not None:
                    s, v = op.prev_token
                    if need.get(s.num, (None, 0))[1] < v:
                        need[s.num] = (s, v)
                for sn, (s, v) in need.items():
                    if waited.get(sn, 0) < v:
                        e.wait_ge(s, v)
                        waited[sn] = v
                ins = op.fn(e)
                if op.is_dma:
                    ins.then_inc(op.token[0], 16)
                elif op.signal:
                    ins.then_inc(op.token[0], 1)
            if ename == "sp":
                for s, v in final_dma:
                    if waited.get(s.num, 0) < v:
                        e.wait_ge(s, v)

        block = stack.enter_context(nc.Block())

        @block.tensor
        def _(e):
            run("pe", e)

        @block.scalar
        def _(e):
            run("act", e)

        @block.vector
        def _(e):
            run("dve", e)

        @block.gpsimd
        def _(e):
            run("pool", e)

        @block.sync
        def _(e):
            run("sp", e)


def pipeline(stage_lists):
    n = len(stage_lists)
    ns = max(len(x) for x in stage_lists) if n else 0
    for t in range(n + ns - 1):
        for k in reversed(range(ns)):
            i = t - k
            if 0 <= i < n and k < len(stage_lists[i]) and stage_lists[i][k] is not None:
                stage_lists[i][k]()


def _t5_bucket(dist):
    n_buckets, max_distance = 32, 2048
    max_exact = n_buckets // 2
    d = np.maximum(dist, 1).astype(np.float32)
    large = max_exact + (np.log(d / max_exact) / math.log(max_distance / max_exact)
                         * (n_buckets - max_exact)).astype(np.int32)
    large = np.minimum(large, n_buckets - 1)
    return np.where(dist < max_exact, dist, large).astype(np.int32)


GROUP_D = (1, 4, 16)


def _static_index_tables():
    bk = [_t5_bucket(np.arange(0, 129, dtype=np.int32) * d) for d in GROUP_D]
    k = np.arange(128)[:, None]
    q = np.arange(128)[None, :]
    tb = np.full((128, 640), 32, np.int32)
    for g in range(2):
        dist = q + 128 - k
        tb[:, 256 * g:256 * g + 128] = np.where(dist <= 128, bk[g][np.clip(dist, 0, 128)], 32)
        dist = q - k
        tb[:, 256 * g + 128:256 * g + 256] = np.where(dist >= 0, bk[g][np.clip(dist, 0, 128)], 32)
    dist = q - k
    tb[:, 512:640] = np.where(dist >= 0, bk[2][np.clip(dist, 0, 128)], 32)
    fb = np.full((128, 8, 3), 32, np.int32)
    m = np.arange(128)
    for i in range(8):
        fb[:, i, 2] = bk[2][128 - m]
        j = 128 + (i // 4) - m
        fb[:, i, 1] = np.where((j >= 1) & (j <= 128), bk[1][np.clip(j, 0, 128)], 32)
        j = 128 + i - m
        fb[:, i, 0] = np.where((j >= 1) & (j <= 128), bk[0][np.clip(j, 0, 128)], 32)
    fn = np.full((32, 32, 3), 32, np.int32)
    for s in range(4):
        for i in range(8):
            for ip in range(i + 1):
                dlt = i - ip
                p = s * 8 + ip
                fn[p, s * 8 + i, 0] = bk[0][dlt]
                if dlt % 4 == 0:
                    fn[p, s * 8 + i, 1] = bk[1][dlt // 4]
                if dlt == 0:
                    fn[p, s * 8 + i, 2] = bk[2][0]
    return tb, fb, fn


def _const_pack():
    c = np.zeros((128, 776), np.float32)
    c[:, 0:128] = np.eye(128)
    s = np.arange(128)[:, None]
    l = np.arange(128)[None, :]
    c[:, 128:256] = (s <= l)
    same = (s // 8 == l // 8) & (s < 32) & (l < 32)
    c[:, 256:384] = same & (s <= l)
    c[:, 384:512] = 1.0
    c[:, 512:640] = same
    for sq in range(4):
        c[sq * 8:(sq + 1) * 8, 640 + sq] = 1.0
    for sq in range(4):
        c[:, 648 + sq * 32 + sq * 8: 648 + sq * 32 + sq * 8 + 8] = 1.0
    return c


def build_program():
    nc = bass.Bass("TRN2", target_bir_lowering=False)

    def din(name, shape, dt=F32):
        return nc.dram_tensor(name, list(shape), dt, kind="ExternalInput").ap()

    def dout(name, shape):
        return nc.dram_tensor(name, list(shape), F32, kind="ExternalOutput").ap()

    x_all = din("x_all", [NT, D])
    w_in = din("w_in", [D, NIN])
    w_a = din("w_a", [768, D])
    w_b = din("w_b", [2048, D])
    w_o = din("w_o", [D, D])
    norm_g = din("norm_g", [1, D])
    final_g = din("final_g", [1, D])
    ssm_norm = din("ssm_norm", [1, 2048])
    cwT = din("cwT", [128, 32, 4])
    cbT = din("cbT", [128, 32])
    cb_row = din("cb_row", [1, 4096])
    hpar = din("hpar", [1, 96])
    tbraw = din("tbraw", [128, 12, 640])
    fbraw = din("fbraw", [128, 8, 3, 12])
    fnraw = din("fnraw", [32, 32, 3, 12])
    cst = din("cst", [128, 776])
    cache_k = din("cache_k", [4, 2048, 768])
    cache_v = din("cache_v", [4, 2048, 768])
    scT = din("scT", [128, 32, 4, 3])
    st_nat = din("st_nat", [4, 32, 64, 128])
    st_T = din("st_T", [128, 4, 2048])

    y_out = dout("y_out", [NT, D])
    k_out = dout("k_out", [NT, 768])
    v_out = dout("v_out", [NT, 768])
    conv_out = dout("conv_out", [15, 4096])
    ssm_p = dout("ssm_p", [32, 64, 128])
    ssm_s = dout("ssm_s", [4, 32, 64, 128])
    acT_d = nc.dram_tensor("acT_d", [32, NT], F32, kind="Internal").ap()

    w_in_v = w_in.rearrange("(kc p) n -> p kc n", p=128)

    TT = [(i * 128, 128) for i in range(16)] + [(LP, NS)]
    TB = [(i * 512, 512) for i in range(4)] + [(LP, NS)]

    with ExitStack() as top:
        S = Sched(nc)

        def SB(st, name, shape, dt):
            return st.enter_context(nc.sbuf_tensor(name, list(shape), dt))

        PS = [top.enter_context(nc.psum_tensor("ps%d" % i, [128, 512], F32)) for i in range(8)]
        PB = [Buf("ps%d" % i, excl=True) for i in range(8)]

        XN = SB(top, "XN", [128, 8, NT], BF16); bXN = Buf()
        ysT = SB(top, "ysT", [128, 16, NT], BF16); bys = Buf()
        tabs = ExitStack()
        wbuf = [SB(top, "wbuf0", [128, 8, 512], BF16)]
        bw = [Buf(), Buf()]
        nwb = [2]
        C = SB(top, "C", [128, 776], F32); bC = Buf()
        Cb = SB(top, "Cb", [128, 776], BF16); bCb = Buf()
        scr = SB(top, "scr", [128, 16], F32)
        hp_bc = SB(top, "hp_bc", [128, 96], F32); bhp = Buf()
        IDf = C[:, 0:128]; TRIp = C[:, 128:256]; TRIs = C[:, 256:384]; ONESf = C[:, 384:512]; ONESs = C[:, 512:640]
        SEG = C[0:32, 640:644]
        IDb = Cb[:, 0:128]; TRIpb = Cb[:, 128:256]; TRIsb = Cb[:, 256:384]; ONESb = Cb[:, 384:512]
        SEGXb = Cb[:, 648:776]

        S.dma(lambda e: e.dma_start(out=C[:], in_=cst), writes=[bC])
        S.add("dve", lambda e: e.tensor_copy(out=Cb[:], in_=C[:]), reads=[bC], writes=[bCb])
        S.dma(lambda e: e.dma_start(out=hp_bc[:], in_=hpar.to_broadcast([128, 96])), writes=[bhp])

        wcount = [0]

        def load_w(segs):
            i = wcount[0] % nwb[0]
            wcount[0] += 1
            t, b = wbuf[i], bw[i]
            for (src, nk, c0, n) in segs:
                S.dma(lambda e, src=src, nk=nk, c0=c0, n=n: e.dma_start(out=t[:, 0:nk, c0:c0 + n], in_=src),
                      writes=[b], eng="pool")
            return t, b

        def win_seg(col0, n, c0):
            return (w_in_v[:, :, col0:col0 + n], 8, c0, n)

        with ExitStack() as p0:
            gbc = SB(p0, "gbc", [128, D], F32); bg = Buf()
            xin = [SB(p0, "xin%d" % i, [128, D], F32) for i in range(2)]; bxin = [Buf(), Buf()]
            junk = SB(p0, "junk", [128, D], BF16); bjunk = Buf()
            hb = [SB(p0, "hb%d" % i, [128, D], BF16) for i in range(2)]; bhb = [Buf(), Buf()]
            ssq = SB(p0, "ssq", [128, 51], F32); bss = Buf()
            S.dma(lambda e: e.dma_start(out=gbc[:], in_=norm_g.to_broadcast([128, D])), writes=[bg])
            S.add("dve", lambda e: e.memset(ssq[:], 0.0), writes=[bss])
            for ti, (t0, rows) in enumerate(TT):
                xi, bxi = xin[ti % 2], bxin[ti % 2]
                S.dma(lambda e, xi=xi, t0=t0, rows=rows: e.dma_start(out=xi[0:rows, :], in_=x_all[t0:t0 + rows, :]), writes=[bxi])
                S.add("act", lambda e, xi=xi, rows=rows, ti=ti: e.activation(out=junk[0:rows, :], in_=xi[0:rows, :], func=AF.Square,
                                                                             accum_out=ssq[0:rows, ti:ti + 1]), reads=[bxi], writes=[bjunk, bss])
            S.add("act", lambda e: e.activation(out=ssq[:, 17:34], in_=ssq[:, 0:17], func=AF.Sqrt, scale=1.0 / D, bias=1e-6), reads=[bss], writes=[bss])
            S.add("dve", lambda e: e.reciprocal(out=ssq[:, 34:51], in_=ssq[:, 17:34]), reads=[bss], writes=[bss])
            for ti, (t0, rows) in enumerate(TT):
                xi, bxi = xin[ti % 2], bxin[ti % 2]
                h_, bh_ = hb[ti % 2], bhb[ti % 2]
                pb, bpb = PS[ti % 2], PB[ti % 2]
                S.dma(lambda e, xi=xi, t0=t0, rows=rows: e.dma_start(out=xi[0:rows, :], in_=x_all[t0:t0 + rows, :]), writes=[bxi])
                S.add("dve", lambda e, xi=xi, h_=h_, rows=rows, ti=ti: e.scalar_tensor_tensor(
                    out=h_[0:rows, :], in0=xi[0:rows, :], scalar=ssq[0:rows, 34 + ti:35 + ti], in1=gbc[0:rows, :],
                    op0=ALU.mult, op1=ALU.mult), reads=[bxi, bss, bg], writes=[bh_])
                pv = pb[:].bitcast(BF16)
                for kc in range(8):
                    S.add("pe", lambda e, pv=pv, h_=h_, kc=kc, rows=rows: e.transpose(
                        out=pv[:, kc * 128:kc * 128 + rows], in_=h_[0:rows, kc * 128:(kc + 1) * 128],
                        identity=IDb[0:rows, 0:rows]), reads=[bh_, bCb], writes=[bpb])
                S.add("act", lambda e, pv=pv, t0=t0, rows=rows: e.copy(
                    out=XN[:, :, t0:t0 + rows], in_=pv.rearrange("p (k t) -> p k t", k=8)[:, :, 0:rows]),
                    reads=[bpb], writes=[bXN, bpb])
        S.barrier(scr)

        wbuf.append(SB(tabs, "wbuf1", [128, 8, 512], BF16))
        dtt = SB(tabs, "dtt", [128, 17, 32], F32)
        acum = SB(tabs, "acum", [128, 17, 32], F32)
        eacum = SB(tabs, "eacum", [128, 17, 32], F32)
        dtw = SB(tabs, "dtw", [128, 17, 32], F32)
        decay = SB(tabs, "decay", [128, 17, 32], F32)
        totb = SB(tabs, "totb", [128, 17, 32], F32)
        nacum = SB(tabs, "nacum", [128, 17, 32], F32)
        bT = Buf()
        with ExitStack() as p1:
            dta = SB(p1, "dta", [128, 17, 32], F32)
            acTs = SB(p1, "acTs", [32, NT], F32); bacT = Buf()
            wt, bwt = load_w([win_seg(9216, 32, 0)])
            S.add("dve", lambda e: e.memset(dtt[:], 0.0), writes=[bT])
            for ti, (t0, rows) in enumerate(TT):
                pb, bpb = (PS[2], PB[2]) if ti < 16 else (PS[3], PB[3])
                c0 = (ti % 16) * 32
                for kc in range(8):
                    S.add("pe", lambda e, pb=pb, kc=kc, t0=t0, rows=rows, c0=c0: e.matmul(
                        pb[0:rows, c0:c0 + 32], lhsT=XN[:, kc, t0:t0 + rows], rhs=wt[:, kc, 0:32],
                        start=(kc == 0), stop=(kc == 7)), reads=[bXN, bwt], writes=[bpb])
            S.add("dve", lambda e: e.tensor_tensor(out=dtt[:, 0:16, :], in0=PS[2][:].rearrange("p (c h) -> p c h", h=32),
                                                   in1=hp_bc[:, 0:32].unsqueeze(1).to_broadcast([128, 16, 32]), op=ALU.add),
                  reads=[PB[2], bhp], writes=[bT, PB[2]])
            S.add("dve", lambda e: e.tensor_tensor(out=dtt[0:32, 16, :], in0=PS[3][0:32, 0:32], in1=hp_bc[0:32, 0:32], op=ALU.add),
                  reads=[PB[3], bhp], writes=[bT, PB[3]])
            S.add("act", lambda e: e.activation(out=dtt[:], in_=dtt[:], func=AF.Exp), reads=[bT], writes=[bT])
            S.add("act", lambda e: e.activation(out=dtt[:], in_=dtt[:], func=AF.Ln, bias=1.0), reads=[bT], writes=[bT])
            S.add("dve", lambda e: e.memset(dtt[32:64, 16, :], 0.0), writes=[bT])
            S.add("dve", lambda e: e.memset(dtt[64:128, 16, :], 0.0), writes=[bT])
            S.add("act", lambda e: e.activation(out=hp_bc[:, 32:64], in_=hp_bc[:, 32:64], func=AF.Exp), reads=[bhp], writes=[bhp])
            S.add("dve", lambda e: e.scalar_tensor_tensor(out=dta[:], in0=dtt[:], scalar=-1.0,
                                                          in1=hp_bc[:, 32:64].unsqueeze(1).to_broadcast([128, 17, 32]),
                                                          op0=ALU.mult, op1=ALU.mult), reads=[bT, bhp], writes=[bT])
            for c in range(17):
                tri = TRIp if c < 16 else TRIs
                ones = ONESf if c < 16 else ONESs
                pa, bpa = (PS[4], PB[4]) if c < 16 else (PS[5], PB[5])
                pt, bpt = (PS[6], PB[6]) if c < 16 else (PS[7], PB[7])
                c0 = (c % 16) * 32
                S.add("pe", lambda e, pa=pa, tri=tri, c=c, c0=c0: e.matmul(pa[:, c0:c0 + 32], lhsT=tri, rhs=dta[:, c, :],
                                                                         start=True, stop=True), reads=[bT, bC], writes=[bpa])
                S.add("pe", lambda e, pt=pt, ones=ones, c=c, c0=c0: e.matmul(pt[:, c0:c0 + 32], lhsT=ones, rhs=dta[:, c, :],
                                                                           start=True, stop=True), reads=[bT, bC], writes=[bpt])
            S.add("dve", lambda e: e.tensor_copy(out=acum[:, 0:16, :], in_=PS[4][:].rearrange("p (c h) -> p c h", h=32)),
                  reads=[PB[4]], writes=[bT, PB[4]])
            S.add("dve", lambda e: e.tensor_copy(out=acum[:, 16, :], in_=PS[5][:, 0:32]), reads=[PB[5]], writes=[bT, PB[5]])
            S.add("dve", lambda e: e.tensor_copy(out=totb[:, 0:16, :], in_=PS[6][:].rearrange("p (c h) -> p c h", h=32)),
                  reads=[PB[6]], writes=[bT, PB[6]])
            S.add("dve", lambda e: e.tensor_copy(out=totb[:, 16, :], in_=PS[7][:, 0:32]), reads=[PB[7]], writes=[bT, PB[7]])
            S.add("act", lambda e: e.activation(out=eacum[:], in_=acum[:], func=AF.Exp), reads=[bT], writes=[bT])
            S.add("act", lambda e: e.activation(out=decay[:], in_=totb[:], func=AF.Exp), reads=[bT], writes=[bT])
            S.add("dve", lambda e: e.tensor_sub(out=dtw[:], in0=totb[:], in1=acum[:]), reads=[bT], writes=[bT])
            S.add("act", lambda e: e.activation(out=dtw[:], in_=dtw[:], func=AF.Exp), reads=[bT], writes=[bT])
            S.add("dve", lambda e: e.tensor_mul(out=dtw[:], in0=dtw[:], in1=dtt[:]), reads=[bT], writes=[bT])
            S.add("dve", lambda e: e.tensor_scalar(out=nacum[:], in0=acum[:], scalar1=-1.0, scalar2=None, op0=ALU.mult), reads=[bT], writes=[bT])
            for c in range(17):
                tri = TRIp if c < 16 else TRIs
                pb, bpb = PS[c % 2], PB[c % 2]
                rows = 128 if c < 16 else 32
                S.add("pe", lambda e, pb=pb, tri=tri, c=c, rows=rows: e.matmul(pb[0:32, 0:rows], lhsT=dta[:, c, :], rhs=tri[:, 0:rows],
                                                                             start=True, stop=True), reads=[bT, bC], writes=[bpb])
                S.add("dve", lambda e, pb=pb, c=c, rows=rows: e.tensor_copy(out=acTs[:, c * 128:c * 128 + rows], in_=pb[0:32, 0:rows]),
                      reads=[bpb], writes=[bacT, bpb])
            bscr = Buf()
            S.dma(lambda e: e.dma_start(out=acT_d, in_=acTs[:]), reads=[bacT], writes=[bscr])
        S.barrier(scr)

        with ExitStack() as p2:
            cw = SB(p2, "cw", [128, 32, 4], F32); bcw = Buf()
            cb = SB(p2, "cb", [128, 32], F32)
            sct = SB(p2, "sct", [128, 32, 4, 3], F32)
            S.dma(lambda e: e.dma_start(out=cw[:], in_=cwT), writes=[bcw])
            S.dma(lambda e: e.dma_start(out=cb[:], in_=cbT), writes=[bcw])
            S.dma(lambda e: e.dma_start(out=sct[:], in_=scT), writes=[bcw])
            S.add("pool", lambda e: e.tensor_scalar(out=cw[:], in0=cw[:], scalar1=0.5, scalar2=None, op0=ALU.mult), reads=[bcw], writes=[bcw])
            S.add("pool", lambda e: e.tensor_scalar(out=cb[:], in0=cb[:], scalar1=0.5, scalar2=None, op0=ALU.mult), reads=[bcw], writes=[bcw])
            hsel = SB(p2, "hsel", [128, 8, 15], BF16); bhsel = Buf()
            S.add("pool", lambda e: e.tensor_copy(out=hsel[:, :, 0:3], in_=XN[:, :, 2045:2048]), reads=[bXN], writes=[bhsel])
            for s in range(4):
                S.add("pool", lambda e, s=s: e.tensor_copy(out=hsel[:, :, 3 + 3 * s:6 + 3 * s], in_=XN[:, :, LP + 8 * s + 5:LP + 8 * s + 8]),
                      reads=[bXN], writes=[bhsel])
            wz = [SB(p2, "wz%d" % i, [128, 8, 256], BF16) for i in range(1)] * 2; bwz = [Buf()] * 2
            xwb = SB(p2, "xwb", [128, 3 + LP], BF16); bxw = Buf()
            xwsb = SB(p2, "xwsb", [128, 4, 11], BF16); bxws = Buf()
            tnh2 = [SB(p2, "tnh2_%d" % i, [128, 512], BF16) for i in range(2)]; btnh2 = [Buf(), Buf()]
            dg = SB(p2, "dg", [128, 4, 128], BF16); bdg = Buf()
            cbrf = SB(p2, "cbrf", [1, 128], F32); bcbrf = Buf()
            cbrb = SB(p2, "cbrb", [1, 128], BF16); bcbrb = Buf()
            onesrow = SB(p2, "onesrow", [1, 512], BF16); bones = Buf()
            S.add("pool", lambda e: e.memset(onesrow[:], 1.0), writes=[bones])
            xdt2 = [SB(p2, "xdt%d" % i, [128, 256], BF16) for i in range(2)]; bxdt2 = [Buf(), Buf()]
            xdb2 = [SB(p2, "xdb%d" % i, [128, 256], BF16) for i in range(2)]; bxdb2 = [Buf(), Buf()]
            XT = SB(p2, "XT", [128, 4, NT], BF16); bXT = [Buf() for _ in range(4)]
            xtok = SB(p2, "xtok", [128, 17, 256], BF16); bxtk = [Buf() for _ in range(17)]
            btok = SB(p2, "btok", [128, 17, 128], BF16); bbtok = Buf()
            CBm = SB(p2, "CBm", [128, 17, 128], BF16); bCBm = Buf()
            CTs = SB(p2, "CTs", [128, 4, 32], BF16); bCTs = Buf()
            abc = [SB(p2, "abc%d" % i, [128, 4, 128], F32) for i in range(2)]; babc = [Buf() for _ in range(2)]
            Dt = [SB(p2, "Dt%d" % i, [128, 2, 128], F32) for i in range(1)] * 2; bDt = [Buf()] * 2
            Et = [SB(p2, "Et%d" % i, [128, 4, 128], BF16) for i in range(2)]; bEt = [Buf(), Buf()]
            Mt = [SB(p2, "Mt%d" % i, [128, 4, 128], BF16) for i in range(2)]; bMt = [Buf(), Buf()]
            Bw = [SB(p2, "Bw%d" % i, [128, 256], BF16) for i in range(2)]; bBw = [Buf(), Buf()]
            yt = [SB(p2, "yt%d" % i, [128, 256], F32) for i in range(2)]; byt = [Buf(), Buf()]
            STt = SB(p2, "STt", [128, 256], F32); bST = Buf()
            STb = SB(p2, "STb", [128, 256], BF16); bSTb = Buf()
            tz = SB(p2, "tz", [128, 256], F32); btz = Buf()
            uz = tz; buz = btz
            yg = tz; byg = btz
            ynb2 = [SB(p2, "ynb%d" % i, [128, 256], BF16) for i in range(2)]; bynb2 = [Buf(), Buf()]; ynb = ynb2[0]; bynb = bynb2[0]
            gs = SB(p2, "gs", [128, 51], F32); bgs = Buf()
            nrm = SB(p2, "nrm", [128, 256], F32); bnrm = Buf()
            cvo = abc[0][:, :, :].rearrange("p a b -> p (a b)"); bcvo = babc[0]
            h0T = SB(p2, "h0T", [128, 4, 256], BF16); bh0T = Buf()
            h0n = [SB(p2, "h0n%d" % i, [128, 2, 128], F32) for i in range(1)] * 2; bh0n = [Buf()] * 2
            wxm = SB(p2, "wxm", [32, 256], BF16); bwxm = Buf(); xsr = SB(p2, "xsr", [32, 256], BF16); bxsr = Buf()
            dsg = SB(p2, "dsg", [32, 4, 4], F32); bdsg = Buf()
            dtaE = SB(p2, "dtaE", [32, 256], F32); bdtaE = Buf()
            dcol = SB(p2, "dcol", [128, 2, 4], F32); bdcol = Buf()
            nst = [SB(p2, "nst%d" % i, [128, 2, 128], F32) for i in range(1)] * 2; bnst = [Buf()] * 2
            stp = nst[0]; bstp = bnst[0]
            jk = ynb; bjk = bynb

            WT = {}
            wzt, bwzt = wz[0], bwz[0]

            def g_loadw(g):
                WT[g] = load_w([win_seg(5120 + 256 * g, 256, 0), win_seg(7168 + 128 * g, 128, 256), win_seg(8192 + 128 * g, 128, 384)])

            def g_pre(g):
                wt, bwt = WT[g]
                S.dma(lambda e: e.dma_start(out=wzt[:], in_=w_in_v[:, :, 3072 + 256 * g:3072 + 256 * (g + 1)]), writes=[bwzt], eng="pool")
                S.dma(lambda e: e.dma_start(out=nrm[:], in_=ssm_norm[:, 256 * g:256 * (g + 1)].to_broadcast([128, 256])), writes=[bnrm])
                S.dma(lambda e: e.dma_start(out=h0T[:], in_=st_T[:, :, 256 * g:256 * (g + 1)]), writes=[bh0T], eng="pool")
                for kc in range(8):
                    S.add("pe", lambda e, kc=kc, wt=wt: e.matmul(PS[2][0:15, :], lhsT=hsel[:, kc, :], rhs=wt[:, kc, :],
                                                               start=(kc == 0), stop=(kc == 7)), reads=[bhsel, bwt], writes=[PB[2]])
                S.add("act", lambda e: e.copy(out=cvo[0:15, :], in_=PS[2][0:15, :]), reads=[PB[2]], writes=[bcvo, PB[2]])
                S.dma(lambda e, g=g: e.dma_start(out=conv_out[:, 256 * g:256 * (g + 1)], in_=cvo[0:15, 0:256]), reads=[bcvo])
                S.dma(lambda e, g=g: e.dma_start(out=conv_out[:, 2048 + 128 * g:2048 + 128 * (g + 1)], in_=cvo[0:15, 256:384]), reads=[bcvo])
                S.dma(lambda e, g=g: e.dma_start(out=conv_out[:, 3072 + 128 * g:3072 + 128 * (g + 1)], in_=cvo[0:15, 384:512]), reads=[bcvo])

            def g_conv_items(g, tiles):
                wt, bwt = WT[g]
                conv_items = []
                for ct in tiles:
                    ctile = (2 * g + ct) if ct < 2 else (16 + g if ct == 2 else 24 + g)
                    for bi, (t0, n) in enumerate(TB):
                        def c0(ct=ct, ctile=ctile, bi=bi, t0=t0, n=n, wt=wt, bwt=bwt):
                            pb, bpb = PS[bi % 2], PB[bi % 2]
                            if bi == 0:
                                S.add("pool", lambda e: e.memset(xwb[:, 0:3], 0.0), writes=[bxw])
                                S.add("pool", lambda e: e.tensor_copy(out=xwsb[:, :, 0:3], in_=sct[:, ctile, :, :]), reads=[bcw], writes=[bxws])
                                S.add("pool", lambda e: e.tensor_tensor(out=dg[:], in0=IDb.unsqueeze(1).to_broadcast([128, 4, 128]),
                                                                        in1=cw[:, ctile, :].unsqueeze(2).to_broadcast([128, 4, 128]), op=ALU.mult),
                                      reads=[bCb, bcw], writes=[bdg])
                                S.dma(lambda e: e.dma_start(out=cbrf[:], in_=cb_row[:, ctile * 128:(ctile + 1) * 128]), writes=[bcbrf])
                                S.add("pool", lambda e: e.tensor_scalar(out=cbrb[:], in0=cbrf[:], scalar1=0.5, scalar2=None, op0=ALU.mult), reads=[bcbrf], writes=[bcbrb])
                            for kc in range(8):
                                S.add("pe", lambda e, kc=kc: e.matmul(pb[:, 0:n], lhsT=wt[:, kc, ct * 128:(ct + 1) * 128], rhs=XN[:, kc, t0:t0 + n],
                                                                     start=(kc == 0), stop=(kc == 7)), reads=[bXN, bwt], writes=[bpb])
                            if bi < 4:
                                S.add("act", lambda e: e.copy(out=xwb[:, 3 + 512 * bi:3 + 512 * (bi + 1)], in_=pb[:, :]), reads=[bpb], writes=[bxw, bpb])
                            else:
                                S.add("act", lambda e: e.copy(out=xwsb[:, :, 3:11], in_=pb[:, 0:32].rearrange("p (s t) -> p s t", s=4)),
                                      reads=[bpb], writes=[bxws, bpb])

                        def c1(ct=ct, bi=bi):
                            cps, bcps = PS[2 + (bi % 2)], PB[2 + (bi % 2)]
                            tn_, btn_ = tnh2[bi % 2], btnh2[bi % 2]
                            if bi < 4:
                                S.add("pe", lambda e: e.matmul(cps[:, :], lhsT=cbrb[0:1, :], rhs=onesrow[0:1, :], start=True, stop=False),
                                      reads=[bcbrb, bones], writes=[bcps])
                                for tap in range(4):
                                    S.add("pe", lambda e, tap=tap: e.matmul(cps[:, :], lhsT=dg[:, tap, :], rhs=xwb[:, 512 * bi + tap:512 * bi + tap + 512],
                                                                           start=False, stop=(tap == 3)), reads=[bdg, bxw], writes=[bcps])
                                S.add("act", lambda e: e.activation(out=tn_[:], in_=cps[:, :], func=AF.Tanh), reads=[bcps], writes=[btn_])
                                S.add("dve", lambda e: e.scalar_tensor_tensor(out=XT[:, ct, 512 * bi:512 * (bi + 1)], in0=tn_[:], scalar=1.0, in1=cps[:, :],
                                                                              op0=ALU.add, op1=ALU.mult), reads=[btn_, bcps], writes=[bXT[ct]])
                            else:
                                S.add("pe", lambda e: e.matmul(cps[:, 0:32], lhsT=cbrb[0:1, :], rhs=onesrow[0:1, 0:32], start=True, stop=False),
                                      reads=[bcbrb, bones], writes=[bcps])
                                for tap in range(4):
                                    S.add("pe", lambda e, tap=tap: e.matmul(cps[:, 0:32].rearrange("p (s t) -> p s t", s=4), lhsT=dg[:, tap, :],
                                                                           rhs=xwsb[:, :, tap:tap + 8], start=False, stop=(tap == 3)),
                                          reads=[bdg, bxws], writes=[bcps])
                                S.add("act", lambda e: e.activation(out=tn_[:, 0:32], in_=cps[:, 0:32], func=AF.Tanh), reads=[bcps], writes=[btn_])
                                S.add("dve", lambda e: e.scalar_tensor_tensor(out=XT[:, ct, LP:NT], in0=tn_[:, 0:32], scalar=1.0, in1=cps[:, 0:32],
                                                                              op0=ALU.add, op1=ALU.mult), reads=[btn_, bcps], writes=[bXT[ct]])
                        conv_items.append([c0, c1])
                return conv_items

            def g_mid(g):
                for c4 in range(5):
                    chunks = list(range(c4 * 4, min(c4 * 4 + 4, 17)))
                    ix = c4 % 2
                    pvx, bpx = PS[0 + ix][:].bitcast(BF16), PB[0 + ix]
                    pvb, bpb_ = PS[3 + ix][:].bitcast(BF16), PB[3 + ix]
                    pcb, bpc = (PS[2], PB[2]) if ix == 0 else (PS[5], PB[5])
                    for j, c in enumerate(chunks):
                        t0, rows = TT[c]
                        for k in range(2):
                            S.add("pe", lambda e, pvx=pvx, j=j, k=k, t0=t0, rows=rows: e.transpose(
                                out=pvx[0:rows, j * 256 + k * 128:j * 256 + (k + 1) * 128], in_=XT[:, k, t0:t0 + rows], identity=IDb),
                                reads=[bXT[k], bCb], writes=[bpx])
                    nchk = len(chunks)
                    rws = 128 if chunks[-1] < 16 else 32
                    S.add("act", lambda e, pvx=pvx, c4=c4, nchk=nchk, rws=rws: e.copy(
                        out=xtok[0:rws, c4 * 4:c4 * 4 + nchk, :], in_=pvx[0:rws, 0:nchk * 256].rearrange("p (c x) -> p c x", x=256)),
                        reads=[bpx], writes=[bxtk[c_] for c_ in chunks] + [bpx])
                    for j, c in enumerate(chunks):
                        t0, rows = TT[c]
                        S.add("pe", lambda e, pvb=pvb, j=j, t0=t0, rows=rows: e.transpose(
                            out=pvb[0:rows, j * 128:(j + 1) * 128], in_=XT[:, 2, t0:t0 + rows], identity=IDb),
                            reads=[bXT[2], bCb], writes=[bpb_])
                    S.add("act", lambda e, pvb=pvb, c4=c4, nchk=nchk, rws=rws: e.copy(
                        out=btok[0:rws, c4 * 4:c4 * 4 + nchk, :], in_=pvb[0:rws, 0:nchk * 128].rearrange("p (c x) -> p c x", x=128)),
                        reads=[bpb_], writes=[bbtok, bpb_])
                    for j, c in enumerate(chunks):
                        t0, rows = TT[c]
                        S.add("pe", lambda e, pcb=pcb, j=j, t0=t0, rows=rows: e.matmul(
                            pcb[0:rows, j * 128:j * 128 + rows], lhsT=XT[:, 2, t0:t0 + rows], rhs=XT[:, 3, t0:t0 + rows],
                            start=True, stop=True), reads=[bXT[2], bXT[3]], writes=[bpc])
                    if rws == 128:
                        S.add("dve", lambda e, pcb=pcb, c4=c4, nchk=nchk: e.tensor_tensor(
                            out=CBm[:, c4 * 4:c4 * 4 + nchk, :], in0=pcb[:, 0:nchk * 128].rearrange("p (c x) -> p c x", x=128),
                            in1=TRIpb.unsqueeze(1).to_broadcast([128, nchk, 128]), op=ALU.mult),
                            reads=[bpc, bCb], writes=[bCBm, bpc])
                    else:
                        S.add("dve", lambda e, pcb=pcb: e.tensor_tensor(out=CBm[0:32, 16, 0:32], in0=pcb[0:32, 0:32], in1=TRIsb[0:32, 0:32], op=ALU.mult),
                              reads=[bpc, bCb], writes=[bCBm, bpc])
                S.add("pool", lambda e: e.tensor_tensor(out=CTs[:], in0=XT[:, 3, LP:NT].unsqueeze(1).to_broadcast([128, 4, 32]),
                                                        in1=SEGXb.rearrange("p (s t) -> p s t", s=4), op=ALU.mult),
                      reads=[bXT[3], bCb], writes=[bCTs])
                S.add("dve", lambda e, g=g: e.tensor_tensor(out=dsg[:], in0=dtw[0:32, 16, 4 * g:4 * g + 4].unsqueeze(1).to_broadcast([32, 4, 4]),
                                                           in1=SEG.unsqueeze(2).to_broadcast([32, 4, 4]), op=ALU.mult), reads=[bT, bC], writes=[bdsg])
                S.add("pool", lambda e: e.tensor_copy(out=xsr[:], in_=xtok[0:32, 16, :]), reads=[bxtk[16]], writes=[bxsr])
                S.add("dve", lambda e: e.memset(gs[:], 0.0), writes=[bgs])
                S.add("dve", lambda e: e.memset(STt[:], 0.0), writes=[bST])
                S.add("dve", lambda e: e.memset(STb[:], 0.0), writes=[bSTb])

            def g_chunk_items(g):
                chunk_items = []
                for c in range(17):
                    def s0(c=c, g=g):
                        t0, rows = TT[c]
                        a_, ba_ = abc[c % 2], babc[c % 2]
                        d_, bd_ = Dt[c % 2], bDt[c % 2]
                        e_, be_ = Et[c % 2], bEt[c % 2]
                        w_, bw_ = Bw[c % 2], bBw[c % 2]
                        xdt, bxdt = xdt2[c % 2], bxdt2[c % 2]
                        xdb, bxdb = xdb2[c % 2], bxdb2[c % 2]
                        S.dma(lambda e: e.dma_start(out=a_[:, :, 0:rows], in_=acT_d[4 * g:4 * g + 4, t0:t0 + rows].partition_broadcast(128)),
                              reads=[bscr], writes=[ba_])
                        for j in (2, 3):
                            hh = 4 * g + j
                            S.add("dve", lambda e, j=j, hh=hh: e.tensor_scalar(
                                out=d_[0:rows, j - 2, 0:rows], in0=a_[0:rows, j, 0:rows], scalar1=acum[0:rows, c, hh:hh + 1], scalar2=0.0,
                                op0=ALU.subtract, op1=ALU.min), reads=[ba_, bT], writes=[bd_])
                        for j in (0, 1):
                            hh = 4 * g + j
                            S.add("act", lambda e, j=j, hh=hh: e.activation(out=e_[0:rows, j, 0:rows], in_=a_[0:rows, j, 0:rows], func=AF.Exp,
                                                                           bias=nacum[0:rows, c, hh:hh + 1]), reads=[ba_, bT], writes=[be_])
                        S.add("act", lambda e: e.activation(out=e_[0:rows, 2:4, 0:rows], in_=d_[0:rows, 0:2, 0:rows], func=AF.Exp), reads=[bd_], writes=[be_])
                        S.add("pool", lambda e: e.tensor_tensor(
                            out=xdt[0:rows, :].rearrange("p (h x) -> p h x", h=4), in0=xtok[0:rows, c, :].rearrange("p (h x) -> p h x", h=4),
                            in1=dtt[0:rows, c, 4 * g:4 * g + 4].unsqueeze(2).to_broadcast([rows, 4, 64]), op=ALU.mult), reads=[bxtk[c], bT], writes=[bxdt])
                        S.add("pool", lambda e: e.tensor_tensor(
                            out=xdb[0:rows, :].rearrange("p (h x) -> p h x", h=4), in0=xtok[0:rows, c, :].rearrange("p (h x) -> p h x", h=4),
                            in1=hp_bc[0:rows, 64 + 4 * g:68 + 4 * g].unsqueeze(2).to_broadcast([rows, 4, 64]), op=ALU.mult), reads=[bxtk[c], bhp], writes=[bxdb])
                        if c < 16:
                            S.add("pool", lambda e: e.tensor_tensor(
                                out=w_[:, :].rearrange("p (h x) -> p h x", h=4), in0=xtok[:, c, :].rearrange("p (h x) -> p h x", h=4),
                                in1=dtw[:, c, 4 * g:4 * g + 4].unsqueeze(2).to_broadcast([128, 4, 64]), op=ALU.mult), reads=[bxtk[c], bT], writes=[bw_])

                    def s1(c=c, g=g, wzt=wzt, bwzt=bwzt):
                        t0, rows = TT[c]
                        e_, be_ = Et[c % 2], bEt[c % 2]
                        m_, bm_ = Mt[c % 2], bMt[c % 2]
                        w_, bw_ = Bw[c % 2], bBw[c % 2]
                        xdt, bxdt = xdt2[c % 2], bxdt2[c % 2]
                        xdb, bxdb = xdb2[c % 2], bxdb2[c % 2]
                        yb, byb = PS[4 + (c % 2)], PB[4 + (c % 2)]
                        sbk, bsbk = PS[6], PB[6]
                        zb, bzb = PS[7], PB[7]
                        S.add("dve", lambda e: e.scalar_tensor_tensor(
                            out=m_[0:rows, :, 0:rows], in0=e_[0:rows, :, 0:rows], scalar=1.0, in1=CBm[0:rows, c, 0:rows].unsqueeze(1).to_broadcast([rows, 4, rows]),
                            op0=ALU.min, op1=ALU.mult), reads=[be_, bCBm], writes=[bm_])
                        for kc in range(8):
                            S.add("pe", lambda e, kc=kc: e.matmul(zb[0:rows, 0:256], lhsT=XN[:, kc, t0:t0 + rows], rhs=wzt[:, kc, :],
                                                                 start=(kc == 0), stop=(kc == 7)), reads=[bXN, bwzt], writes=[bzb])
                        if c < 16:
                            S.add("pe", lambda e: e.matmul(sbk[:, 0:256], lhsT=btok[:, c, :], rhs=w_[:, :], start=True, stop=True),
                                  reads=[bw_, bbtok], writes=[bsbk])
                            S.add("pe", lambda e: e.matmul(yb[:, 256:512], lhsT=XT[:, 3, t0:t0 + 128], rhs=STb[:],
                                                           start=True, stop=True, skip_group_check=True), reads=[bXT[3], bSTb], writes=[byb])
                        else:
                            for sq in range(4):
                                S.add("pe", lambda e, sq=sq: e.matmul(yb[0:32, 256:512], lhsT=CTs[:, sq, :], rhs=h0T[:, sq, :],
                                                                     start=(sq == 0), stop=(sq == 3), skip_group_check=True), reads=[bCTs, bh0T], writes=[byb])
                        S.add("pe", lambda e: e.matmul(yb[0:rows, 0:256], lhsT=IDb[0:rows, 0:rows], rhs=xdb[0:rows, :],
                                                       start=False, stop=False, skip_group_check=True), reads=[bxdb, bCb], writes=[byb])
                        for j in range(4):
                            S.add("pe", lambda e, j=j: e.matmul(yb[0:rows, 64 * j:64 * j + 64], lhsT=m_[0:rows, j, 0:rows], rhs=xdt[0:rows, 64 * j:64 * j + 64],
                                                               start=False, stop=True, skip_group_check=True), reads=[bm_, bxdt], writes=[byb])

                    def s2(c=c, g=g):
                        t0, rows = TT[c]
                        y_, by_ = yt[c % 2], byt[c % 2]
                        yb, byb = PS[4 + (c % 2)], PB[4 + (c % 2)]
                        sbk, bsbk = PS[6], PB[6]
                        zb, bzb = PS[7], PB[7]
                        if c < 16:
                            S.add("pool", lambda e: e.tensor_tensor(
                                out=STt[:].rearrange("p (h x) -> p h x", h=4), in0=STt[:].rearrange("p (h x) -> p h x", h=4),
                                in1=decay[:, c, 4 * g:4 * g + 4].unsqueeze(2).to_broadcast([128, 4, 64]), op=ALU.mult), reads=[bT, bSTb], writes=[bST])
                            S.add("dve", lambda e: e.tensor_tensor(out=STt[:], in0=STt[:], in1=sbk[:, 0:256], op=ALU.add), reads=[bsbk], writes=[bST, bsbk])
                            S.add("act", lambda e: e.copy(out=STb[:], in_=STt[:]), reads=[bST], writes=[bSTb])
                        S.add("dve", lambda e: e.tensor_tensor(
                            out=y_[0:rows, :].rearrange("p (h x) -> p h x", h=4), in0=yb[0:rows, 256:512].rearrange("p (h x) -> p h x", h=4),
                            in1=eacum[0:rows, c, 4 * g:4 * g + 4].unsqueeze(2).to_broadcast([rows, 4, 64]), op=ALU.mult), reads=[byb, bT], writes=[by_])
                        S.add("dve", lambda e: e.tensor_tensor(out=y_[0:rows, :], in0=y_[0:rows, :], in1=yb[0:rows, 0:256], op=ALU.add), reads=[byb], writes=[by_, byb])
                        S.add("act", lambda e: e.activation(out=tz[0:rows, :], in_=zb[0:rows, 0:256], func=AF.Tanh, scale=0.5), reads=[bzb], writes=[btz])
                        S.add("dve", lambda e: e.scalar_tensor_tensor(out=tz[0:rows, :], in0=tz[0:rows, :], scalar=1.0, in1=zb[0:rows, 0:256],
                                                                      op0=ALU.add, op1=ALU.mult), reads=[bzb], writes=[btz, bzb])
                        S.add("dve", lambda e: e.tensor_tensor(out=tz[0:rows, :], in0=tz[0:rows, :], in1=y_[0:rows, :], op=ALU.mult), reads=[by_], writes=[btz])
                        S.add("act", lambda e: e.activation(out=ynb[0:rows, :], in_=tz[0:rows, :], func=AF.Square, accum_out=gs[0:rows, c:c + 1]),
                              reads=[btz], writes=[bynb, bgs])
                        S.add("act", lambda e: e.copy(out=xtok[0:rows, c, :], in_=tz[0:rows, :]), reads=[btz], writes=[bxtk[c]])
                    chunk_items.append([s0, s1, s2])
                return chunk_items

            def g_post(g):
                S.add("act", lambda e: e.activation(out=gs[:, 17:34], in_=gs[:, 0:17], func=AF.Sqrt, scale=1.0 / 256, bias=4e-5), reads=[bgs], writes=[bgs])
                S.add("dve", lambda e: e.reciprocal(out=gs[:, 34:51], in_=gs[:, 17:34]), reads=[bgs], writes=[bgs])
                norm_items = []
                for c in range(17):
                    def n0(c=c):
                        t0, rows = TT[c]
                        yn_, byn_ = ynb2[c % 2], bynb2[c % 2]
                        S.add("dve", lambda e: e.scalar_tensor_tensor(out=yn_[0:rows, :], in0=xtok[0:rows, c, :], scalar=gs[0:rows, 34 + c:35 + c], in1=nrm[0:rows, :],
                                                                      op0=ALU.mult, op1=ALU.mult), reads=[bxtk[c], bgs, bnrm], writes=[byn_])

                    def n1(c=c, g=g):
                        t0, rows = TT[c]
                        yn_, byn_ = ynb2[c % 2], bynb2[c % 2]
                        nb = 3 if c % 2 == 0 else 2
                        pv = PS[nb][:].bitcast(BF16)
                        for k in range(2):
                            S.add("pe", lambda e, k=k: e.transpose(out=pv[:, k * 128:k * 128 + rows], in_=yn_[0:rows, k * 128:(k + 1) * 128],
                                                                  identity=IDb[0:rows, 0:rows]), reads=[byn_, bCb], writes=[PB[nb]])
                        S.add("act", lambda e: e.copy(out=ysT[:, 2 * g:2 * g + 2, t0:t0 + rows], in_=pv[:, 0:256].rearrange("p (k t) -> p k t", k=2)[:, :, 0:rows]),
                              reads=[PB[nb]], writes=[bys, PB[nb]])
                    norm_items.append([n0, n1])
                pipeline(norm_items)
                for k in range(2):
                    S.add("pe", lambda e, k=k: e.transpose(out=PS[2][:, k * 128:(k + 1) * 128], in_=STt[:, k * 128:(k + 1) * 128], identity=IDf),
                          reads=[bST, bC], writes=[PB[2]])
                S.add("act", lambda e: e.copy(out=stp[:], in_=PS[2][:, 0:256].rearrange("p (k n) -> p k n", k=2)), reads=[PB[2]], writes=[bstp, PB[2]])
                S.dma(lambda e, g=g: e.dma_start(out=ssm_p[4 * g:4 * g + 4, :, :].rearrange("(a b) p n -> (b p) a n", b=2), in_=stp[:]), reads=[bstp])
                S.add("dve", lambda e, g=g: e.tensor_scalar(out=dtaE[:].rearrange("p (h x) -> p h x", h=4),
                                                           in0=totb[0:32, 16, 4 * g:4 * g + 4].unsqueeze(2).to_broadcast([32, 4, 64]),
                                                           scalar1=0.125, scalar2=None, op0=ALU.mult), reads=[bT], writes=[bdtaE])
                for k in range(2):
                    S.add("pe", lambda e, k=k: e.matmul(PS[2][:, 256 + 4 * k:260 + 4 * k], lhsT=dtaE[:, k * 128:(k + 1) * 128], rhs=SEG,
                                                       start=True, stop=True), reads=[bdtaE, bC], writes=[PB[2]])
                S.add("act", lambda e: e.activation(out=dcol[:], in_=PS[2][:, 256:264].rearrange("p (k s) -> p k s", k=2), func=AF.Exp),
                      reads=[PB[2]], writes=[bdcol, PB[2]])
                for s in range(4):
                    S.dma(lambda e, g=g, s=s: e.dma_start(out=h0n[s % 2][:], in_=st_nat[s, 4 * g:4 * g + 4, :, :].rearrange("(a b) p n -> (b p) a n", b=2)),
                          writes=[bh0n[s % 2]])
                    S.add("dve", lambda e, s=s: e.tensor_tensor(out=wxm[:, :].rearrange("p (h x) -> p h x", h=4),
                                                               in0=xsr[:, :].rearrange("p (h x) -> p h x", h=4),
                                                               in1=dsg[:, s, :].unsqueeze(2).to_broadcast([32, 4, 64]), op=ALU.mult),
                          reads=[bxsr, bdsg], writes=[bwxm])
                    for k in range(2):
                        S.add("pe", lambda e, s=s, k=k: e.matmul(PS[6 + (s % 2)][:, k * 128:(k + 1) * 128],
                                                                lhsT=wxm[:, k * 128:(k + 1) * 128], rhs=btok[0:32, 16, :], start=True, stop=True),
                              reads=[bwxm, bbtok], writes=[PB[6 + (s % 2)]])
                    for k in range(2):
                        S.add("dve", lambda e, s=s, k=k: e.scalar_tensor_tensor(out=nst[s % 2][:, k, :], in0=h0n[s % 2][:, k, :], scalar=dcol[:, k, s:s + 1],
                                                                              in1=PS[6 + (s % 2)][:, k * 128:(k + 1) * 128], op0=ALU.mult, op1=ALU.add),
                              reads=[bh0n[s % 2], bdcol, PB[6 + (s % 2)]], writes=[bnst[s % 2], PB[6 + (s % 2)]])
                    S.dma(lambda e, g=g, s=s: e.dma_start(out=ssm_s[s, 4 * g:4 * g + 4, :, :].rearrange("(a b) p n -> (b p) a n", b=2), in_=nst[s % 2][:]), reads=[bnst[s % 2]])

            def interleave(a, b):
                out = []
                nb = len(b)
                for i, x in enumerate(a):
                    out.append(x)
                    if i < nb:
                        out.append(b[i])
                out.extend(b[len(a):])
                return out

            g_loadw(0)
            for g in range(8):
                g_pre(g)
                pipeline(g_conv_items(g, range(4)))
                g_mid(g)
                if g + 1 < 8:
                    g_loadw(g + 1)
                pipeline(g_chunk_items(g))
                g_post(g)
        S.barrier(scr)
        tabs.close()
        nwb[0] = 1

        build_attention_and_tail(nc, S, top, SB, PS, PB, XN, bXN, ysT, bys, load_w, win_seg, w_in_v, C, Cb, bC, bCb, scr, hp_bc, bhp,
                                 dict(x_all=x_all, w_a=w_a, w_b=w_b, w_o=w_o, final_g=final_g, tbraw=tbraw, fbraw=fbraw, fnraw=fnraw,
                                      cache_k=cache_k, cache_v=cache_v, y_out=y_out, k_out=k_out, v_out=v_out), TT, TB)
        S.emit(top)
    return nc


def build_attention_and_tail(nc, S, top, SB, PS, PB, XN, bXN, ysT, bys, load_w, win_seg, w_in_v, C, Cb, bC, bCb, scr, hp_bc, bhp, T, TT, TB):
    IDf = C[:, 0:128]; IDb = Cb[:, 0:128]; ONESb = Cb[:, 384:512]
    x_all, w_a, w_b, w_o, final_g = T["x_all"], T["w_a"], T["w_b"], T["w_o"], T["final_g"]
    k_out, v_out, y_out = T["k_out"], T["v_out"], T["y_out"]
    yaT = SB(top, "yaT", [128, 6, NT], BF16); bya = Buf()
    Qs = SB(top, "Qs", [32, 768], BF16); bQs = Buf()
    Ks = SB(top, "Ks", [32, 768], F32); bKs = Buf()
    Vs = SB(top, "Vs", [32, 768], BF16); bVs = Buf()

    with ExitStack() as p3:
        TM = SB(p3, "TM", [128, 2, 640], BF16); bTM = Buf()
        tstage = SB(p3, "tstage", [128, 2, 640], F32); bts = Buf()
        QT = SB(p3, "QT", [128, NT], BF16); bQT = Buf()
        KT = SB(p3, "KT", [128, NT], BF16); bKT = Buf()
        Kf = SB(p3, "Kf", [128, NT], F32); bKf = Buf()
        Vf = Kf; bVf = bKf
        VTb = SB(p3, "VTb", [128, NT], BF16); bVTb = Buf()
        SG = SB(p3, "SG", [128, NT], BF16); bSG = Buf()
        tg = SB(p3, "tg", [128, 512], F32); btg = Buf()
        Vx = [SB(p3, "Vx%d" % i, [128, 16, 2, 128], BF16) for i in range(3)]; bVx = [Buf(), Buf(), Buf()]
        ost = [SB(p3, "ost%d" % i, [128, 4, 128], F32) for i in range(1)] * 2; bost = [Buf()] * 2
        Pt = [SB(p3, "Pt%d" % i, [128, 512], BF16) for i in range(3)]; bPt = [Buf(), Buf(), Buf()]
        rec = SB(p3, "rec", [128, 512], F32); brec = Buf()
        tmpo = rec; btmpo = Buf()
        for i in range(3):
            S.add("pool", lambda e, i=i: e.memset(Vx[i][:], 1.0), writes=[bVx[i]])

        for hp in range(6):
            S.dma(lambda e, hp=hp: e.dma_start(out=tstage[:], in_=T["tbraw"][:, 2 * hp:2 * hp + 2, :]), writes=[bts])
            S.add("act", lambda e: e.copy(out=TM[:], in_=tstage[:]), reads=[bts], writes=[bTM])
            wt, bwt = load_w([win_seg(128 * hp, 128, 0), win_seg(768 + 128 * hp, 128, 128),
                              win_seg(1536 + 128 * hp, 128, 256), win_seg(2304 + 128 * hp, 128, 384)])
            def emit_kv_out(which, src, bsrc, dst, hp=hp):
                for c4 in range(5):
                    tiles = list(range(c4 * 4, min(c4 * 4 + 4, 17)))
                    o_, bo_ = ost[(which * 5 + c4) % 2], bost[(which * 5 + c4) % 2]
                    for j, ti in enumerate(tiles):
                        t0, rows = TT[ti]
                        S.add("pe", lambda e, j=j, t0=t0, rows=rows, src=src: e.transpose(out=PS[2][0:rows, j * 128:(j + 1) * 128], in_=src[:, t0:t0 + rows], identity=IDf),
                              reads=[bsrc, bC], writes=[PB[2]])
                    nt_ = len(tiles)
                    rws = 128 if tiles[-1] < 16 else 32
                    S.add("act", lambda e, o_=o_, nt_=nt_, rws=rws: e.copy(out=o_[0:rws, 0:nt_, :], in_=PS[2][0:rws, 0:nt_ * 128].rearrange("p (c x) -> p c x", x=128)),
                          reads=[PB[2]], writes=[bo_])
                    if which == 1 and rws == 128:
                        S.add("dve", lambda e, c4=c4: e.tensor_copy(out=Vx[0][:, c4 * 4:c4 * 4 + 4, 0, 0:64], in_=PS[2][:, :].rearrange("p (c x) -> p c x", x=128)[:, :, 0:64]),
                              reads=[PB[2]], writes=[bVx[0]])
                        S.add("dve", lambda e, c4=c4: e.tensor_copy(out=Vx[0][:, c4 * 4:c4 * 4 + 4, 1, 64:128], in_=PS[2][:, :].rearrange("p (c x) -> p c x", x=128)[:, :, 64:128]),
                              reads=[PB[2]], writes=[bVx[0], PB[2]])
                    if rws == 32:
                        tgt, btgt = (Ks, bKs) if which == 0 else (Vs, bVs)
                        S.add("dve", lambda e, tgt=tgt, hp=hp: e.tensor_copy(out=tgt[:, 128 * hp:128 * (hp + 1)], in_=PS[2][0:32, 0:128]), reads=[PB[2]], writes=[btgt, PB[2]])
                    S.add("dve", lambda e: e.memset(scr[0:1, 15:16], 0.0), reads=[PB[2]], writes=[PB[2]])
                    if rws == 128:
                        S.dma(lambda e, o_=o_, c4=c4, hp=hp, dst=dst: e.dma_start(
                            out=dst[c4 * 512:(c4 + 1) * 512, 128 * hp:128 * (hp + 1)].rearrange("(c p) x -> p c x", p=128), in_=o_[:, :, :]), reads=[bo_])
                    else:
                        S.dma(lambda e, o_=o_, hp=hp, dst=dst: e.dma_start(out=dst[LP:NT, 128 * hp:128 * (hp + 1)], in_=o_[0:32, 0, :]), reads=[bo_])
            for seg in range(4):
                for bi, (t0, n) in enumerate(TB):
                    pb, bpb = PS[bi % 2], PB[bi % 2]
                    for kc in range(8):
                        S.add("pe", lambda e, pb=pb, kc=kc, t0=t0, n=n, seg=seg, wt=wt: e.matmul(
                            pb[:, 0:n], lhsT=wt[:, kc, seg * 128:(seg + 1) * 128], rhs=XN[:, kc, t0:t0 + n],
                            start=(kc == 0), stop=(kc == 7)), reads=[bXN, bwt], writes=[bpb])
                    if seg == 0:
                        S.add("act", lambda e, pb=pb, t0=t0, n=n: e.activation(out=QT[:, t0:t0 + n], in_=pb[:, 0:n], func=AF.Identity, scale=0.125),
                              reads=[bpb], writes=[bQT, bpb])
                    elif seg == 1:
                        S.add("act", lambda e, pb=pb, t0=t0, n=n: e.copy(out=KT[:, t0:t0 + n], in_=pb[:, 0:n]), reads=[bpb], writes=[bKT])
                        S.add("dve", lambda e, pb=pb, t0=t0, n=n: e.tensor_copy(out=Kf[:, t0:t0 + n], in_=pb[:, 0:n]), reads=[bpb], writes=[bKf, bpb])
                    elif seg == 2:
                        S.add("act", lambda e, pb=pb, t0=t0, n=n: e.copy(out=VTb[:, t0:t0 + n], in_=pb[:, 0:n]), reads=[bpb], writes=[bVTb])
                        S.add("dve", lambda e, pb=pb, t0=t0, n=n: e.tensor_copy(out=Vf[:, t0:t0 + n], in_=pb[:, 0:n]), reads=[bpb], writes=[bVf, bpb])
                    else:
                        S.add("act", lambda e, pb=pb, n=n: e.activation(out=tg[:, 0:n], in_=pb[:, 0:n], func=AF.Tanh, scale=0.5), reads=[bpb], writes=[btg])
                        S.add("dve", lambda e, pb=pb, t0=t0, n=n: e.scalar_tensor_tensor(out=SG[:, t0:t0 + n], in0=tg[:, 0:n], scalar=1.0, in1=pb[:, 0:n],
                                                                                       op0=ALU.add, op1=ALU.mult), reads=[btg, bpb], writes=[bSG, bpb])
                if seg == 1:
                    emit_kv_out(0, Kf, bKf, k_out)
                elif seg == 2:
                    emit_kv_out(1, Vf, bVf, v_out)
            pv = PS[3][:].bitcast(BF16)
            S.add("pe", lambda e, pv=pv: e.transpose(out=pv[0:32, 0:128], in_=QT[:, LP:NT], identity=IDb), reads=[bQT, bCb], writes=[PB[3]])
            S.add("act", lambda e, pv=pv, hp=hp: e.copy(out=Qs[:, 128 * hp:128 * (hp + 1)], in_=pv[0:32, 0:128]), reads=[PB[3]], writes=[bQs, PB[3]])
            for pi, (mod, vx, bvx) in enumerate(((4, Vx[1], bVx[1]), (16, Vx[2], bVx[2]))):
                for half in range(2):
                    for j in range(8):
                        tl = half * 8 + j
                        if mod == 4:
                            r, cbk = tl // 4, tl % 4
                            src = VTb[:, r + 512 * cbk:r + 512 * cbk + 512:4]
                        else:
                            src = VTb[:, tl:LP:16]
                        S.add("pe", lambda e, pv=pv, j=j, src=src: e.transpose(out=pv[:, j * 128:(j + 1) * 128], in_=src, identity=IDb),
                              reads=[bVTb, bCb], writes=[PB[3]])
                    S.add("act", lambda e, pv=pv, vx=vx, half=half: e.copy(out=vx[:, half * 8:half * 8 + 8, 0, 0:64],
                                                                        in_=pv[:, :].rearrange("p (c x) -> p c x", x=128)[:, :, 0:64]), reads=[PB[3]], writes=[bvx])
                    S.add("dve", lambda e, pv=pv, vx=vx, half=half: e.tensor_copy(out=vx[:, half * 8:half * 8 + 8, 1, 64:128],
                                                                               in_=pv[:, :].rearrange("p (c x) -> p c x", x=128)[:, :, 64:128]),
                          reads=[PB[3]], writes=[bvx, PB[3]])
            bank_items = []
            bctr = [0]
            for hd in range(2):
                hs = slice(64 * hd, 64 * hd + 64)
                for R in range(4):
                    accb, baccb = PS[6 + ((hd * 4 + R) % 2)], PB[6 + ((hd * 4 + R) % 2)]
                    banks = []
                    for half in range(2):
                        s_mm, pv_mm = [], []
                        qb0 = 4 * R + 2 * half
                        lc = (qb0 - 4 * R) * 128
                        if qb0 > 0:
                            s_mm.append((0, 128, KT[hs, (qb0 - 1) * 128:qb0 * 128], QT[hs, qb0 * 128:(qb0 + 1) * 128]))
                            pv_mm.append((accb[:, lc:lc + 128], 0, 128, Vx[0][:, qb0 - 1, hd, :], bVx[0]))
                        s_mm.append((128, 256, KT[hs, qb0 * 128:(qb0 + 1) * 128], QT[hs, qb0 * 128:(qb0 + 2) * 128]))
                        pv_mm.append((accb[:, lc:lc + 256], 128, 256, Vx[0][:, qb0, hd, :], bVx[0]))
                        s_mm.append((384, 128, KT[hs, (qb0 + 1) * 128:(qb0 + 2) * 128], QT[hs, (qb0 + 1) * 128:(qb0 + 2) * 128]))
                        pv_mm.append((accb[:, lc + 128:lc + 256], 384, 128, Vx[0][:, qb0 + 1, hd, :], bVx[0]))
                        banks.append((s_mm, TM[:, hd, 0:256].unsqueeze(1).to_broadcast([128, 2, 256]), pv_mm, 256))
                    for half in range(2):
                        s_mm, pv_mm = [], []
                        for rl in range(2):
                            r = 2 * half + rl
                            qcols = QT[hs, r + 512 * R:r + 512 * R + 512:4]
                            oc = accb[:, r:512:4]
                            if R > 0:
                                s_mm.append((rl * 256, 128, KT[hs, r + 512 * (R - 1):r + 512 * R:4], qcols))
                                pv_mm.append((oc, rl * 256, 128, Vx[1][:, r * 4 + R - 1, hd, :], bVx[1]))
                            s_mm.append((rl * 256 + 128, 128, KT[hs, r + 512 * R:r + 512 * R + 512:4], qcols))
                            pv_mm.append((oc, rl * 256 + 128, 128, Vx[1][:, r * 4 + R, hd, :], bVx[1]))
                        banks.append((s_mm, TM[:, hd, 256:512].unsqueeze(1).to_broadcast([128, 2, 256]), pv_mm, 256))
                    s_mm, pv_mm = [], []
                    for r in range(16):
                        s_mm.append((r * 32, 32, KT[hs, r:LP:16], QT[hs, r + 512 * R:r + 512 * R + 512:16]))
                        pv_mm.append((accb[:, r:512:16], r * 32, 32, Vx[2][:, r, hd, :], bVx[2]))
                    banks.append((s_mm, TM[:, hd, 512 + 32 * R:544 + 32 * R].unsqueeze(1).to_broadcast([128, 16, 32]), pv_mm, 32))

                    for bk, (s_mm, mask_ap, pv_mm, bw_) in enumerate(banks):
                        i = bctr[0] % 3
                        bctr[0] += 1
                        meng = "dve" if bk == 4 else "pool"

                        def sA(i=i, s_mm=s_mm, mask_ap=mask_ap, bw_=bw_, meng=meng, bk=bk, hd=hd):
                            sb_, bsb_ = PS[3 + i], PB[3 + i]
                            p_, bp_ = Pt[i], bPt[i]
                            if bk < 4:
                                c0g = 0 if bk < 2 else 256
                                for half in range(2):
                                    S.add("pe", lambda e, half=half, c0g=c0g: e.matmul(sb_[:, 256 * half:256 * (half + 1)], lhsT=IDb, rhs=TM[:, hd, c0g:c0g + 256],
                                                                                    start=(half == 0), stop=False, skip_group_check=True),
                                          reads=[bTM, bCb], writes=[bsb_])
                                for (c0, n, lhsT, rhs) in s_mm:
                                    S.add("pe", lambda e, c0=c0, n=n, lhsT=lhsT, rhs=rhs: e.matmul(sb_[:, c0:c0 + n], lhsT=lhsT, rhs=rhs, start=False, stop=True,
                                                                                                skip_group_check=True), reads=[bKT, bQT], writes=[bsb_])
                                S.add("act", lambda e: e.activation(out=p_[:], in_=sb_[:], func=AF.Exp), reads=[bsb_], writes=[bp_, bsb_])
                            else:
                                S.add("pe", lambda e: e.matmul(sb_[:].rearrange("p (a b) -> p a b", b=bw_), lhsT=IDb, rhs=mask_ap,
                                                               start=True, stop=False, skip_group_check=True), reads=[bTM, bCb], writes=[bsb_])
                                for (c0, n, lhsT, rhs) in s_mm:
                                    S.add("pe", lambda e, c0=c0, n=n, lhsT=lhsT, rhs=rhs: e.matmul(sb_[:, c0:c0 + n], lhsT=lhsT, rhs=rhs, start=False, stop=True,
                                                                                                skip_group_check=True), reads=[bKT, bQT], writes=[bsb_])
                                S.add("act", lambda e: e.activation(out=p_[:], in_=sb_[:], func=AF.Exp), reads=[bsb_], writes=[bp_, bsb_])

                        def sB(i=i, pv_mm=pv_mm, bk=bk, accb=accb, baccb=baccb, hd=hd, R=R, hp=hp, last=(bk == len(banks) - 1)):
                            p_, bp_ = Pt[i], bPt[i]
                            for k, (oc, c0, n, lhsT, bl) in enumerate(pv_mm):
                                st = (bk == 0 and k == 0)
                                S.add("pe", lambda e, oc=oc, c0=c0, n=n, lhsT=lhsT, st=st: e.matmul(
                                    oc, lhsT=lhsT, rhs=p_[:, c0:c0 + n], start=st, stop=True, skip_group_check=True),
                                    reads=[bp_, bl], writes=[baccb])
                            if last:
                                ns = slice(64 * hd, 64 * hd + 64)
                                ds_ = slice(64 * (1 - hd), 64 * (1 - hd) + 64)
                                S.add("dve", lambda e: e.reciprocal(out=rec[ds_, :], in_=accb[ds_, :]), reads=[baccb], writes=[brec])
                                S.add("dve", lambda e: e.scalar_tensor_tensor(out=tmpo[ns, :], in0=accb[ns, :], scalar=0.5, in1=rec[ds_, :],
                                                                              op0=ALU.mult, op1=ALU.mult), reads=[baccb, brec], writes=[btmpo, baccb])
                                S.add("dve", lambda e: e.tensor_tensor(out=yaT[ns, hp, 512 * R:512 * (R + 1)], in0=tmpo[ns, :], in1=SG[ns, 512 * R:512 * (R + 1)], op=ALU.mult),
                                      reads=[btmpo, bSG], writes=[bya])
                        bank_items.append([sA, None, sB])
            pipeline(bank_items)
            S.add("pool", lambda e, hp=hp: e.tensor_copy(out=yaT[:, hp, LP:NT], in_=SG[:, LP:NT]), reads=[bSG], writes=[bya])
    S.barrier(scr)

    with ExitStack() as p4:
        FB = SB(p4, "FB", [128, 8, 36], F32); bFB = Buf()
        FN = SB(p4, "FN", [32, 32, 36], F32); bFN = Buf()
        S.dma(lambda e: e.dma_start(out=FB[:], in_=T["fbraw"].rearrange("p i g h -> p i (g h)")), writes=[bFB])
        S.dma(lambda e: e.dma_start(out=FN[:], in_=T["fnraw"].rearrange("p q g h -> p q (g h)")), writes=[bFN])
        S.add("act", lambda e: e.activation(out=FB[:], in_=FB[:], func=AF.Exp), reads=[bFB], writes=[bFB])
        S.add("act", lambda e: e.activation(out=FN[:], in_=FN[:], func=AF.Exp), reads=[bFN], writes=[bFN])
        SEL = SB(p4, "SEL", [32, 32, 128], BF16); bSEL = Buf()
        S.add("pool", lambda e: e.tensor_copy(out=SEL[:], in_=Cb[0:32, 0:32].unsqueeze(2).to_broadcast([32, 32, 128])), reads=[bCb], writes=[bSEL])
        K1 = [SB(p4, "K1_%d" % i, [128, 768], F32) for i in range(2)]; bK1 = [Buf(), Buf()]
        V1 = [SB(p4, "V1_%d" % i, [128, 768], BF16) for i in range(2)]; bV1 = [Buf(), Buf()]
        K2 = [SB(p4, "K2_%d" % i, [128, 768], F32) for i in range(2)]; bK2 = [Buf(), Buf()]
        V2 = [SB(p4, "V2_%d" % i, [128, 768], BF16) for i in range(2)]; bV2 = [Buf(), Buf()]
        K3 = [SB(p4, "K3_%d" % i, [128, 768], F32) for i in range(2)]; bK3 = [Buf(), Buf()]
        V3 = [SB(p4, "V3_%d" % i, [128, 768], BF16) for i in range(2)]; bV3 = [Buf(), Buf()]
        prod = SB(p4, "prod", [128, 3, 768], F32); bprod = Buf()
        Sc = [SB(p4, "Sc%d" % i, [128, 48], F32) for i in range(2)]; bSc = [Buf(), Buf()]
        Pc = [SB(p4, "Pc%d" % i, [128, 36], BF16) for i in range(2)]; bPc = [Buf(), Buf()]
        Pn = SB(p4, "Pn", [32, 36], F32); bPn = Buf()
        Pnb = [SB(p4, "Pnb%d" % i, [32, 12], BF16) for i in range(2)]; bPnb = [Buf(), Buf()]
        osb = SB(p4, "osb", [128, 12], F32); bosb = Buf()
        rcs = SB(p4, "rcs", [128, 12], F32); brcs = Buf()
        it = 0
        n3 = 0
        n2 = 0
        for s in range(4):
            k1, bk1, v1, bv1 = K1[s % 2], bK1[s % 2], V1[s % 2], bV1[s % 2]
            S.dma(lambda e, k1=k1, s=s: e.dma_start(out=k1[:], in_=T["cache_k"][s, 1920:2048, :]), writes=[bk1])
            S.dma(lambda e, v1=v1, s=s: e.dma_start(out=v1[:], in_=T["cache_v"][s, 1920:2048, :]), writes=[bv1], eng="pool")
            for r in range(4):
                k2, bk2, v2, bv2 = K2[n2 % 2], bK2[n2 % 2], V2[n2 % 2], bV2[n2 % 2]
                n2 += 1
                S.dma(lambda e, k2=k2, s=s, r=r: e.dma_start(out=k2[:], in_=T["cache_k"][s, 1536 + r:2048:4, :]), writes=[bk2])
                S.dma(lambda e, v2=v2, s=s, r=r: e.dma_start(out=v2[:], in_=T["cache_v"][s, 1536 + r:2048:4, :]), writes=[bv2], eng="pool")
                for i in (r, r + 4):
                    k3, bk3, v3, bv3 = K3[n3 % 2], bK3[n3 % 2], V3[n3 % 2], bV3[n3 % 2]
                    n3 += 1
                    S.dma(lambda e, k3=k3, s=s, i=i: e.dma_start(out=k3[:], in_=T["cache_k"][s, i:2048:16, :]), writes=[bk3])
                    S.dma(lambda e, v3=v3, s=s, i=i: e.dma_start(out=v3[:], in_=T["cache_v"][s, i:2048:16, :]), writes=[bv3], eng="pool")
                    col = s * 8 + i
                    sc_, bsc_ = Sc[it % 2], bSc[it % 2]
                    pc_, bpc_ = Pc[it % 2], bPc[it % 2]
                    pnb_, bpnb_ = Pnb[it % 2], bPnb[it % 2]
                    ob, bob = PS[2 + 2 * (it % 2)], PB[2 + 2 * (it % 2)]
                    db, bdb = PS[3 + 2 * (it % 2)], PB[3 + 2 * (it % 2)]
                    it += 1
                    ktiles = ((0, k1, bk1), (1, k2, bk2), (2, k3, bk3))
                    vtiles = ((0, v1, bv1), (1, v2, bv2), (2, v3, bv3))
                    S.add("pe", lambda e, col=col: e.matmul(PS[0][:, 0:512], lhsT=SEL[:, col, :], rhs=Qs[:, 0:512], start=True, stop=True), reads=[bSEL, bQs], writes=[PB[0]])
                    S.add("pe", lambda e, col=col: e.matmul(PS[1][:, 0:256], lhsT=SEL[:, col, :], rhs=Qs[:, 512:768], start=True, stop=True), reads=[bSEL, bQs], writes=[PB[1]])
                    for (gi, kt_, bkt_) in ktiles:
                        S.add("dve", lambda e, gi=gi, kt_=kt_: e.tensor_tensor(out=prod[:, gi, 0:512], in0=kt_[:, 0:512], in1=PS[0][:, 0:512], op=ALU.mult),
                              reads=[bkt_, PB[0]], writes=[bprod])
                        S.add("dve", lambda e, gi=gi, kt_=kt_: e.tensor_tensor(out=prod[:, gi, 512:768], in0=kt_[:, 512:768], in1=PS[1][:, 0:256], op=ALU.mult),
                              reads=[bkt_, PB[1]], writes=[bprod])
                    S.add("dve", lambda e, sc_=sc_: e.tensor_reduce(out=sc_[:, 0:36], in_=prod[:].rearrange("p g (h x) -> p (g h) x", x=64), axis=mybir.AxisListType.X, op=ALU.add),
                          reads=[bprod], writes=[bsc_])
                    S.add("dve", lambda e: e.tensor_tensor(out=prod[0:32, 0, 0:512], in0=Ks[:, 0:512], in1=PS[0][0:32, 0:512], op=ALU.mult), reads=[bKs, PB[0]], writes=[bprod, PB[0]])
                    S.add("dve", lambda e: e.tensor_tensor(out=prod[0:32, 0, 512:768], in0=Ks[:, 512:768], in1=PS[1][0:32, 0:256], op=ALU.mult), reads=[bKs, PB[1]], writes=[bprod, PB[1]])
                    S.add("dve", lambda e, sc_=sc_: e.tensor_reduce(out=sc_[0:32, 36:48], in_=prod[0:32, 0, :].rearrange("p (h x) -> p h x", x=64), axis=mybir.AxisListType.X, op=ALU.add),
                          reads=[bprod], writes=[bsc_])
                    S.add("act", lambda e, sc_=sc_: e.activation(out=sc_[:], in_=sc_[:], func=AF.Exp), reads=[bsc_], writes=[bsc_])
                    S.add("pool", lambda e, sc_=sc_, pc_=pc_, i=i: e.tensor_tensor(out=pc_[:], in0=sc_[:, 0:36], in1=FB[:, i, :], op=ALU.mult), reads=[bsc_, bFB], writes=[bpc_])
                    S.add("pool", lambda e, sc_=sc_, col=col: e.tensor_tensor(out=Pn[:].rearrange("p (g h) -> p g h", g=3), in0=sc_[0:32, 36:48].unsqueeze(1).to_broadcast([32, 3, 12]),
                                                                            in1=FN[:, col, :].rearrange("p (g h) -> p g h", g=3), op=ALU.mult), reads=[bsc_, bFN], writes=[bPn])
                    S.add("pool", lambda e: e.tensor_tensor(out=Pn[:, 0:12], in0=Pn[:, 0:12], in1=Pn[:, 12:24], op=ALU.add), reads=[], writes=[bPn])
                    S.add("pool", lambda e, pnb_=pnb_: e.tensor_tensor(out=pnb_[:], in0=Pn[:, 0:12], in1=Pn[:, 24:36], op=ALU.add), reads=[bPn], writes=[bpnb_])
                    for hp in range(6):
                        for (gi, vt_, bvt_) in vtiles:
                            S.add("pe", lambda e, ob=ob, vt_=vt_, pc_=pc_, hp=hp, gi=gi: e.matmul(ob[:, 2 * hp:2 * hp + 2], lhsT=vt_[:, 128 * hp:128 * (hp + 1)],
                                                                                          rhs=pc_[:, gi * 12 + 2 * hp:gi * 12 + 2 * hp + 2], start=(gi == 0), stop=False),
                                  reads=[bvt_, bpc_], writes=[bob])
                        S.add("pe", lambda e, ob=ob, pnb_=pnb_, hp=hp: e.matmul(ob[:, 2 * hp:2 * hp + 2], lhsT=Vs[:, 128 * hp:128 * (hp + 1)], rhs=pnb_[:, 2 * hp:2 * hp + 2],
                                                                             start=False, stop=True), reads=[bVs, bpnb_], writes=[bob])
                    for gi in range(3):
                        S.add("pe", lambda e, db=db, pc_=pc_, gi=gi: e.matmul(db[:, 0:12], lhsT=ONESb, rhs=pc_[:, gi * 12:gi * 12 + 12], start=(gi == 0), stop=False),
                              reads=[bCb, bpc_], writes=[bdb])
                    S.add("pe", lambda e, db=db, pnb_=pnb_: e.matmul(db[:, 0:12], lhsT=ONESb[0:32, :], rhs=pnb_[:], start=False, stop=True), reads=[bCb, bpnb_], writes=[bdb])
                    S.add("dve", lambda e, db=db: e.reciprocal(out=rcs[:], in_=db[:, 0:12]), reads=[bdb], writes=[brcs, bdb])
                    S.add("dve", lambda e, ob=ob: e.scalar_tensor_tensor(out=osb[:], in0=ob[:, 0:12], scalar=0.5, in1=rcs[:], op0=ALU.mult, op1=ALU.mult),
                          reads=[bob, brcs], writes=[bosb, bob])
                    cc = LP + col
                    S.add("dve", lambda e, cc=cc: e.tensor_tensor(out=yaT[0:64, :, cc:cc + 1], in0=yaT[0:64, :, cc:cc + 1], in1=osb[0:64, 0:12:2].unsqueeze(2), op=ALU.mult),
                          reads=[bosb], writes=[bya])
                    S.add("dve", lambda e, cc=cc: e.tensor_tensor(out=yaT[64:128, :, cc:cc + 1], in0=yaT[64:128, :, cc:cc + 1], in1=osb[64:128, 1:12:2].unsqueeze(2), op=ALU.mult),
                          reads=[bosb], writes=[bya])
    S.barrier(scr)

    with ExitStack() as p5:
        mT = SB(p5, "mT", [128, 8, NT], BF16); bmT = Buf()
        p5a = ExitStack()
        wd = [SB(p5a, "wd%d" % i, [128, 38, 128], BF16) for i in range(2)]; bwd = [Buf(), Buf()]
        sa = SB(p5a, "sa", [128, 512], F32); bsa = Buf()
        sbb = SB(p5a, "sbb", [128, 512], F32); bsbb = Buf()
        m1 = SB(p5a, "m1", [128, 512], F32); bm1 = Buf()
        m2 = SB(p5a, "m2", [128, 512], F32); bm2 = Buf()
        w_a_v = w_a.rearrange("(kc p) n -> p kc n", p=128)
        w_b_v = w_b.rearrange("(kc p) n -> p kc n", p=128)
        w_o_v = w_o.rearrange("(kc p) n -> p kc n", p=128)
        for dti in range(8):
            w_, bw_ = wd[dti % 2], bwd[dti % 2]
            cs = slice(128 * dti, 128 * (dti + 1))
            S.dma(lambda e, w_=w_, cs=cs: e.dma_start(out=w_[:, 0:6, :], in_=w_a_v[:, :, cs]), writes=[bw_], eng="pool")
            S.dma(lambda e, w_=w_, cs=cs: e.dma_start(out=w_[:, 6:22, :], in_=w_b_v[:, :, cs]), writes=[bw_], eng="pool")
            S.dma(lambda e, w_=w_, dti=dti: e.dma_start(out=w_[:, 22:30, :], in_=w_in_v[:, :, 9248 + 128 * dti:9248 + 128 * (dti + 1)]), writes=[bw_], eng="pool")
            S.dma(lambda e, w_=w_, dti=dti: e.dma_start(out=w_[:, 30:38, :], in_=w_in_v[:, :, 10272 + 128 * dti:10272 + 128 * (dti + 1)]), writes=[bw_], eng="pool")
            for bi, (t0, n) in enumerate(TB):
                o4 = 4 * (bi % 2)
                pA, pBk, pga, pgb = PS[o4], PS[o4 + 1], PS[o4 + 2], PS[o4 + 3]
                bA, bBk, bga, bgb = PB[o4], PB[o4 + 1], PB[o4 + 2], PB[o4 + 3]
                for kc in range(6):
                    S.add("pe", lambda e, pA=pA, w_=w_, kc=kc, t0=t0, n=n: e.matmul(pA[:, 0:n], lhsT=w_[:, kc, :], rhs=yaT[:, kc, t0:t0 + n], start=(kc == 0), stop=(kc == 5)),
                          reads=[bw_, bya], writes=[bA])
                for kc in range(16):
                    S.add("pe", lambda e, pBk=pBk, w_=w_, kc=kc, t0=t0, n=n: e.matmul(pBk[:, 0:n], lhsT=w_[:, 6 + kc, :], rhs=ysT[:, kc, t0:t0 + n], start=(kc == 0), stop=(kc == 15)),
                          reads=[bw_, bys], writes=[bBk])
                for kc in range(8):
                    S.add("pe", lambda e, pga=pga, w_=w_, kc=kc, t0=t0, n=n: e.matmul(pga[:, 0:n], lhsT=w_[:, 22 + kc, :], rhs=XN[:, kc, t0:t0 + n], start=(kc == 0), stop=(kc == 7)),
                          reads=[bw_, bXN], writes=[bga])
                for kc in range(8):
                    S.add("pe", lambda e, pgb=pgb, w_=w_, kc=kc, t0=t0, n=n: e.matmul(pgb[:, 0:n], lhsT=w_[:, 30 + kc, :], rhs=XN[:, kc, t0:t0 + n], start=(kc == 0), stop=(kc == 7)),
                          reads=[bw_, bXN], writes=[bgb])
                S.add("act", lambda e, pga=pga, n=n: e.activation(out=sa[:, 0:n], in_=pga[:, 0:n], func=AF.Tanh, scale=0.5), reads=[bga], writes=[bsa, bga])
                S.add("act", lambda e, pgb=pgb, n=n: e.activation(out=sbb[:, 0:n], in_=pgb[:, 0:n], func=AF.Tanh, scale=0.5), reads=[bgb], writes=[bsbb, bgb])
                S.add("dve", lambda e, pA=pA, n=n: e.scalar_tensor_tensor(out=m1[:, 0:n], in0=sa[:, 0:n], scalar=1.0, in1=pA[:, 0:n], op0=ALU.add, op1=ALU.mult),
                      reads=[bsa, bA], writes=[bm1, bA])
                S.add("dve", lambda e, pBk=pBk, n=n: e.scalar_tensor_tensor(out=m2[:, 0:n], in0=sbb[:, 0:n], scalar=1.0, in1=pBk[:, 0:n], op0=ALU.add, op1=ALU.mult),
                      reads=[bsbb, bBk], writes=[bm2, bBk])
                S.add("dve", lambda e, n=n: e.tensor_tensor(out=m1[:, 0:n], in0=m1[:, 0:n], in1=m2[:, 0:n], op=ALU.add), reads=[bm2], writes=[bm1])
                S.add("act", lambda e, dti=dti, t0=t0, n=n: e.activation(out=mT[:, dti, t0:t0 + n], in_=m1[:, 0:n], func=AF.Identity, scale=0.5),
                      reads=[bm1], writes=[bmT])
        S.barrier(scr)
        p5a.close()
        wo = SB(p5, "wo", [128, 8, D], BF16); bwo = Buf()
        S.dma(lambda e: e.dma_start(out=wo[:, :, 0:512], in_=w_o_v[:, :, 0:512]), writes=[bwo], eng="pool")
        S.dma(lambda e: e.dma_start(out=wo[:, :, 512:1024], in_=w_o_v[:, :, 512:1024]), writes=[bwo], eng="pool")
        fg = SB(p5, "fg", [128, D], F32); bfg = Buf()
        S.dma(lambda e: e.dma_start(out=fg[:], in_=final_g.to_broadcast([128, D])), writes=[bfg])
        xr = [SB(p5, "xr%d" % i, [128, D], F32) for i in range(2)]; bxr = [Buf(), Buf()]
        yo = xr; byo = bxr
        jk2 = SB(p5, "jk2", [128, D], BF16); bjk2 = Buf()
        fs = SB(p5, "fs", [128, 4], F32); bfs = Buf()
        for ti, (t0, rows) in enumerate(TT):
            x_, bx_ = xr[ti % 2], bxr[ti % 2]
            y_, by_ = yo[ti % 2], byo[ti % 2]
            S.dma(lambda e, x_=x_, t0=t0, rows=rows: e.dma_start(out=x_[0:rows, :], in_=x_all[t0:t0 + rows, :]), writes=[bx_])
            for hf in range(2):
                pb, bpb = PS[2 * (ti % 2) + hf], PB[2 * (ti % 2) + hf]
                for kc in range(8):
                    S.add("pe", lambda e, pb=pb, kc=kc, t0=t0, rows=rows, hf=hf: e.matmul(pb[0:rows, :], lhsT=mT[:, kc, t0:t0 + rows], rhs=wo[:, kc, 512 * hf:512 * (hf + 1)],
                                                                                      start=(kc == 0), stop=(kc == 7)), reads=[bmT, bwo], writes=[bpb])
                S.add("dve", lambda e, pb=pb, x_=x_, hf=hf, rows=rows: e.tensor_tensor(out=x_[0:rows, 512 * hf:512 * (hf + 1)], in0=x_[0:rows, 512 * hf:512 * (hf + 1)], in1=pb[0:rows, :], op=ALU.add),
                      reads=[bpb], writes=[bx_, bpb])
            S.add("dve", lambda e: e.memset(fs[:, 0:1], 0.0), writes=[bfs])
            S.add("act", lambda e, x_=x_, rows=rows: e.activation(out=jk2[0:rows, :], in_=x_[0:rows, :], func=AF.Square, accum_out=fs[0:rows, 0:1]), reads=[bx_], writes=[bjk2, bfs])
            S.add("act", lambda e, rows=rows: e.activation(out=fs[0:rows, 1:2], in_=fs[0:rows, 0:1], func=AF.Sqrt, scale=1.0 / D, bias=1e-6), reads=[bfs], writes=[bfs])
            S.add("dve", lambda e, rows=rows: e.reciprocal(out=fs[0:rows, 2:3], in_=fs[0:rows, 1:2]), reads=[bfs], writes=[bfs])
            S.add("dve", lambda e, x_=x_, y_=y_, rows=rows: e.scalar_tensor_tensor(out=y_[0:rows, :], in0=x_[0:rows, :], scalar=fs[0:rows, 2:3], in1=fg[0:rows, :], op0=ALU.mult, op1=ALU.mult),
                  reads=[bx_, bfs, bfg], writes=[by_])
            S.dma(lambda e, y_=y_, t0=t0, rows=rows: e.dma_start(out=y_out[t0:t0 + rows, :], in_=y_[0:rows, :]), reads=[by_])


_PROG = None


def kernel(x_prompt, x_sample, cache_k, cache_v, state_conv, state_ssm,
           norm_g, w_in, conv_w, conv_b, dt_bias, a_log, d_skip, ssm_norm,
           w_branch_a, w_branch_b, w_out, rel_bias, final_norm):
    global _PROG
    f = np.float32
    asf = lambda a: np.ascontiguousarray(np.asarray(a, dtype=f))
    x_prompt, x_sample = asf(x_prompt), asf(x_sample)
    cache_k, cache_v = np.asarray(cache_k, dtype=f), np.asarray(cache_v, dtype=f)
    state_conv, state_ssm = asf(state_conv), asf(state_ssm)
    rel_bias = asf(rel_bias)
    tb_i, fb_i, fn_i = _static_index_tables()
    rb_ext = np.concatenate([rel_bias, np.full((1, 12), NEG, f)], axis=0)
    tbraw = np.ascontiguousarray(rb_ext[tb_i].transpose(0, 2, 1))
    fbraw = np.ascontiguousarray(rb_ext[fb_i])
    fnraw = np.ascontiguousarray(rb_ext[fn_i])
    cst = _const_pack()
    cwT = np.ascontiguousarray(asf(conv_w)[0].reshape(4, 32, 128).transpose(2, 1, 0))
    cbT = np.ascontiguousarray(asf(conv_b)[0].reshape(32, 128).T)
    hpar = np.concatenate([asf(dt_bias)[0], asf(a_log)[0], asf(d_skip)[0]])[None, :]
    common = dict(w_in=asf(w_in)[0], w_a=asf(w_branch_a)[0], w_b=asf(w_branch_b)[0], w_o=asf(w_out)[0],
                  norm_g=asf(norm_g), final_g=asf(final_norm)[None, :], ssm_norm=asf(ssm_norm),
                  cwT=cwT, cbT=cbT, cb_row=asf(conv_b), hpar=np.ascontiguousarray(hpar), tbraw=tbraw, fbraw=fbraw, fnraw=fnraw, cst=cst)
    in_maps = []
    for c in range(NCORES):
        sl = slice(4 * c, 4 * c + 4)
        m = dict(common)
        m["x_all"] = np.ascontiguousarray(np.concatenate([x_prompt[c], x_sample[sl].reshape(NS, D)], axis=0))
        m["cache_k"] = np.ascontiguousarray(cache_k[0, sl].reshape(4, 2048, 768))
        m["cache_v"] = np.ascontiguousarray(cache_v[0, sl].reshape(4, 2048, 768))
        sc = state_conv[0, sl]
        m["scT"] = np.ascontiguousarray(sc.reshape(4, 3, 32, 128).transpose(3, 2, 0, 1))
        st = state_ssm[0, sl]
        m["st_nat"] = np.ascontiguousarray(st)
        m["st_T"] = np.ascontiguousarray(st.reshape(4, 2048, 128).transpose(2, 0, 1))
        in_maps.append(m)
    if _PROG is None:
        _PROG = build_program()
    res = run_bass_kernel_spmd(_PROG, in_maps, core_ids=list(range(NCORES)))
    R = res.results
    y_p = np.stack([R[c]["y_out"][:LP] for c in range(NCORES)])
    y_s = np.concatenate([R[c]["y_out"][LP:].reshape(4, 8, D) for c in range(NCORES)])
    k_p = np.stack([R[c]["k_out"][:LP].reshape(LP, 12, 64) for c in range(NCORES)])[None]
    v_p = np.stack([R[c]["v_out"][:LP].reshape(LP, 12, 64) for c in range(NCORES)])[None]
    k_s = np.concatenate([R[c]["k_out"][LP:].reshape(4, 8, 12, 64) for c in range(NCORES)])[None]
    v_s = np.concatenate([R[c]["v_out"][LP:].reshape(4, 8, 12, 64) for c in range(NCORES)])[None]
    c_p = np.stack([R[c]["conv_out"][0:3] for c in range(NCORES)])[None]
    c_s = np.concatenate([R[c]["conv_out"][3:15].reshape(4, 3, 4096) for c in range(NCORES)])[None]
    s_p = np.stack([R[c]["ssm_p"] for c in range(NCORES)])[None]
    s_s = np.concatenate([R[c]["ssm_s"] for c in range(NCORES)])[None]
    outs = (y_p, y_s, k_p, v_p, c_p, s_p, k_s, v_s, c_s, s_s)
    return tuple(np.ascontiguousarray(o.astype(np.float32)) for o in outs)
```

```python
import math
import numpy as np
from contextlib import ExitStack
import concourse.bass as bass
import concourse.mybir as mybir
from concourse.bass_utils import run_bass_kernel_spmd

F32 = mybir.dt.float32
BF16 = mybir.dt.bfloat16
ALU = mybir.AluOpType
AF = mybir.ActivationFunctionType

NCORES = 8
D = 1024
LP = 2048
NS = 32
NT = LP + NS
NIN = 11296
NEG = -30000.0


class Buf:
    __slots__ = ("name", "w", "r", "excl")

    def __init__(self, name="", excl=False):
        self.name = name
        self.w = None
        self.r = []
        self.excl = excl


class Op:
    __slots__ = ("eng", "fn", "deps", "hard", "idx", "signal", "token", "is_dma", "prev_token")

    def __init__(self, eng, fn, idx, is_dma):
        self.eng = eng
        self.fn = fn
        self.idx = idx
        self.deps = set()
        self.hard = set()
        self.signal = False
        self.token = None
        self.is_dma = is_dma
        self.prev_token = None


class Sched:
    ENGS = ("pe", "act", "dve", "pool", "sp")

    def __init__(self, nc, n_dma_sems=40):
        self.nc = nc
        self.ops = []
        self.n_dma_sems = n_dma_sems
        self.dma_since_barrier = []

    def add(self, eng, fn, reads=(), writes=(), is_dma=False):
        op = Op(eng, fn, len(self.ops), is_dma)
        self.ops.append(op)
        writes = list(writes) + [b for b in reads if b.excl]
        wset = set(id(b) for b in writes)
        for b in reads:
            if id(b) in wset:
                continue
            if b.w is not None:
                op.deps.add(b.w)
                op.hard.add(b.w)
            b.r.append(op.idx)
        done = set()
        for b in writes:
            if id(b) in done:
                continue
            done.add(id(b))
            if b.w is not None:
                op.deps.add(b.w)
                op.hard.add(b.w)
            op.deps.update(b.r)
            b.w = op.idx
            b.r = []
        op.deps.discard(op.idx)
        op.hard.discard(op.idx)
        if is_dma:
            self.dma_since_barrier.append(op.idx)
        return op

    def dma(self, fn, reads=(), writes=(), eng="sp"):
        return self.add(eng, fn, reads, writes, is_dma=True)

    def barrier(self, scratch):
        bars = {}
        firsts = []
        for i, e in enumerate(self.ENGS):
            b = Buf("bar")
            bars[e] = b
            if e in ("pe", "sp"):
                fn = lambda en: en.nop()
            elif e == "act":
                fn = (lambda en, i=i: en.copy(out=scratch[0:1, i:i + 1], in_=scratch[0:1, i:i + 1]))
            else:
                fn = (lambda en, i=i: en.memset(scratch[0:1, i:i + 1], 0.0))
            op = self.add(e, fn, writes=[b])
            firsts.append(op)
        for op in firsts:
            op.deps.update(self.dma_since_barrier)
            op.deps.discard(op.idx)
        self.dma_since_barrier = []
        for i, e in enumerate(self.ENGS):
            if e in ("pe", "sp"):
                fn = lambda en: en.nop()
            elif e == "act":
                fn = (lambda en, i=i: en.copy(out=scratch[0:1, 8 + i:9 + i], in_=scratch[0:1, 8 + i:9 + i]))
            else:
                fn = (lambda en, i=i: en.memset(scratch[0:1, 8 + i:9 + i], 0.0))
            self.add(e, fn, reads=list(bars.values()))

    def emit(self, stack):
        nc = self.nc
        ops = self.ops
        def needs(op, dop):
            if dop.is_dma or dop.eng != op.eng:
                return True
            if op.eng in ("pe", "sp"):
                return False
            return dop.idx in op.hard

        for op in ops:
            for d in op.deps:
                dop = ops[d]
                if needs(op, dop):
                    dop.signal = True
        esem = {e: stack.enter_context(nc.semaphore("s_" + e)) for e in self.ENGS}
        dsem = [stack.enter_context(nc.semaphore("d%d" % i)) for i in range(self.n_dma_sems)]
        cnt = {e: 0 for e in self.ENGS}
        duse = [0] * self.n_dma_sems
        ndma = 0
        for op in ops:
            if op.is_dma:
                k = ndma % self.n_dma_sems
                ndma += 1
                if duse[k] > 0:
                    op.prev_token = (dsem[k], 16 * duse[k])
                duse[k] += 1
                op.token = (dsem[k], 16 * duse[k])
            elif op.signal:
                cnt[op.eng] += 1
                op.token = (esem[op.eng], cnt[op.eng])
        per_eng = {e: [op for op in ops if op.eng == e] for e in self.ENGS}
        final_dma = [(dsem[k], 16 * duse[k]) for k in range(self.n_dma_sems) if duse[k] > 0]

        def run(ename, e):
            waited = {}
            for op in per_eng[ename]:
                need = {}
                for d in op.deps:
                    dop = ops[d]
                    if not needs(op, dop):
                        continue
                    s, v = dop.token
                    if need.get(s.num, (None, 0))[1] < v:
                        need[s.num] = (s, v)
                if op.prev_token is not None:
                    s, v = op.prev_token
                    if need.get(s.num, (None, 0))[1] < v:
                        need[s.num] = (s, v)
                for sn, (s, v) in need.items():
                    if waited.get(sn, 0) < v:
                        e.wait_ge(s, v)
                        waited[sn] = v
                ins = op.fn(e)
                if op.is_dma:
                    ins.then_inc(op.token[0], 16)
                elif op.signal:
                    ins.then_inc(op.token[0], 1)
            if ename == "sp":
                for s, v in final_dma:
                    if waited.get(s.num, 0) < v:
                        e.wait_ge(s, v)

        block = stack.enter_context(nc.Block())

        @block.tensor
        def _(e):
            run("pe", e)

        @block.scalar
        def _(e):
            run("act", e)

        @block.vector
        def _(e):
            run("dve", e)

        @block.gpsimd
        def _(e):
            run("pool", e)

        @block.sync
        def _(e):
            run("sp", e)


def pipeline(stage_lists):
    n = len(stage_lists)
    ns = max(len(x) for x in stage_lists) if n else 0
    for t in range(n + ns - 1):
        for k in reversed(range(ns)):
            i = t - k
            if 0 <= i < n and k < len(stage_lists[i]) and stage_lists[i][k] is not None:
                stage_lists[i][k]()


def _t5_bucket(dist):
    n_buckets, max_distance = 32, 2048
    max_exact = n_buckets // 2
    d = np.maximum(dist, 1).astype(np.float32)
    large = max_exact + (np.log(d / max_exact) / math.log(max_distance / max_exact)
                         * (n_buckets - max_exact)).astype(np.int32)
    large = np.minimum(large, n_buckets - 1)
    return np.where(dist < max_exact, dist, large).astype(np.int32)


GROUP_D = (1, 4, 16)


def _static_index_tables():
    bk = [_t5_bucket(np.arange(0, 129, dtype=np.int32) * d) for d in GROUP_D]
    k = np.arange(128)[:, None]
    q = np.arange(128)[None, :]
    tb = np.full((128, 640), 32, np.int32)
    for g in range(2):
        dist = q + 128 - k
        tb[:, 256 * g:256 * g + 128] = np.where(dist <= 128, bk[g][np.clip(dist, 0, 128)], 32)
        dist = q - k
        tb[:, 256 * g + 128:256 * g + 256] = np.where(dist >= 0, bk[g][np.clip(dist, 0, 128)], 32)
    dist = q - k
    tb[:, 512:640] = np.where(dist >= 0, bk[2][np.clip(dist, 0, 128)], 32)
    fb = np.full((128, 8, 3), 32, np.int32)
    m = np.arange(128)
    for i in range(8):
        fb[:, i, 2] = bk[2][128 - m]
        j = 128 + (i // 4) - m
        fb[:, i, 1] = np.where((j >= 1) & (j <= 128), bk[1][np.clip(j, 0, 128)], 32)
        j = 128 + i - m
        fb[:, i, 0] = np.where((j >= 1) & (j <= 128), bk[0][np.clip(j, 0, 128)], 32)
    fn = np.full((32, 32, 3), 32, np.int32)
    for s in range(4):
        for i in range(8):
            for ip in range(i + 1):
                dlt = i - ip
                p = s * 8 + ip
                fn[p, s * 8 + i, 0] = bk[0][dlt]
                if dlt % 4 == 0:
                    fn[p, s * 8 + i, 1] = bk[1][dlt // 4]
                if dlt == 0:
                    fn[p, s * 8 + i, 2] = bk[2][0]
    return tb, fb, fn


def _const_pack():
    c = np.zeros((128, 776), np.float32)
    c[:, 0:128] = np.eye(128)
    s = np.arange(128)[:, None]
    l = np.arange(128)[None, :]
    c[:, 128:256] = (s <= l)
    same = (s // 8 == l // 8) & (s < 32) & (l < 32)
    c[:, 256:384] = same & (s <= l)
    c[:, 384:512] = 1.0
    c[:, 512:640] = same
    for sq in range(4):
        c[sq * 8:(sq + 1) * 8, 640 + sq] = 1.0
    for sq in range(4):
        c[:, 648 + sq * 32 + sq * 8: 648 + sq * 32 + sq * 8 + 8] = 1.0
    return c


def build_program():
    nc = bass.Bass("TRN2", target_bir_lowering=False)

    def din(name, shape, dt=F32):
        return nc.dram_tensor(name, list(shape), dt, kind="ExternalInput").ap()

    def dout(name, shape):
        return nc.dram_tensor(name, list(shape), F32, kind="ExternalOutput").ap()

    x_all = din("x_all", [NT, D])
    w_in = din("w_in", [D, NIN])
    w_a = din("w_a", [768, D])
    w_b = din("w_b", [2048, D])
    w_o = din("w_o", [D, D])
    norm_g = din("norm_g", [1, D])
    final_g = din("final_g", [1, D])
    ssm_norm = din("ssm_norm", [1, 2048])
    cwT = din("cwT", [128, 32, 4])
    cbT = din("cbT", [128, 32])
    cb_row = din("cb_row", [1, 4096])
    hpar = din("hpar", [1, 96])
    tbraw = din("tbraw", [128, 12, 640])
    fbraw = din("fbraw", [128, 8, 3, 12])
    fnraw = din("fnraw", [32, 32, 3, 12])
    cst = din("cst", [128, 776])
    cache_k = din("cache_k", [4, 2048, 768])
    cache_v = din("cache_v", [4, 2048, 768])
    scT = din("scT", [128, 32, 4, 3])
    st_nat = din("st_nat", [4, 32, 64, 128])
    st_T = din("st_T", [128, 4, 2048])

    y_out = dout("y_out", [NT, D])
    k_out = dout("k_out", [NT, 768])
    v_out = dout("v_out", [NT, 768])
    conv_out = dout("conv_out", [15, 4096])
    ssm_p = dout("ssm_p", [32, 64, 128])
    ssm_s = dout("ssm_s", [4, 32, 64, 128])
    acT_d = nc.dram_tensor("acT_d", [32, NT], F32, kind="Internal").ap()

    w_in_v = w_in.rearrange("(kc p) n -> p kc n", p=128)

    TT = [(i * 128, 128) for i in range(16)] + [(LP, NS)]
    TB = [(i * 512, 512) for i in range(4)] + [(LP, NS)]

    with ExitStack() as top:
        S = Sched(nc)

        def SB(st, name, shape, dt):
            return st.enter_context(nc.sbuf_tensor(name, list(shape), dt))

        PS = [top.enter_context(nc.psum_tensor("ps%d" % i, [128, 512], F32)) for i in range(8)]
        PB = [Buf("ps%d" % i, excl=True) for i in range(8)]

        XN = SB(top, "XN", [128, 8, NT], BF16); bXN = Buf()
        ysT = SB(top, "ysT", [128, 16, NT], BF16); bys = Buf()
        tabs = ExitStack()
        wbuf = [SB(top, "wbuf0", [128, 8, 512], BF16)]
        bw = [Buf(), Buf()]
        nwb = [2]
        C = SB(top, "C", [128, 776], F32); bC = Buf()
        Cb = SB(top, "Cb", [128, 776], BF16); bCb = Buf()
        scr = SB(top, "scr", [128, 16], F32)
        hp_bc = SB(top, "hp_bc", [128, 96], F32); bhp = Buf()
        IDf = C[:, 0:128]; TRIp = C[:, 128:256]; TRIs = C[:, 256:384]; ONESf = C[:, 384:512]; ONESs = C[:, 512:640]
        SEG = C[0:32, 640:644]
        IDb = Cb[:, 0:128]; TRIpb = Cb[:, 128:256]; TRIsb = Cb[:, 256:384]; ONESb = Cb[:, 384:512]
        SEGXb = Cb[:, 648:776]

        S.dma(lambda e: e.dma_start(out=C[:], in_=cst), writes=[bC])
        S.add("dve", lambda e: e.tensor_copy(out=Cb[:], in_=C[:]), reads=[bC], writes=[bCb])
        S.dma(lambda e: e.dma_start(out=hp_bc[:], in_=hpar.to_broadcast([128, 96])), writes=[bhp])

        wcount = [0]

        def load_w(segs):
            i = wcount[0] % nwb[0]
            wcount[0] += 1
            t, b = wbuf[i], bw[i]
            for (src, nk, c0, n) in segs:
                S.dma(lambda e, src=src, nk=nk, c0=c0, n=n: e.dma_start(out=t[:, 0:nk, c0:c0 + n], in_=src),
                      writes=[b], eng="pool")
            return t, b

        def win_seg(col0, n, c0):
            return (w_in_v[:, :, col0:col0 + n], 8, c0, n)

        with ExitStack() as p0:
            gbc = SB(p0, "gbc", [128, D], F32); bg = Buf()
            xin = [SB(p0, "xin%d" % i, [128, D], F32) for i in range(2)]; bxin = [Buf(), Buf()]
            junk = SB(p0, "junk", [128, D], BF16); bjunk = Buf()
            hb = [SB(p0, "hb%d" % i, [128, D], BF16) for i in range(2)]; bhb = [Buf(), Buf()]
            ssq = SB(p0, "ssq", [128, 51], F32); bss = Buf()
            S.dma(lambda e: e.dma_start(out=gbc[:], in_=norm_g.to_broadcast([128, D])), writes=[bg])
            S.add("dve", lambda e: e.memset(ssq[:], 0.0), writes=[bss])
            for ti, (t0, rows) in enumerate(TT):
                xi, bxi = xin[ti % 2], bxin[ti % 2]
                S.dma(lambda e, xi=xi, t0=t0, rows=rows: e.dma_start(out=xi[0:rows, :], in_=x_all[t0:t0 + rows, :]), writes=[bxi])
                S.add("act", lambda e, xi=xi, rows=rows, ti=ti: e.activation(out=junk[0:rows, :], in_=xi[0:rows, :], func=AF.Square,
                                                                             accum_out=ssq[0:rows, ti:ti + 1]), reads=[bxi], writes=[bjunk, bss])
            S.add("act", lambda e: e.activation(out=ssq[:, 17:34], in_=ssq[:, 0:17], func=AF.Sqrt, scale=1.0 / D, bias=1e-6), reads=[bss], writes=[bss])
            S.add("dve", lambda e: e.reciprocal(out=ssq[:, 34:51], in_=ssq[:, 17:34]), reads=[bss], writes=[bss])
            for ti, (t0, rows) in enumerate(TT):
                xi, bxi = xin[ti % 2], bxin[ti % 2]
                h_, bh_ = hb[ti % 2], bhb[ti % 2]
                pb, bpb = PS[ti % 2], PB[ti % 2]
                S.dma(lambda e, xi=xi, t0=t0, rows=rows: e.dma_start(out=xi[0:rows, :], in_=x_all[t0:t0 + rows, :]), writes=[bxi])
                S.add("dve", lambda e, xi=xi, h_=h_, rows=rows, ti=ti: e.scalar_tensor_tensor(
                    out=h_[0:rows, :], in0=xi[0:rows, :], scalar=ssq[0:rows, 34 + ti:35 + ti], in1=gbc[0:rows, :],
                    op0=ALU.mult, op1=ALU.mult), reads=[bxi, bss, bg], writes=[bh_])
                pv = pb[:].bitcast(BF16)
                for kc in range(8):
                    S.add("pe", lambda e, pv=pv, h_=h_, kc=kc, rows=rows: e.transpose(
                        out=pv[:, kc * 128:kc * 128 + rows], in_=h_[0:rows, kc * 128:(kc + 1) * 128],
                        identity=IDb[0:rows, 0:rows]), reads=[bh_, bCb], writes=[bpb])
                S.add("act", lambda e, pv=pv, t0=t0, rows=rows: e.copy(
                    out=XN[:, :, t0:t0 + rows], in_=pv.rearrange("p (k t) -> p k t", k=8)[:, :, 0:rows]),
                    reads=[bpb], writes=[bXN, bpb])
        S.barrier(scr)

        wbuf.append(SB(tabs, "wbuf1", [128, 8, 512], BF16))
        dtt = SB(tabs, "dtt", [128, 17, 32], F32)
        acum = SB(tabs, "acum", [128, 17, 32], F32)
        eacum = SB(tabs, "eacum", [128, 17, 32], F32)
        dtw = SB(tabs, "dtw", [128, 17, 32], F32)
        decay = SB(tabs, "decay", [128, 17, 32], F32)
        totb = SB(tabs, "totb", [128, 17, 32], F32)
        nacum = SB(tabs, "nacum", [128, 17, 32], F32)
        bT = Buf()
        with ExitStack() as p1:
            dta = SB(p1, "dta", [128, 17, 32], F32)
            acTs = SB(p1, "acTs", [32, NT], F32); bacT = Buf()
            wt, bwt = load_w([win_seg(9216, 32, 0)])
            S.add("dve", lambda e: e.memset(dtt[:], 0.0), writes=[bT])
            for ti, (t0, rows) in enumerate(TT):
                pb, bpb = (PS[2], PB[2]) if ti < 16 else (PS[3], PB[3])
                c0 = (ti % 16) * 32
                for kc in range(8):
                    S.add("pe", lambda e, pb=pb, kc=kc, t0=t0, rows=rows, c0=c0: e.matmul(
                        pb[0:rows, c0:c0 + 32], lhsT=XN[:, kc, t0:t0 + rows], rhs=wt[:, kc, 0:32],
                        start=(kc == 0), stop=(kc == 7)), reads=[bXN, bwt], writes=[bpb])
            S.add("dve", lambda e: e.tensor_tensor(out=dtt[:, 0:16, :], in0=PS[2][:].rearrange("p (c h) -> p c h", h=32),
                                                   in1=hp_bc[:, 0:32].unsqueeze(1).to_broadcast([128, 16, 32]), op=ALU.add),
                  reads=[PB[2], bhp], writes=[bT, PB[2]])
            S.add("dve", lambda e: e.tensor_tensor(out=dtt[0:32, 16, :], in0=PS[3][0:32, 0:32], in1=hp_bc[0:32, 0:32], op=ALU.add),
                  reads=[PB[3], bhp], writes=[bT, PB[3]])
            S.add("act", lambda e: e.activation(out=dtt[:], in_=dtt[:], func=AF.Exp), reads=[bT], writes=[bT])
            S.add("act", lambda e: e.activation(out=dtt[:], in_=dtt[:], func=AF.Ln, bias=1.0), reads=[bT], writes=[bT])
            S.add("dve", lambda e: e.memset(dtt[32:64, 16, :], 0.0), writes=[bT])
            S.add("dve", lambda e: e.memset(dtt[64:128, 16, :], 0.0), writes=[bT])
            S.add("act", lambda e: e.activation(out=hp_bc[:, 32:64], in_=hp_bc[:, 32:64], func=AF.Exp), reads=[bhp], writes=[bhp])
            S.add("dve", lambda e: e.scalar_tensor_tensor(out=dta[:], in0=dtt[:], scalar=-1.0,
                                                          in1=hp_bc[:, 32:64].unsqueeze(1).to_broadcast([128, 17, 32]),
                                                          op0=ALU.mult, op1=ALU.mult), reads=[bT, bhp], writes=[bT])
            for c in range(17):
                tri = TRIp if c < 16 else TRIs
                ones = ONESf if c < 16 else ONESs
                pa, bpa = (PS[4], PB[4]) if c < 16 else (PS[5], PB[5])
                pt, bpt = (PS[6], PB[6]) if c < 16 else (PS[7], PB[7])
                c0 = (c % 16) * 32
                S.add("pe", lambda e, pa=pa, tri=tri, c=c, c0=c0: e.matmul(pa[:, c0:c0 + 32], lhsT=tri, rhs=dta[:, c, :],
                                                                         start=True, stop=True), reads=[bT, bC], writes=[bpa])
                S.add("pe", lambda e, pt=pt, ones=ones, c=c, c0=c0: e.matmul(pt[:, c0:c0 + 32], lhsT=ones, rhs=dta[:, c, :],
                                                                           start=True, stop=True), reads=[bT, bC], writes=[bpt])
            S.add("dve", lambda e: e.tensor_copy(out=acum[:, 0:16, :], in_=PS[4][:].rearrange("p (c h) -> p c h", h=32)),
                  reads=[PB[4]], writes=[bT, PB[4]])
            S.add("dve", lambda e: e.tensor_copy(out=acum[:, 16, :], in_=PS[5][:, 0:32]), reads=[PB[5]], writes=[bT, PB[5]])
            S.add("dve", lambda e: e.tensor_copy(out=totb[:, 0:16, :], in_=PS[6][:].rearrange("p (c h) -> p c h", h=32)),
                  reads=[PB[6]], writes=[bT, PB[6]])
            S.add("dve", lambda e: e.tensor_copy(out=totb[:, 16, :], in_=PS[7][:, 0:32]), reads=[PB[7]], writes=[bT, PB[7]])
            S.add("act", lambda e: e.activation(out=eacum[:], in_=acum[:], func=AF.Exp), reads=[bT], writes=[bT])
            S.add("act", lambda e: e.activation(out=decay[:], in_=totb[:], func=AF.Exp), reads=[bT], writes=[bT])
            S.add("dve", lambda e: e.tensor_sub(out=dtw[:], in0=totb[:], in1=acum[:]), reads=[bT], writes=[bT])
            S.add("act", lambda e: e.activation(out=dtw[:], in_=dtw[:], func=AF.Exp), reads=[bT], writes=[bT])
            S.add("dve", lambda e: e.tensor_mul(out=dtw[:], in0=dtw[:], in1=dtt[:]), reads=[bT], writes=[bT])
            S.add("dve", lambda e: e.tensor_scalar(out=nacum[:], in0=acum[:], scalar1=-1.0, scalar2=None, op0=ALU.mult), reads=[bT], writes=[bT])
            for c in range(17):
                tri = TRIp if c < 16 else TRIs
                pb, bpb = PS[c % 2], PB[c % 2]
                rows = 128 if c < 16 else 32
                S.add("pe", lambda e, pb=pb, tri=tri, c=c, rows=rows: e.matmul(pb[0:32, 0:rows], lhsT=dta[:, c, :], rhs=tri[:, 0:rows],
                                                                             start=True, stop=True), reads=[bT, bC], writes=[bpb])
                S.add("dve", lambda e, pb=pb, c=c, rows=rows: e.tensor_copy(out=acTs[:, c * 128:c * 128 + rows], in_=pb[0:32, 0:rows]),
                      reads=[bpb], writes=[bacT, bpb])
            bscr = Buf()
            S.dma(lambda e: e.dma_start(out=acT_d, in_=acTs[:]), reads=[bacT], writes=[bscr])
        S.barrier(scr)

        with ExitStack() as p2:
            cw = SB(p2, "cw", [128, 32, 4], F32); bcw = Buf()
            cb = SB(p2, "cb", [128, 32], F32)
            sct = SB(p2, "sct", [128, 32, 4, 3], F32)
            S.dma(lambda e: e.dma_start(out=cw[:], in_=cwT), writes=[bcw])
            S.dma(lambda e: e.dma_start(out=cb[:], in_=cbT), writes=[bcw])
            S.dma(lambda e: e.dma_start(out=sct[:], in_=scT), writes=[bcw])
            S.add("pool", lambda e: e.tensor_scalar(out=cw[:], in0=cw[:], scalar1=0.5, scalar2=None, op0=ALU.mult), reads=[bcw], writes=[bcw])
            S.add("pool", lambda e: e.tensor_scalar(out=cb[:], in0=cb[:], scalar1=0.5, scalar2=None, op0=ALU.mult), reads=[bcw], writes=[bcw])
            hsel = SB(p2, "hsel", [128, 8, 15], BF16); bhsel = Buf()
            S.add("pool", lambda e: e.tensor_copy(out=hsel[:, :, 0:3], in_=XN[:, :, 2045:2048]), reads=[bXN], writes=[bhsel])
            for s in range(4):
                S.add("pool", lambda e, s=s: e.tensor_copy(out=hsel[:, :, 3 + 3 * s:6 + 3 * s], in_=XN[:, :, LP + 8 * s + 5:LP + 8 * s + 8]),
                      reads=[bXN], writes=[bhsel])
            wz = [SB(p2, "wz%d" % i, [128, 8, 256], BF16) for i in range(1)] * 2; bwz = [Buf()] * 2
            xwb = SB(p2, "xwb", [128, 3 + LP], BF16); bxw = Buf()
            xwsb = SB(p2, "xwsb", [128, 4, 11], BF16); bxws = Buf()
            tnh2 = [SB(p2, "tnh2_%d" % i, [128, 512], BF16) for i in range(2)]; btnh2 = [Buf(), Buf()]
            dg = SB(p2, "dg", [128, 4, 128], BF16); bdg = Buf()
            cbrf = SB(p2, "cbrf", [1, 128], F32); bcbrf = Buf()
            cbrb = SB(p2, "cbrb", [1, 128], BF16); bcbrb = Buf()
            onesrow = SB(p2, "onesrow", [1, 512], BF16); bones = Buf()
            S.add("pool", lambda e: e.memset(onesrow[:], 1.0), writes=[bones])
            xdt2 = [SB(p2, "xdt%d" % i, [128, 256], BF16) for i in range(2)]; bxdt2 = [Buf(), Buf()]
            xdb2 = [SB(p2, "xdb%d" % i, [128, 256], BF16) for i in range(2)]; bxdb2 = [Buf(), Buf()]
            XT = SB(p2, "XT", [128, 4, NT], BF16); bXT = [Buf() for _ in range(4)]
            xtok = SB(p2, "xtok", [128, 17, 256], BF16); bxtk = [Buf() for _ in range(17)]
            btok = SB(p2, "btok", [128, 17, 128], BF16); bbtok = Buf()
            CBm = SB(p2, "CBm", [128, 17, 128], BF16); bCBm = Buf()
            CTs = SB(p2, "CTs", [128, 4, 32], BF16); bCTs = Buf()
            abc = [SB(p2, "abc%d" % i, [128, 4, 128], F32) for i in range(2)]; babc = [Buf() for _ in range(2)]
            Dt = [SB(p2, "Dt%d" % i, [128, 2, 128], F32) for i in range(1)] * 2; bDt = [Buf()] * 2
            Et = [SB(p2, "Et%d" % i, [128, 4, 128], BF16) for i in range(2)]; bEt = [Buf(), Buf()]
            Mt = [SB(p2, "Mt%d" % i, [128, 4, 128], BF16) for i in range(2)]; bMt = [Buf(), Buf()]
            Bw = [SB(p2, "Bw%d" % i, [128, 256], BF16) for i in range(2)]; bBw = [Buf(), Buf()]
            yt = [SB(p2, "yt%d" % i, [128, 256], F32) for i in range(2)]; byt = [Buf(), Buf()]
            STt = SB(p2, "STt", [128, 256], F32); bST = Buf()
            STb = SB(p2, "STb", [128, 256], BF16); bSTb = Buf()
            tz = SB(p2, "tz", [128, 256], F32); btz = Buf()
            uz = tz; buz = btz
            yg = tz; byg = btz
            ynb2 = [SB(p2, "ynb%d" % i, [128, 256], BF16) for i in range(2)]; bynb2 = [Buf(), Buf()]; ynb = ynb2[0]; bynb = bynb2[0]
            gs = SB(p2, "gs", [128, 51], F32); bgs = Buf()
            nrm = SB(p2, "nrm", [128, 256], F32); bnrm = Buf()
            cvo = abc[0][:, :, :].rearrange("p a b -> p (a b)"); bcvo = babc[0]
            h0T = SB(p2, "h0T", [128, 4, 256], BF16); bh0T = Buf()
            h0n = [SB(p2, "h0n%d" % i, [128, 2, 128], F32) for i in range(1)] * 2; bh0n = [Buf()] * 2
            wxm = SB(p2, "wxm", [32, 256], BF16); bwxm = Buf(); xsr = SB(p2, "xsr", [32, 256], BF16); bxsr = Buf()
            dsg = SB(p2, "dsg", [32, 4, 4], F32); bdsg = Buf()
            dtaE = SB(p2, "dtaE", [32, 256], F32); bdtaE = Buf()
            dcol = SB(p2, "dcol", [128, 2, 4], F32); bdcol = Buf()
            nst = [SB(p2, "nst%d" % i, [128, 2, 128], F32) for i in range(1)] * 2; bnst = [Buf()] * 2
            stp = nst[0]; bstp = bnst[0]
            jk = ynb; bjk = bynb

            WT = {}
            wzt, bwzt = wz[0], bwz[0]

            def g_loadw(g):
                WT[g] = load_w([win_seg(5120 + 256 * g, 256, 0), win_seg(7168 + 128 * g, 128, 256), win_seg(8192 + 128 * g, 128, 384)])

            def g_pre(g):
                wt, bwt = WT[g]
                S.dma(lambda e: e.dma_start(out=wzt[:], in_=w_in_v[:, :, 3072 + 256 * g:3072 + 256 * (g + 1)]), writes=[bwzt], eng="pool")
                S.dma(lambda e: e.dma_start(out=nrm[:], in_=ssm_norm[:, 256 * g:256 * (g + 1)].to_broadcast([128, 256])), writes=[bnrm])
                S.dma(lambda e: e.dma_start(out=h0T[:], in_=st_T[:, :, 256 * g:256 * (g + 1)]), writes=[bh0T], eng="pool")
                for kc in range(8):
                    S.add("pe", lambda e, kc=kc, wt=wt: e.matmul(PS[2][0:15, :], lhsT=hsel[:, kc, :], rhs=wt[:, kc, :],
                                                               start=(kc == 0), stop=(kc == 7)), reads=[bhsel, bwt], writes=[PB[2]])
                S.add("act", lambda e: e.copy(out=cvo[0:15, :], in_=PS[2][0:15, :]), reads=[PB[2]], writes=[bcvo, PB[2]])
                S.dma(lambda e, g=g: e.dma_start(out=conv_out[:, 256 * g:256 * (g + 1)], in_=cvo[0:15, 0:256]), reads=[bcvo])
                S.dma(lambda e, g=g: e.dma_start(out=conv_out[:, 2048 + 128 * g:2048 + 128 * (g + 1)], in_=cvo[0:15, 256:384]), reads=[bcvo])
                S.dma(lambda e, g=g: e.dma_start(out=conv_out[:, 3072 + 128 * g:3072 + 128 * (g + 1)], in_=cvo[0:15, 384:512]), reads=[bcvo])

            def g_conv_items(g, tiles):
                wt, bwt = WT[g]
                conv_items = []
                for ct in tiles:
                    ctile = (2 * g + ct) if ct < 2 else (16 + g if ct == 2 else 24 + g)
                    for bi, (t0, n) in enumerate(TB):
                        def c0(ct=ct, ctile=ctile, bi=bi, t0=t0, n=n, wt=wt, bwt=bwt):
                            pb, bpb = PS[bi % 2], PB[bi % 2]
                            if bi == 0:
                                S.add("pool", lambda e: e.memset(xwb[:, 0:3], 0.0), writes=[bxw])
                                S.add("pool", lambda e: e.tensor_copy(out=xwsb[:, :, 0:3], in_=sct[:, ctile, :, :]), reads=[bcw], writes=[bxws])
                                S.add("pool", lambda e: e.tensor_tensor(out=dg[:], in0=IDb.unsqueeze(1).to_broadcast([128, 4, 128]),
                                                                        in1=cw[:, ctile, :].unsqueeze(2).to_broadcast([128, 4, 128]), op=ALU.mult),
                                      reads=[bCb, bcw], writes=[bdg])
                                S.dma(lambda e: e.dma_start(out=cbrf[:], in_=cb_row[:, ctile * 128:(ctile + 1) * 128]), writes=[bcbrf])
                                S.add("pool", lambda e: e.tensor_scalar(out=cbrb[:], in0=cbrf[:], scalar1=0.5, scalar2=None, op0=ALU.mult), reads=[bcbrf], writes=[bcbrb])
                            for kc in range(8):
                                S.add("pe", lambda e, kc=kc: e.matmul(pb[:, 0:n], lhsT=wt[:, kc, ct * 128:(ct + 1) * 128], rhs=XN[:, kc, t0:t0 + n],
                                                                     start=(kc == 0), stop=(kc == 7)), reads=[bXN, bwt], writes=[bpb])
                            if bi < 4:
                                S.add("act", lambda e: e.copy(out=xwb[:, 3 + 512 * bi:3 + 512 * (bi + 1)], in_=pb[:, :]), reads=[bpb], writes=[bxw, bpb])
                            else:
                                S.add("act", lambda e: e.copy(out=xwsb[:, :, 3:11], in_=pb[:, 0:32].rearrange("p (s t) -> p s t", s=4)),
                                      reads=[bpb], writes=[bxws, bpb])

                        def c1(ct=ct, bi=bi):
                            cps, bcps = PS[2 + (bi % 2)], PB[2 + (bi % 2)]
                            tn_, btn_ = tnh2[bi % 2], btnh2[bi % 2]
                            if bi < 4:
                                S.add("pe", lambda e: e.matmul(cps[:, :], lhsT=cbrb[0:1, :], rhs=onesrow[0:1, :], start=True, stop=False),
                                      reads=[bcbrb, bones], writes=[bcps])
                                for tap in range(4):
                                    S.add("pe", lambda e, tap=tap: e.matmul(cps[:, :], lhsT=dg[:, tap, :], rhs=xwb[:, 512 * bi + tap:512 * bi + tap + 512],
                                                                           start=False, stop=(tap == 3)), reads=[bdg, bxw], writes=[bcps])
                                S.add("act", lambda e: e.activation(out=tn_[:], in_=cps[:, :], func=AF.Tanh), reads=[bcps], writes=[btn_])
                                S.add("dve", lambda e: e.scalar_tensor_tensor(out=XT[:, ct, 512 * bi:512 * (bi + 1)], in0=tn_[:], scalar=1.0, in1=cps[:, :],
                                                                              op0=ALU.add, op1=ALU.mult), reads=[btn_, bcps], writes=[bXT[ct]])
                            else:
                                S.add("pe", lambda e: e.matmul(cps[:, 0:32], lhsT=cbrb[0:1, :], rhs=onesrow[0:1, 0:32], start=True, stop=False),
                                      reads=[bcbrb, bones], writes=[bcps])
                                for tap in range(4):
                                    S.add("pe", lambda e, tap=tap: e.matmul(cps[:, 0:32].rearrange("p (s t) -> p s t", s=4), lhsT=dg[:, tap, :],
                                                                           rhs=xwsb[:, :, tap:tap + 8], start=False, stop=(tap == 3)),
                                          reads=[bdg, bxws], writes=[bcps])
                                S.add("act", lambda e: e.activation(out=tn_[:, 0:32], in_=cps[:, 0:32], func=AF.Tanh), reads=[bcps], writes=[btn_])
                                S.add("dve", lambda e: e.scalar_tensor_tensor(out=XT[:, ct, LP:NT], in0=tn_[:, 0:32], scalar=1.0, in1=cps[:, 0:32],
                                                                              op0=ALU.add, op1=ALU.mult), reads=[btn_, bcps], writes=[bXT[ct]])
                        conv_items.append([c0, c1])
                return conv_items

            def g_mid(g):
                for c4 in range(5):
                    chunks = list(range(c4 * 4, min(c4 * 4 + 4, 17)))
                    ix = c4 % 2
                    pvx, bpx = PS[0 + ix][:].bitcast(BF16), PB[0 + ix]
                    pvb, bpb_ = PS[3 + ix][:].bitcast(BF16), PB[3 + ix]
                    pcb, bpc = (PS[2], PB[2]) if ix == 0 else (PS[5], PB[5])
                    for j, c in enumerate(chunks):
                        t0, rows = TT[c]
                        for k in range(2):
                            S.add("pe", lambda e, pvx=pvx, j=j, k=k, t0=t0, rows=rows: e.transpose(
                                out=pvx[0:rows, j * 256 + k * 128:j * 256 + (k + 1) * 128], in_=XT[:, k, t0:t0 + rows], identity=IDb),
                                reads=[bXT[k], bCb], writes=[bpx])
                    nchk = len(chunks)
                    rws = 128 if chunks[-1] < 16 else 32
                    S.add("act", lambda e, pvx=pvx, c4=c4, nchk=nchk, rws=rws: e.copy(
                        out=xtok[0:rws, c4 * 4:c4 * 4 + nchk, :], in_=pvx[0:rws, 0:nchk * 256].rearrange("p (c x) -> p c x", x=256)),
                        reads=[bpx], writes=[bxtk[c_] for c_ in chunks] + [bpx])
                    for j, c in enumerate(chunks):
                        t0, rows = TT[c]
                        S.add("pe", lambda e, pvb=pvb, j=j, t0=t0, rows=rows: e.transpose(
                            out=pvb[0:rows, j * 128:(j + 1) * 128], in_=XT[:, 2, t0:t0 + rows], identity=IDb),
                            reads=[bXT[2], bCb], writes=[bpb_])
                    S.add("act", lambda e, pvb=pvb, c4=c4, nchk=nchk, rws=rws: e.copy(
                        out=btok[0:rws, c4 * 4:c4 * 4 + nchk, :], in_=pvb[0:rws, 0:nchk * 128].rearrange("p (c x) -> p c x", x=128)),
                        reads=[bpb_], writes=[bbtok, bpb_])
                    for j, c in enumerate(chunks):
                        t0, rows = TT[c]
                        S.add("pe", lambda e, pcb=pcb, j=j, t0=t0, rows=rows: e.matmul(
                            pcb[0:rows, j * 128:j * 128 + rows], lhsT=XT[:, 2, t0:t0 + rows], rhs=XT[:, 3, t0:t0 + rows],
                            start=True, stop=True), reads=[bXT[2], bXT[3]], writes=[bpc])
                    if rws == 128:
                        S.add("dve", lambda e, pcb=pcb, c4=c4, nchk=nchk: e.tensor_tensor(
                            out=CBm[:, c4 * 4:c4 * 4 + nchk, :], in0=pcb[:, 0:nchk * 128].rearrange("p (c x) -> p c x", x=128),
                            in1=TRIpb.unsqueeze(1).to_broadcast([128, nchk, 128]), op=ALU.mult),
                            reads=[bpc, bCb], writes=[bCBm, bpc])
                    else:
                        S.add("dve", lambda e, pcb=pcb: e.tensor_tensor(out=CBm[0:32, 16, 0:32], in0=pcb[0:32, 0:32], in1=TRIsb[0:32, 0:32], op=ALU.mult),
                              reads=[bpc, bCb], writes=[bCBm, bpc])
                S.add("pool", lambda e: e.tensor_tensor(out=CTs[:], in0=XT[:, 3, LP:NT].unsqueeze(1).to_broadcast([128, 4, 32]),
                                                        in1=SEGXb.rearrange("p (s t) -> p s t", s=4), op=ALU.mult),
                      reads=[bXT[3], bCb], writes=[bCTs])
                S.add("dve", lambda e, g=g: e.tensor_tensor(out=dsg[:], in0=dtw[0:32, 16, 4 * g:4 * g + 4].unsqueeze(1).to_broadcast([32, 4, 4]),
                                                           in1=SEG.unsqueeze(2).to_broadcast([32, 4, 4]), op=ALU.mult), reads=[bT, bC], writes=[bdsg])
                S.add("pool", lambda e: e.tensor_copy(out=xsr[:], in_=xtok[0:32, 16, :]), reads=[bxtk[16]], writes=[bxsr])
                S.add("dve", lambda e: e.memset(gs[:], 0.0), writes=[bgs])
                S.add("dve", lambda e: e.memset(STt[:], 0.0), writes=[bST])
                S.add("dve", lambda e: e.memset(STb[:], 0.0), writes=[bSTb])

            def g_chunk_items(g):
                chunk_items = []
                for c in range(17):
                    def s0(c=c, g=g):
                        t0, rows = TT[c]
                        a_, ba_ = abc[c % 2], babc[c % 2]
                        d_, bd_ = Dt[c % 2], bDt[c % 2]
                        e_, be_ = Et[c % 2], bEt[c % 2]
                        w_, bw_ = Bw[c % 2], bBw[c % 2]
                        xdt, bxdt = xdt2[c % 2], bxdt2[c % 2]
                        xdb, bxdb = xdb2[c % 2], bxdb2[c % 2]
                        S.dma(lambda e: e.dma_start(out=a_[:, :, 0:rows], in_=acT_d[4 * g:4 * g + 4, t0:t0 + rows].partition_broadcast(128)),
                              reads=[bscr], writes=[ba_])
                        for j in (2, 3):
                            hh = 4 * g + j
                            S.add("dve", lambda e, j=j, hh=hh: e.tensor_scalar(
                                out=d_[0:rows, j - 2, 0:rows], in0=a_[0:rows, j, 0:rows], scalar1=acum[0:rows, c, hh:hh + 1], scalar2=0.0,
                                op0=ALU.subtract, op1=ALU.min), reads=[ba_, bT], writes=[bd_])
                        for j in (0, 1):
                            hh = 4 * g + j
                            S.add("act", lambda e, j=j, hh=hh: e.activation(out=e_[0:rows, j, 0:rows], in_=a_[0:rows, j, 0:rows], func=AF.Exp,
                                                                           bias=nacum[0:rows, c, hh:hh + 1]), reads=[ba_, bT], writes=[be_])
                        S.add("act", lambda e: e.activation(out=e_[0:rows, 2:4, 0:rows], in_=d_[0:rows, 0:2, 0:rows], func=AF.Exp), reads=[bd_], writes=[be_])
                        S.add("pool", lambda e: e.tensor_tensor(
                            out=xdt[0:rows, :].rearrange("p (h x) -> p h x", h=4), in0=xtok[0:rows, c, :].rearrange("p (h x) -> p h x", h=4),
                            in1=dtt[0:rows, c, 4 * g:4 * g + 4].unsqueeze(2).to_broadcast([rows, 4, 64]), op=ALU.mult), reads=[bxtk[c], bT], writes=[bxdt])
                        S.add("pool", lambda e: e.tensor_tensor(
                            out=xdb[0:rows, :].rearrange("p (h x) -> p h x", h=4), in0=xtok[0:rows, c, :].rearrange("p (h x) -> p h x", h=4),
                            in1=hp_bc[0:rows, 64 + 4 * g:68 + 4 * g].unsqueeze(2).to_broadcast([rows, 4, 64]), op=ALU.mult), reads=[bxtk[c], bhp], writes=[bxdb])
                        if c < 16:
                            S.add("pool", lambda e: e.tensor_tensor(
                                out=w_[:, :].rearrange("p (h x) -> p h x", h=4), in0=xtok[:, c, :].rearrange("p (h x) -> p h x", h=4),
                                in1=dtw[:, c, 4 * g:4 * g + 4].unsqueeze(2).to_broadcast([128, 4, 64]), op=ALU.mult), reads=[bxtk[c], bT], writes=[bw_])

                    def s1(c=c, g=g, wzt=wzt, bwzt=bwzt):
                        t0, rows = TT[c]
                        e_, be_ = Et[c % 2], bEt[c % 2]
                        m_, bm_ = Mt[c % 2], bMt[c % 2]
                        w_, bw_ = Bw[c % 2], bBw[c % 2]
                        xdt, bxdt = xdt2[c % 2], bxdt2[c % 2]
                        xdb, bxdb = xdb2[c % 2], bxdb2[c % 2]
                        yb, byb = PS[4 + (c % 2)], PB[4 + (c % 2)]
                        sbk, bsbk = PS[6], PB[6]
                        zb, bzb = PS[7], PB[7]
                        S.add("dve", lambda e: e.scalar_tensor_tensor(
                            out=m_[0:rows, :, 0:rows], in0=e_[0:rows, :, 0:rows], scalar=1.0, in1=CBm[0:rows, c, 0:rows].unsqueeze(1).to_broadcast([rows, 4, rows]),
                            op0=ALU.min, op1=ALU.mult), reads=[be_, bCBm], writes=[bm_])
                        for kc in range(8):
                            S.add("pe", lambda e, kc=kc: e.matmul(zb[0:rows, 0:256], lhsT=XN[:, kc, t0:t0 + rows], rhs=wzt[:, kc, :],
                                                                 start=(kc == 0), stop=(kc == 7)), reads=[bXN, bwzt], writes=[bzb])
                        if c < 16:
                            S.add("pe", lambda e: e.matmul(sbk[:, 0:256], lhsT=btok[:, c, :], rhs=w_[:, :], start=True, stop=True),
                                  reads=[bw_, bbtok], writes=[bsbk])
                            S.add("pe", lambda e: e.matmul(yb[:, 256:512], lhsT=XT[:, 3, t0:t0 + 128], rhs=STb[:],
                                                           start=True, stop=True, skip_group_check=True), reads=[bXT[3], bSTb], writes=[byb])
                        else:
                            for sq in range(4):
                                S.add("pe", lambda e, sq=sq: e.matmul(yb[0:32, 256:512], lhsT=CTs[:, sq, :], rhs=h0T[:, sq, :],
                                                                     start=(sq == 0), stop=(sq == 3), skip_group_check=True), reads=[bCTs, bh0T], writes=[byb])
                        S.add("pe", lambda e: e.matmul(yb[0:rows, 0:256], lhsT=IDb[0:rows, 0:rows], rhs=xdb[0:rows, :],
                                                       start=False, stop=False, skip_group_check=True), reads=[bxdb, bCb], writes=[byb])
                        for j in range(4):
                            S.add("pe", lambda e, j=j: e.matmul(yb[0:rows, 64 * j:64 * j + 64], lhsT=m_[0:rows, j, 0:rows], rhs=xdt[0:rows, 64 * j:64 * j + 64],
                                                               start=False, stop=True, skip_group_check=True), reads=[bm_, bxdt], writes=[byb])

                    def s2(c=c, g=g):
                        t0, rows = TT[c]
                        y_, by_ = yt[c % 2], byt[c % 2]
                        yb, byb = PS[4 + (c % 2)], PB[4 + (c % 2)]
                        sbk, bsbk = PS[6], PB[6]
                        zb, bzb = PS[7], PB[7]
                        if c < 16:
                            S.add("pool", lambda e: e.tensor_tensor(
                                out=STt[:].rearrange("p (h x) -> p h x", h=4), in0=STt[:].rearrange("p (h x) -> p h x", h=4),
                                in1=decay[:, c, 4 * g:4 * g + 4].unsqueeze(2).to_broadcast([128, 4, 64]), op=ALU.mult), reads=[bT, bSTb], writes=[bST])
                            S.add("dve", lambda e: e.tensor_tensor(out=STt[:], in0=STt[:], in1=sbk[:, 0:256], op=ALU.add), reads=[bsbk], writes=[bST, bsbk])
                            S.add("act", lambda e: e.copy(out=STb[:], in_=STt[:]), reads=[bST], writes=[bSTb])
                        S.add("dve", lambda e: e.tensor_tensor(
                            out=y_[0:rows, :].rearrange("p (h x) -> p h x", h=4), in0=yb[0:rows, 256:512].rearrange("p (h x) -> p h x", h=4),
                            in1=eacum[0:rows, c, 4 * g:4 * g + 4].unsqueeze(2).to_broadcast([rows, 4, 64]), op=ALU.mult), reads=[byb, bT], writes=[by_])
                        S.add("dve", lambda e: e.tensor_tensor(out=y_[0:rows, :], in0=y_[0:rows, :], in1=yb[0:rows, 0:256], op=ALU.add), reads=[byb], writes=[by_, byb])
                        S.add("act", lambda e: e.activation(out=tz[0:rows, :], in_=zb[0:rows, 0:256], func=AF.Tanh, scale=0.5), reads=[bzb], writes=[btz])
                        S.add("dve", lambda e: e.scalar_tensor_tensor(out=tz[0:rows, :], in0=tz[0:rows, :], scalar=1.0, in1=zb[0:rows, 0:256],
                                                                      op0=ALU.add, op1=ALU.mult), reads=[bzb], writes=[btz, bzb])
                        S.add("dve", lambda e: e.tensor_tensor(out=tz[0:rows, :], in0=tz[0:rows, :], in1=y_[0:rows, :], op=ALU.mult), reads=[by_], writes=[btz])
                        S.add("act", lambda e: e.activation(out=ynb[0:rows, :], in_=tz[0:rows, :], func=AF.Square, accum_out=gs[0:rows, c:c + 1]),
                              reads=[btz], writes=[bynb, bgs])
                        S.add("act", lambda e: e.copy(out=xtok[0:rows, c, :], in_=tz[0:rows, :]), reads=[btz], writes=[bxtk[c]])
                    chunk_items.append([s0, s1, s2])
                return chunk_items

            def g_post(g):
                S.add("act", lambda e: e.activation(out=gs[:, 17:34], in_=gs[:, 0:17], func=AF.Sqrt, scale=1.0 / 256, bias=4e-5), reads=[bgs], writes=[bgs])
                S.add("dve", lambda e: e.reciprocal(out=gs[:, 34:51], in_=gs[:, 17:34]), reads=[bgs], writes=[bgs])
                norm_items = []
                for c in range(17):
                    def n0(c=c):
                        t0, rows = TT[c]
                        yn_, byn_ = ynb2[c % 2], bynb2[c % 2]
                        S.add("dve", lambda e: e.scalar_tensor_tensor(out=yn_[0:rows, :], in0=xtok[0:rows, c, :], scalar=gs[0:rows, 34 + c:35 + c], in1=nrm[0:rows, :],
                                                                      op0=ALU.mult, op1=ALU.mult), reads=[bxtk[c], bgs, bnrm], writes=[byn_])

                    def n1(c=c, g=g):
                        t0, rows = TT[c]
                        yn_, byn_ = ynb2[c % 2], bynb2[c % 2]
                        nb = 3 if c % 2 == 0 else 2
                        pv = PS[nb][:].bitcast(BF16)
                        for k in range(2):
                            S.add("pe", lambda e, k=k: e.transpose(out=pv[:, k * 128:k * 128 + rows], in_=yn_[0:rows, k * 128:(k + 1) * 128],
                                                                  identity=IDb[0:rows, 0:rows]), reads=[byn_, bCb], writes=[PB[nb]])
                        S.add("act", lambda e: e.copy(out=ysT[:, 2 * g:2 * g + 2, t0:t0 + rows], in_=pv[:, 0:256].rearrange("p (k t) -> p k t", k=2)[:, :, 0:rows]),
                              reads=[PB[nb]], writes=[bys, PB[nb]])
                    norm_items.append([n0, n1])
                pipeline(norm_items)
                for k in range(2):
                    S.add("pe", lambda e, k=k: e.transpose(out=PS[2][:, k * 128:(k + 1) * 128], in_=STt[:, k * 128:(k + 1) * 128], identity=IDf),
                          reads=[bST, bC], writes=[PB[2]])
                S.add("act", lambda e: e.copy(out=stp[:], in_=PS[2][:, 0:256].rearrange("p (k n) -> p k n", k=2)), reads=[PB[2]], writes=[bstp, PB[2]])
                S.dma(lambda e, g=g: e.dma_start(out=ssm_p[4 * g:4 * g + 4, :, :].rearrange("(a b) p n -> (b p) a n", b=2), in_=stp[:]), reads=[bstp])
                S.add("dve", lambda e, g=g: e.tensor_scalar(out=dtaE[:].rearrange("p (h x) -> p h x", h=4),
                                                           in0=totb[0:32, 16, 4 * g:4 * g + 4].unsqueeze(2).to_broadcast([32, 4, 64]),
                                                           scalar1=0.125, scalar2=None, op0=ALU.mult), reads=[bT], writes=[bdtaE])
                for k in range(2):
                    S.add("pe", lambda e, k=k: e.matmul(PS[2][:, 256 + 4 * k:260 + 4 * k], lhsT=dtaE[:, k * 128:(k + 1) * 128], rhs=SEG,
                                                       start=True, stop=True), reads=[bdtaE, bC], writes=[PB[2]])
                S.add("act", lambda e: e.activation(out=dcol[:], in_=PS[2][:, 256:264].rearrange("p (k s) -> p k s", k=2), func=AF.Exp),
                      reads=[PB[2]], writes=[bdcol, PB[2]])
                for s in range(4):
                    S.dma(lambda e, g=g, s=s: e.dma_start(out=h0n[s % 2][:], in_=st_nat[s, 4 * g:4 * g + 4, :, :].rearrange("(a b) p n -> (b p) a n", b=2)),
                          writes=[bh0n[s % 2]])
                    S.add("dve", lambda e, s=s: e.tensor_tensor(out=wxm[:, :].rearrange("p (h x) -> p h x", h=4),
                                                               in0=xsr[:, :].rearrange("p (h x) -> p h x", h=4),
                                                               in1=dsg[:, s, :].unsqueeze(2).to_broadcast([32, 4, 64]), op=ALU.mult),
                          reads=[bxsr, bdsg], writes=[bwxm])
                    for k in range(2):
                        S.add("pe", lambda e, s=s, k=k: e.matmul(PS[6 + (s % 2)][:, k * 128:(k + 1) * 128],
                                                                lhsT=wxm[:, k * 128:(k + 1) * 128], rhs=btok[0:32, 16, :], start=True, stop=True),
                              reads=[bwxm, bbtok], writes=[PB[6 + (s % 2)]])
                    for k in range(2):
                        S.add("dve", lambda e, s=s, k=k: e.scalar_tensor_tensor(out=nst[s % 2][:, k, :], in0=h0n[s % 2][:, k, :], scalar=dcol[:, k, s:s + 1],
                                                                              in1=PS[6 + (s % 2)][:, k * 128:(k + 1) * 128], op0=ALU.mult, op1=ALU.add),
                              reads=[bh0n[s % 2], bdcol, PB[6 + (s % 2)]], writes=[bnst[s % 2], PB[6 + (s % 2)]])
                    S.dma(lambda e, g=g, s=s: e.dma_start(out=ssm_s[s, 4 * g:4 * g + 4, :, :].rearrange("(a b) p n -> (b p) a n", b=2), in_=nst[s % 2][:]), reads=[bnst[s % 2]])

            def interleave(a, b):
                out = []
                nb = len(b)
                for i, x in enumerate(a):
                    out.append(x)
                    if i < nb:
                        out.append(b[i])
                out.extend(b[len(a):])
                return out

            g_loadw(0)
            for g in range(8):
                g_pre(g)
                pipeline(g_conv_items(g, range(4)))
                g_mid(g)
                if g + 1 < 8:
                    g_loadw(g + 1)
                pipeline(g_chunk_items(g))
                g_post(g)
        S.barrier(scr)
        tabs.close()
        nwb[0] = 1

        build_attention_and_tail(nc, S, top, SB, PS, PB, XN, bXN, ysT, bys, load_w, win_seg, w_in_v, C, Cb, bC, bCb, scr, hp_bc, bhp,
                                 dict(x_all=x_all, w_a=w_a, w_b=w_b, w_o=w_o, final_g=final_g, tbraw=tbraw, fbraw=fbraw, fnraw=fnraw,
                                      cache_k=cache_k, cache_v=cache_v, y_out=y_out, k_out=k_out, v_out=v_out), TT, TB)
        S.emit(top)
    return nc


def build_attention_and_tail(nc, S, top, SB, PS, PB, XN, bXN, ysT, bys, load_w, win_seg, w_in_v, C, Cb, bC, bCb, scr, hp_bc, bhp, T, TT, TB):
    IDf = C[:, 0:128]; IDb = Cb[:, 0:128]; ONESb = Cb[:, 384:512]
    x_all, w_a, w_b, w_o, final_g = T["x_all"], T["w_a"], T["w_b"], T["w_o"], T["final_g"]
    k_out, v_out, y_out = T["k_out"], T["v_out"], T["y_out"]
    yaT = SB(top, "yaT", [128, 6, NT], BF16); bya = Buf()
    Qs = SB(top, "Qs", [32, 768], BF16); bQs = Buf()
    Ks = SB(top, "Ks", [32, 768], F32); bKs = Buf()
    Vs = SB(top, "Vs", [32, 768], BF16); bVs = Buf()

    with ExitStack() as p3:
        TM = SB(p3, "TM", [128, 2, 640], BF16); bTM = Buf()
        tstage = SB(p3, "tstage", [128, 2, 640], F32); bts = Buf()
        QT = SB(p3, "QT", [128, NT], BF16); bQT = Buf()
        KT = SB(p3, "KT", [128, NT], BF16); bKT = Buf()
        Kf = SB(p3, "Kf", [128, NT], F32); bKf = Buf()
        Vf = Kf; bVf = bKf
        VTb = SB(p3, "VTb", [128, NT], BF16); bVTb = Buf()
        SG = SB(p3, "SG", [128, NT], BF16); bSG = Buf()
        tg = SB(p3, "tg", [128, 512], F32); btg = Buf()
        Vx = [SB(p3, "Vx%d" % i, [128, 16, 2, 128], BF16) for i in range(3)]; bVx = [Buf(), Buf(), Buf()]
        ost = [SB(p3, "ost%d" % i, [128, 4, 128], F32) for i in range(1)] * 2; bost = [Buf()] * 2
        Pt = [SB(p3, "Pt%d" % i, [128, 512], BF16) for i in range(3)]; bPt = [Buf(), Buf(), Buf()]
        rec = SB(p3, "rec", [128, 512], F32); brec = Buf()
        tmpo = rec; btmpo = Buf()
        for i in range(3):
            S.add("pool", lambda e, i=i: e.memset(Vx[i][:], 1.0), writes=[bVx[i]])

        for hp in range(6):
            S.dma(lambda e, hp=hp: e.dma_start(out=tstage[:], in_=T["tbraw"][:, 2 * hp:2 * hp + 2, :]), writes=[bts])
            S.add("act", lambda e: e.copy(out=TM[:], in_=tstage[:]), reads=[bts], writes=[bTM])
            wt, bwt = load_w([win_seg(128 * hp, 128, 0), win_seg(768 + 128 * hp, 128, 128),
                              win_seg(1536 + 128 * hp, 128, 256), win_seg(2304 + 128 * hp, 128, 384)])
            def emit_kv_out(which, src, bsrc, dst, hp=hp):
                def rnd(c4):
                    kb = 2 if (which * 5 + c4) % 2 == 0 else 4
                    pk, bpk = PS[kb], PB[kb]
                    tiles = list(range(c4 * 4, min(c4 * 4 + 4, 17)))
                    o_, bo_ = ost[(which * 5 + c4) % 2], bost[(which * 5 + c4) % 2]
                    for j, ti in enumerate(tiles):
                        t0, rows = TT[ti]
                        S.add("pe", lambda e, j=j, t0=t0, rows=rows, src=src: e.transpose(out=pk[0:rows, j * 128:(j + 1) * 128], in_=src[:, t0:t0 + rows], identity=IDf),
                              reads=[bsrc, bC], writes=[bpk])
                    nt_ = len(tiles)
                    rws = 128 if tiles[-1] < 16 else 32
                    S.add("act", lambda e, o_=o_, nt_=nt_, rws=rws: e.copy(out=o_[0:rws, 0:nt_, :], in_=pk[0:rws, 0:nt_ * 128].rearrange("p (c x) -> p c x", x=128)),
                          reads=[bpk], writes=[bo_])
                    if which == 1 and rws == 128:
                        S.add("dve", lambda e, c4=c4: e.tensor_copy(out=Vx[0][:, c4 * 4:c4 * 4 + 4, 0, 0:64], in_=pk[:, :].rearrange("p (c x) -> p c x", x=128)[:, :, 0:64]),
                              reads=[bpk], writes=[bVx[0]])
                        S.add("dve", lambda e, c4=c4: e.tensor_copy(out=Vx[0][:, c4 * 4:c4 * 4 + 4, 1, 64:128], in_=pk[:, :].rearrange("p (c x) -> p c x", x=128)[:, :, 64:128]),
                              reads=[bpk], writes=[bVx[0], bpk])
                    if rws == 32:
                        tgt, btgt = (Ks, bKs) if which == 0 else (Vs, bVs)
                        S.add("dve", lambda e, tgt=tgt, hp=hp: e.tensor_copy(out=tgt[:, 128 * hp:128 * (hp + 1)], in_=pk[0:32, 0:128]), reads=[bpk], writes=[btgt, bpk])
                    S.add("dve", lambda e: e.memset(scr[0:1, 15:16], 0.0), reads=[bpk], writes=[bpk])
                    if rws == 128:
                        S.dma(lambda e, o_=o_, c4=c4, hp=hp, dst=dst: e.dma_start(
                            out=dst[c4 * 512:(c4 + 1) * 512, 128 * hp:128 * (hp + 1)].rearrange("(c p) x -> p c x", p=128), in_=o_[:, :, :]), reads=[bo_])
                    else:
                        S.dma(lambda e, o_=o_, hp=hp, dst=dst: e.dma_start(out=dst[LP:NT, 128 * hp:128 * (hp + 1)], in_=o_[0:32, 0, :]), reads=[bo_])
                for c4 in range(5):
                    rnd(c4)
            for seg in range(4):
                for bi, (t0, n) in enumerate(TB):
                    pb, bpb = PS[bi % 2], PB[bi % 2]
                    for kc in range(8):
                        S.add("pe", lambda e, pb=pb, kc=kc, t0=t0, n=n, seg=seg, wt=wt: e.matmul(
                            pb[:, 0:n], lhsT=wt[:, kc, seg * 128:(seg + 1) * 128], rhs=XN[:, kc, t0:t0 + n],
                            start=(kc == 0), stop=(kc == 7)), reads=[bXN, bwt], writes=[bpb])
                    if seg == 0:
                        S.add("act", lambda e, pb=pb, t0=t0, n=n: e.activation(out=QT[:, t0:t0 + n], in_=pb[:, 0:n], func=AF.Identity, scale=0.125),
                              reads=[bpb], writes=[bQT, bpb])
                    elif seg == 1:
                        S.add("act", lambda e, pb=pb, t0=t0, n=n: e.copy(out=KT[:, t0:t0 + n], in_=pb[:, 0:n]), reads=[bpb], writes=[bKT])
                        S.add("dve", lambda e, pb=pb, t0=t0, n=n: e.tensor_copy(out=Kf[:, t0:t0 + n], in_=pb[:, 0:n]), reads=[bpb], writes=[bKf, bpb])
                    elif seg == 2:
                        S.add("act", lambda e, pb=pb, t0=t0, n=n: e.copy(out=VTb[:, t0:t0 + n], in_=pb[:, 0:n]), reads=[bpb], writes=[bVTb])
                        S.add("dve", lambda e, pb=pb, t0=t0, n=n: e.tensor_copy(out=Vf[:, t0:t0 + n], in_=pb[:, 0:n]), reads=[bpb], writes=[bVf, bpb])
                    else:
                        S.add("act", lambda e, pb=pb, n=n: e.activation(out=tg[:, 0:n], in_=pb[:, 0:n], func=AF.Tanh, scale=0.5), reads=[bpb], writes=[btg])
                        S.add("dve", lambda e, pb=pb, t0=t0, n=n: e.scalar_tensor_tensor(out=SG[:, t0:t0 + n], in0=tg[:, 0:n], scalar=1.0, in1=pb[:, 0:n],
                                                                                       op0=ALU.add, op1=ALU.mult), reads=[btg, bpb], writes=[bSG, bpb])
                if seg == 1:
                    emit_kv_out(0, Kf, bKf, k_out)
                elif seg == 2:
                    emit_kv_out(1, Vf, bVf, v_out)
            pv = PS[3][:].bitcast(BF16)
            S.add("pe", lambda e, pv=pv: e.transpose(out=pv[0:32, 0:128], in_=QT[:, LP:NT], identity=IDb), reads=[bQT, bCb], writes=[PB[3]])
            S.add("act", lambda e, pv=pv, hp=hp: e.copy(out=Qs[:, 128 * hp:128 * (hp + 1)], in_=pv[0:32, 0:128]), reads=[PB[3]], writes=[bQs, PB[3]])
            for pi, (mod, vx, bvx) in enumerate(((4, Vx[1], bVx[1]), (16, Vx[2], bVx[2]))):
                for half in range(2):
                    vb = 3 if (pi * 2 + half) % 2 == 0 else 5
                    pv = PS[vb][:].bitcast(BF16)
                    bpv = PB[vb]
                    for j in range(8):
                        tl = half * 8 + j
                        if mod == 4:
                            r, cbk = tl // 4, tl % 4
                            src = VTb[:, r + 512 * cbk:r + 512 * cbk + 512:4]
                        else:
                            src = VTb[:, tl:LP:16]
                        S.add("pe", lambda e, pv=pv, j=j, src=src: e.transpose(out=pv[:, j * 128:(j + 1) * 128], in_=src, identity=IDb),
                              reads=[bVTb, bCb], writes=[bpv])
                    S.add("act", lambda e, pv=pv, vx=vx, half=half: e.copy(out=vx[:, half * 8:half * 8 + 8, 0, 0:64],
                                                                        in_=pv[:, :].rearrange("p (c x) -> p c x", x=128)[:, :, 0:64]), reads=[bpv], writes=[bvx])
                    S.add("dve", lambda e, pv=pv, vx=vx, half=half: e.tensor_copy(out=vx[:, half * 8:half * 8 + 8, 1, 64:128],
                                                                               in_=pv[:, :].rearrange("p (c x) -> p c x", x=128)[:, :, 64:128]),
                          reads=[bpv], writes=[bvx, bpv])
            bank_items = []
            bctr = [0]
            for hd in range(2):
                hs = slice(64 * hd, 64 * hd + 64)
                for R in range(4):
                    accb, baccb = PS[6 + ((hd * 4 + R) % 2)], PB[6 + ((hd * 4 + R) % 2)]
                    banks = []
                    for half in range(2):
                        s_mm, pv_mm = [], []
                        qb0 = 4 * R + 2 * half
                        lc = (qb0 - 4 * R) * 128
                        if qb0 > 0:
                            s_mm.append((0, 128, KT[hs, (qb0 - 1) * 128:qb0 * 128], QT[hs, qb0 * 128:(qb0 + 1) * 128]))
                            pv_mm.append((accb[:, lc:lc + 128], 0, 128, Vx[0][:, qb0 - 1, hd, :], bVx[0]))
                        s_mm.append((128, 256, KT[hs, qb0 * 128:(qb0 + 1) * 128], QT[hs, qb0 * 128:(qb0 + 2) * 128]))
                        pv_mm.append((accb[:, lc:lc + 256], 128, 256, Vx[0][:, qb0, hd, :], bVx[0]))
                        s_mm.append((384, 128, KT[hs, (qb0 + 1) * 128:(qb0 + 2) * 128], QT[hs, (qb0 + 1) * 128:(qb0 + 2) * 128]))
                        pv_mm.append((accb[:, lc + 128:lc + 256], 384, 128, Vx[0][:, qb0 + 1, hd, :], bVx[0]))
                        banks.append((s_mm, TM[:, hd, 0:256].unsqueeze(1).to_broadcast([128, 2, 256]), pv_mm, 256))
                    for half in range(2):
                        s_mm, pv_mm = [], []
                        for rl in range(2):
                            r = 2 * half + rl
                            qcols = QT[hs, r + 512 * R:r + 512 * R + 512:4]
                            oc = accb[:, r:512:4]
                            if R > 0:
                                s_mm.append((rl * 256, 128, KT[hs, r + 512 * (R - 1):r + 512 * R:4], qcols))
                                pv_mm.append((oc, rl * 256, 128, Vx[1][:, r * 4 + R - 1, hd, :], bVx[1]))
                            s_mm.append((rl * 256 + 128, 128, KT[hs, r + 512 * R:r + 512 * R + 512:4], qcols))
                            pv_mm.append((oc, rl * 256 + 128, 128, Vx[1][:, r * 4 + R, hd, :], bVx[1]))
                        banks.append((s_mm, TM[:, hd, 256:512].unsqueeze(1).to_broadcast([128, 2, 256]), pv_mm, 256))
                    s_mm, pv_mm = [], []
                    for r in range(16):
                        s_mm.append((r * 32, 32, KT[hs, r:LP:16], QT[hs, r + 512 * R:r + 512 * R + 512:16]))
                        pv_mm.append((accb[:, r:512:16], r * 32, 32, Vx[2][:, r, hd, :], bVx[2]))
                    banks.append((s_mm, TM[:, hd, 512 + 32 * R:544 + 32 * R].unsqueeze(1).to_broadcast([128, 16, 32]), pv_mm, 32))

                    for bk, (s_mm, mask_ap, pv_mm, bw_) in enumerate(banks):
                        i = bctr[0] % 3
                        bctr[0] += 1
                        meng = "dve" if bk == 4 else "pool"

                        def sA(i=i, s_mm=s_mm, mask_ap=mask_ap, bw_=bw_, meng=meng, bk=bk, hd=hd):
                            sb_, bsb_ = PS[3 + i], PB[3 + i]
                            p_, bp_ = Pt[i], bPt[i]
                            if bk < 4:
                                c0g = 0 if bk < 2 else 256
                                for half in range(2):
                                    S.add("pe", lambda e, half=half, c0g=c0g: e.matmul(sb_[:, 256 * half:256 * (half + 1)], lhsT=IDb, rhs=TM[:, hd, c0g:c0g + 256],
                                                                                    start=(half == 0), stop=False, skip_group_check=True),
                                          reads=[bTM, bCb], writes=[bsb_])
                                for (c0, n, lhsT, rhs) in s_mm:
                                    S.add("pe", lambda e, c0=c0, n=n, lhsT=lhsT, rhs=rhs: e.matmul(sb_[:, c0:c0 + n], lhsT=lhsT, rhs=rhs, start=False, stop=True,
                                                                                                skip_group_check=True), reads=[bKT, bQT], writes=[bsb_])
                                S.add("act", lambda e: e.activation(out=p_[:], in_=sb_[:], func=AF.Exp), reads=[bsb_], writes=[bp_, bsb_])
                            else:
                                S.add("pe", lambda e: e.matmul(sb_[:].rearrange("p (a b) -> p a b", b=bw_), lhsT=IDb, rhs=mask_ap,
                                                               start=True, stop=False, skip_group_check=True), reads=[bTM, bCb], writes=[bsb_])
                                for (c0, n, lhsT, rhs) in s_mm:
                                    S.add("pe", lambda e, c0=c0, n=n, lhsT=lhsT, rhs=rhs: e.matmul(sb_[:, c0:c0 + n], lhsT=lhsT, rhs=rhs, start=False, stop=True,
                                                                                                skip_group_check=True), reads=[bKT, bQT], writes=[bsb_])
                                S.add("act", lambda e: e.activation(out=p_[:], in_=sb_[:], func=AF.Exp), reads=[bsb_], writes=[bp_, bsb_])

                        def sB(i=i, pv_mm=pv_mm, bk=bk, accb=accb, baccb=baccb, hd=hd, R=R, hp=hp, last=(bk == len(banks) - 1)):
                            p_, bp_ = Pt[i], bPt[i]
                            for k, (oc, c0, n, lhsT, bl) in enumerate(pv_mm):
                                st = (bk == 0 and k == 0)
                                S.add("pe", lambda e, oc=oc, c0=c0, n=n, lhsT=lhsT, st=st: e.matmul(
                                    oc, lhsT=lhsT, rhs=p_[:, c0:c0 + n], start=st, stop=True, skip_group_check=True),
                                    reads=[bp_, bl], writes=[baccb])
                            if last:
                                ns = slice(64 * hd, 64 * hd + 64)
                                ds_ = slice(64 * (1 - hd), 64 * (1 - hd) + 64)
                                S.add("dve", lambda e: e.reciprocal(out=rec[ds_, :], in_=accb[ds_, :]), reads=[baccb], writes=[brec])
                                S.add("dve", lambda e: e.scalar_tensor_tensor(out=tmpo[ns, :], in0=accb[ns, :], scalar=0.5, in1=rec[ds_, :],
                                                                              op0=ALU.mult, op1=ALU.mult), reads=[baccb, brec], writes=[btmpo, baccb])
                                S.add("dve", lambda e: e.tensor_tensor(out=yaT[ns, hp, 512 * R:512 * (R + 1)], in0=tmpo[ns, :], in1=SG[ns, 512 * R:512 * (R + 1)], op=ALU.mult),
                                      reads=[btmpo, bSG], writes=[bya])
                        bank_items.append([sA, None, sB])
            pipeline(bank_items)
            S.add("pool", lambda e, hp=hp: e.tensor_copy(out=yaT[:, hp, LP:NT], in_=SG[:, LP:NT]), reads=[bSG], writes=[bya])
    S.barrier(scr)

    with ExitStack() as p4:
        FB = SB(p4, "FB", [128, 8, 36], F32); bFB = Buf()
        FN = SB(p4, "FN", [32, 32, 36], F32); bFN = Buf()
        S.dma(lambda e: e.dma_start(out=FB[:], in_=T["fbraw"].rearrange("p i g h -> p i (g h)")), writes=[bFB])
        S.dma(lambda e: e.dma_start(out=FN[:], in_=T["fnraw"].rearrange("p q g h -> p q (g h)")), writes=[bFN])
        S.add("act", lambda e: e.activation(out=FB[:], in_=FB[:], func=AF.Exp), reads=[bFB], writes=[bFB])
        S.add("act", lambda e: e.activation(out=FN[:], in_=FN[:], func=AF.Exp), reads=[bFN], writes=[bFN])
        SEL = SB(p4, "SEL", [32, 32, 128], BF16); bSEL = Buf()
        S.add("pool", lambda e: e.tensor_copy(out=SEL[:], in_=Cb[0:32, 0:32].unsqueeze(2).to_broadcast([32, 32, 128])), reads=[bCb], writes=[bSEL])
        K1 = [SB(p4, "K1_%d" % i, [128, 768], F32) for i in range(2)]; bK1 = [Buf(), Buf()]
        V1 = [SB(p4, "V1_%d" % i, [128, 768], BF16) for i in range(2)]; bV1 = [Buf(), Buf()]
        K2 = [SB(p4, "K2_%d" % i, [128, 768], F32) for i in range(2)]; bK2 = [Buf(), Buf()]
        V2 = [SB(p4, "V2_%d" % i, [128, 768], BF16) for i in range(2)]; bV2 = [Buf(), Buf()]
        K3 = [SB(p4, "K3_%d" % i, [128, 768], F32) for i in range(2)]; bK3 = [Buf(), Buf()]
        V3 = [SB(p4, "V3_%d" % i, [128, 768], BF16) for i in range(2)]; bV3 = [Buf(), Buf()]
        prod = SB(p4, "prod", [128, 3, 768], F32); bprod = Buf()
        Sc = [SB(p4, "Sc%d" % i, [128, 48], F32) for i in range(2)]; bSc = [Buf(), Buf()]
        Pc = [SB(p4, "Pc%d" % i, [128, 36], BF16) for i in range(2)]; bPc = [Buf(), Buf()]
        Pn = SB(p4, "Pn", [32, 36], F32); bPn = Buf()
        Pnb = [SB(p4, "Pnb%d" % i, [32, 12], BF16) for i in range(2)]; bPnb = [Buf(), Buf()]
        osb = SB(p4, "osb", [128, 12], F32); bosb = Buf()
        rcs = SB(p4, "rcs", [128, 12], F32); brcs = Buf()
        it = 0
        n3 = 0
        n2 = 0
        for s in range(4):
            k1, bk1, v1, bv1 = K1[s % 2], bK1[s % 2], V1[s % 2], bV1[s % 2]
            S.dma(lambda e, k1=k1, s=s: e.dma_start(out=k1[:], in_=T["cache_k"][s, 1920:2048, :]), writes=[bk1])
            S.dma(lambda e, v1=v1, s=s: e.dma_start(out=v1[:], in_=T["cache_v"][s, 1920:2048, :]), writes=[bv1], eng="pool")
            for r in range(4):
                k2, bk2, v2, bv2 = K2[n2 % 2], bK2[n2 % 2], V2[n2 % 2], bV2[n2 % 2]
                n2 += 1
                S.dma(lambda e, k2=k2, s=s, r=r: e.dma_start(out=k2[:], in_=T["cache_k"][s, 1536 + r:2048:4, :]), writes=[bk2])
                S.dma(lambda e, v2=v2, s=s, r=r: e.dma_start(out=v2[:], in_=T["cache_v"][s, 1536 + r:2048:4, :]), writes=[bv2], eng="pool")
                for i in (r, r + 4):
                    k3, bk3, v3, bv3 = K3[n3 % 2], bK3[n3 % 2], V3[n3 % 2], bV3[n3 % 2]
                    n3 += 1
                    S.dma(lambda e, k3=k3, s=s, i=i: e.dma_start(out=k3[:], in_=T["cache_k"][s, i:2048:16, :]), writes=[bk3])
                    S.dma(lambda e, v3=v3, s=s, i=i: e.dma_start(out=v3[:], in_=T["cache_v"][s, i:2048:16, :]), writes=[bv3], eng="pool")
                    col = s * 8 + i
                    sc_, bsc_ = Sc[it % 2], bSc[it % 2]
                    pc_, bpc_ = Pc[it % 2], bPc[it % 2]
                    pnb_, bpnb_ = Pnb[it % 2], bPnb[it % 2]
                    ob, bob = PS[2 + 2 * (it % 2)], PB[2 + 2 * (it % 2)]
                    db, bdb = PS[3 + 2 * (it % 2)], PB[3 + 2 * (it % 2)]
                    it += 1
                    ktiles = ((0, k1, bk1), (1, k2, bk2), (2, k3, bk3))
                    vtiles = ((0, v1, bv1), (1, v2, bv2), (2, v3, bv3))
                    S.add("pe", lambda e, col=col: e.matmul(PS[0][:, 0:512], lhsT=SEL[:, col, :], rhs=Qs[:, 0:512], start=True, stop=True), reads=[bSEL, bQs], writes=[PB[0]])
                    S.add("pe", lambda e, col=col: e.matmul(PS[1][:, 0:256], lhsT=SEL[:, col, :], rhs=Qs[:, 512:768], start=True, stop=True), reads=[bSEL, bQs], writes=[PB[1]])
                    for (gi, kt_, bkt_) in ktiles:
                        S.add("dve", lambda e, gi=gi, kt_=kt_: e.tensor_tensor(out=prod[:, gi, 0:512], in0=kt_[:, 0:512], in1=PS[0][:, 0:512], op=ALU.mult),
                              reads=[bkt_, PB[0]], writes=[bprod])
                        S.add("dve", lambda e, gi=gi, kt_=kt_: e.tensor_tensor(out=prod[:, gi, 512:768], in0=kt_[:, 512:768], in1=PS[1][:, 0:256], op=ALU.mult),
                              reads=[bkt_, PB[1]], writes=[bprod])
                    S.add("dve", lambda e, sc_=sc_: e.tensor_reduce(out=sc_[:, 0:36], in_=prod[:].rearrange("p g (h x) -> p (g h) x", x=64), axis=mybir.AxisListType.X, op=ALU.add),
                          reads=[bprod], writes=[bsc_])
                    S.add("dve", lambda e: e.tensor_tensor(out=prod[0:32, 0, 0:512], in0=Ks[:, 0:512], in1=PS[0][0:32, 0:512], op=ALU.mult), reads=[bKs, PB[0]], writes=[bprod, PB[0]])
                    S.add("dve", lambda e: e.tensor_tensor(out=prod[0:32, 0, 512:768], in0=Ks[:, 512:768], in1=PS[1][0:32, 0:256], op=ALU.mult), reads=[bKs, PB[1]], writes=[bprod, PB[1]])
                    S.add("dve", lambda e, sc_=sc_: e.tensor_reduce(out=sc_[0:32, 36:48], in_=prod[0:32, 0, :].rearrange("p (h x) -> p h x", x=64), axis=mybir.AxisListType.X, op=ALU.add),
                          reads=[bprod], writes=[bsc_])
                    S.add("act", lambda e, sc_=sc_: e.activation(out=sc_[:], in_=sc_[:], func=AF.Exp), reads=[bsc_], writes=[bsc_])
                    S.add("pool", lambda e, sc_=sc_, pc_=pc_, i=i: e.tensor_tensor(out=pc_[:], in0=sc_[:, 0:36], in1=FB[:, i, :], op=ALU.mult), reads=[bsc_, bFB], writes=[bpc_])
                    S.add("pool", lambda e, sc_=sc_, col=col: e.tensor_tensor(out=Pn[:].rearrange("p (g h) -> p g h", g=3), in0=sc_[0:32, 36:48].unsqueeze(1).to_broadcast([32, 3, 12]),
                                                                            in1=FN[:, col, :].rearrange("p (g h) -> p g h", g=3), op=ALU.mult), reads=[bsc_, bFN], writes=[bPn])
                    S.add("pool", lambda e: e.tensor_tensor(out=Pn[:, 0:12], in0=Pn[:, 0:12], in1=Pn[:, 12:24], op=ALU.add), reads=[], writes=[bPn])
                    S.add("pool", lambda e, pnb_=pnb_: e.tensor_tensor(out=pnb_[:], in0=Pn[:, 0:12], in1=Pn[:, 24:36], op=ALU.add), reads=[bPn], writes=[bpnb_])
                    for hp in range(6):
                        for (gi, vt_, bvt_) in vtiles:
                            S.add("pe", lambda e, ob=ob, vt_=vt_, pc_=pc_, hp=hp, gi=gi: e.matmul(ob[:, 2 * hp:2 * hp + 2], lhsT=vt_[:, 128 * hp:128 * (hp + 1)],
                                                                                          rhs=pc_[:, gi * 12 + 2 * hp:gi * 12 + 2 * hp + 2], start=(gi == 0), stop=False),
                                  reads=[bvt_, bpc_], writes=[bob])
                        S.add("pe", lambda e, ob=ob, pnb_=pnb_, hp=hp: e.matmul(ob[:, 2 * hp:2 * hp + 2], lhsT=Vs[:, 128 * hp:128 * (hp + 1)], rhs=pnb_[:, 2 * hp:2 * hp + 2],
                                                                             start=False, stop=True), reads=[bVs, bpnb_], writes=[bob])
                    for gi in range(3):
                        S.add("pe", lambda e, db=db, pc_=pc_, gi=gi: e.matmul(db[:, 0:12], lhsT=ONESb, rhs=pc_[:, gi * 12:gi * 12 + 12], start=(gi == 0), stop=False),
                              reads=[bCb, bpc_], writes=[bdb])
                    S.add("pe", lambda e, db=db, pnb_=pnb_: e.matmul(db[:, 0:12], lhsT=ONESb[0:32, :], rhs=pnb_[:], start=False, stop=True), reads=[bCb, bpnb_], writes=[bdb])
                    S.add("dve", lambda e, db=db: e.reciprocal(out=rcs[:], in_=db[:, 0:12]), reads=[bdb], writes=[brcs, bdb])
                    S.add("dve", lambda e, ob=ob: e.scalar_tensor_tensor(out=osb[:], in0=ob[:, 0:12], scalar=0.5, in1=rcs[:], op0=ALU.mult, op1=ALU.mult),
                          reads=[bob, brcs], writes=[bosb, bob])
                    cc = LP + col
                    S.add("dve", lambda e, cc=cc: e.tensor_tensor(out=yaT[0:64, :, cc:cc + 1], in0=yaT[0:64, :, cc:cc + 1], in1=osb[0:64, 0:12:2].unsqueeze(2), op=ALU.mult),
                          reads=[bosb], writes=[bya])
                    S.add("dve", lambda e, cc=cc: e.tensor_tensor(out=yaT[64:128, :, cc:cc + 1], in0=yaT[64:128, :, cc:cc + 1], in1=osb[64:128, 1:12:2].unsqueeze(2), op=ALU.mult),
                          reads=[bosb], writes=[bya])
    S.barrier(scr)

    with ExitStack() as p5:
        mT = SB(p5, "mT", [128, 8, NT], BF16); bmT = Buf()
        p5a = ExitStack()
        wd = [SB(p5a, "wd%d" % i, [128, 38, 128], BF16) for i in range(2)]; bwd = [Buf(), Buf()]
        sa = SB(p5a, "sa", [128, 512], F32); bsa = Buf()
        sbb = SB(p5a, "sbb", [128, 512], F32); bsbb = Buf()
        m1 = SB(p5a, "m1", [128, 512], F32); bm1 = Buf()
        m2 = SB(p5a, "m2", [128, 512], F32); bm2 = Buf()
        w_a_v = w_a.rearrange("(kc p) n -> p kc n", p=128)
        w_b_v = w_b.rearrange("(kc p) n -> p kc n", p=128)
        w_o_v = w_o.rearrange("(kc p) n -> p kc n", p=128)
        for dti in range(8):
            w_, bw_ = wd[dti % 2], bwd[dti % 2]
            cs = slice(128 * dti, 128 * (dti + 1))
            S.dma(lambda e, w_=w_, cs=cs: e.dma_start(out=w_[:, 0:6, :], in_=w_a_v[:, :, cs]), writes=[bw_], eng="pool")
            S.dma(lambda e, w_=w_, cs=cs: e.dma_start(out=w_[:, 6:22, :], in_=w_b_v[:, :, cs]), writes=[bw_], eng="pool")
            S.dma(lambda e, w_=w_, dti=dti: e.dma_start(out=w_[:, 22:30, :], in_=w_in_v[:, :, 9248 + 128 * dti:9248 + 128 * (dti + 1)]), writes=[bw_], eng="pool")
            S.dma(lambda e, w_=w_, dti=dti: e.dma_start(out=w_[:, 30:38, :], in_=w_in_v[:, :, 10272 + 128 * dti:10272 + 128 * (dti + 1)]), writes=[bw_], eng="pool")
            for bi, (t0, n) in enumerate(TB):
                o4 = 4 * (bi % 2)
                pA, pBk, pga, pgb = PS[o4], PS[o4 + 1], PS[o4 + 2], PS[o4 + 3]
                bA, bBk, bga, bgb = PB[o4], PB[o4 + 1], PB[o4 + 2], PB[o4 + 3]
                for kc in range(6):
                    S.add("pe", lambda e, pA=pA, w_=w_, kc=kc, t0=t0, n=n: e.matmul(pA[:, 0:n], lhsT=w_[:, kc, :], rhs=yaT[:, kc, t0:t0 + n], start=(kc == 0), stop=(kc == 5)),
                          reads=[bw_, bya], writes=[bA])
                for kc in range(16):
                    S.add("pe", lambda e, pBk=pBk, w_=w_, kc=kc, t0=t0, n=n: e.matmul(pBk[:, 0:n], lhsT=w_[:, 6 + kc, :], rhs=ysT[:, kc, t0:t0 + n], start=(kc == 0), stop=(kc == 15)),
                          reads=[bw_, bys], writes=[bBk])
                for kc in range(8):
                    S.add("pe", lambda e, pga=pga, w_=w_, kc=kc, t0=t0, n=n: e.matmul(pga[:, 0:n], lhsT=w_[:, 22 + kc, :], rhs=XN[:, kc, t0:t0 + n], start=(kc == 0), stop=(kc == 7)),
                          reads=[bw_, bXN], writes=[bga])
                for kc in range(8):
                    S.add("pe", lambda e, pgb=pgb, w_=w_, kc=kc, t0=t0, n=n: e.matmul(pgb[:, 0:n], lhsT=w_[:, 30 + kc, :], rhs=XN[:, kc, t0:t0 + n], start=(kc == 0), stop=(kc == 7)),
                          reads=[bw_, bXN], writes=[bgb])
                S.add("act", lambda e, pga=pga, n=n: e.activation(out=sa[:, 0:n], in_=pga[:, 0:n], func=AF.Tanh, scale=0.5), reads=[bga], writes=[bsa, bga])
                S.add("act", lambda e, pgb=pgb, n=n: e.activation(out=sbb[:, 0:n], in_=pgb[:, 0:n], func=AF.Tanh, scale=0.5), reads=[bgb], writes=[bsbb, bgb])
                S.add("dve", lambda e, pA=pA, n=n: e.scalar_tensor_tensor(out=m1[:, 0:n], in0=sa[:, 0:n], scalar=1.0, in1=pA[:, 0:n], op0=ALU.add, op1=ALU.mult),
                      reads=[bsa, bA], writes=[bm1, bA])
                S.add("dve", lambda e, pBk=pBk, n=n: e.scalar_tensor_tensor(out=m2[:, 0:n], in0=sbb[:, 0:n], scalar=1.0, in1=pBk[:, 0:n], op0=ALU.add, op1=ALU.mult),
                      reads=[bsbb, bBk], writes=[bm2, bBk])
                S.add("dve", lambda e, n=n: e.tensor_tensor(out=m1[:, 0:n], in0=m1[:, 0:n], in1=m2[:, 0:n], op=ALU.add), reads=[bm2], writes=[bm1])
                S.add("act", lambda e, dti=dti, t0=t0, n=n: e.activation(out=mT[:, dti, t0:t0 + n], in_=m1[:, 0:n], func=AF.Identity, scale=0.5),
                      reads=[bm1], writes=[bmT])
        S.barrier(scr)
        p5a.close()
        wo = SB(p5, "wo", [128, 8, D], BF16); bwo = Buf()
        S.dma(lambda e: e.dma_start(out=wo[:, :, 0:512], in_=w_o_v[:, :, 0:512]), writes=[bwo], eng="pool")
        S.dma(lambda e: e.dma_start(out=wo[:, :, 512:1024], in_=w_o_v[:, :, 512:1024]), writes=[bwo], eng="pool")
        fg = SB(p5, "fg", [128, D], F32); bfg = Buf()
        S.dma(lambda e: e.dma_start(out=fg[:], in_=final_g.to_broadcast([128, D])), writes=[bfg])
        xr = [SB(p5, "xr%d" % i, [128, D], F32) for i in range(2)]; bxr = [Buf(), Buf()]
        yo = xr; byo = bxr
        jk2 = SB(p5, "jk2", [128, D], BF16); bjk2 = Buf()
        fs = SB(p5, "fs", [128, 4], F32); bfs = Buf()
        for ti, (t0, rows) in enumerate(TT):
            x_, bx_ = xr[ti % 2], bxr[ti % 2]
            y_, by_ = yo[ti % 2], byo[ti % 2]
            S.dma(lambda e, x_=x_, t0=t0, rows=rows: e.dma_start(out=x_[0:rows, :], in_=x_all[t0:t0 + rows, :]), writes=[bx_])
            for hf in range(2):
                pb, bpb = PS[2 * (ti % 2) + hf], PB[2 * (ti % 2) + hf]
                for kc in range(8):
                    S.add("pe", lambda e, pb=pb, kc=kc, t0=t0, rows=rows, hf=hf: e.matmul(pb[0:rows, :], lhsT=mT[:, kc, t0:t0 + rows], rhs=wo[:, kc, 512 * hf:512 * (hf + 1)],
                                                                                      start=(kc == 0), stop=(kc == 7)), reads=[bmT, bwo], writes=[bpb])
                S.add("dve", lambda e, pb=pb, x_=x_, hf=hf, rows=rows: e.tensor_tensor(out=x_[0:rows, 512 * hf:512 * (hf + 1)], in0=x_[0:rows, 512 * hf:512 * (hf + 1)], in1=pb[0:rows, :], op=ALU.add),
                      reads=[bpb], writes=[bx_, bpb])
            S.add("dve", lambda e: e.memset(fs[:, 0:1], 0.0), writes=[bfs])
            S.add("act", lambda e, x_=x_, rows=rows: e.activation(out=jk2[0:rows, :], in_=x_[0:rows, :], func=AF.Square, accum_out=fs[0:rows, 0:1]), reads=[bx_], writes=[bjk2, bfs])
            S.add("act", lambda e, rows=rows: e.activation(out=fs[0:rows, 1:2], in_=fs[0:rows, 0:1], func=AF.Sqrt, scale=1.0 / D, bias=1e-6), reads=[bfs], writes=[bfs])
            S.add("dve", lambda e, rows=rows: e.reciprocal(out=fs[0:rows, 2:3], in_=fs[0:rows, 1:2]), reads=[bfs], writes=[bfs])
            S.add("dve", lambda e, x_=x_, y_=y_, rows=rows: e.scalar_tensor_tensor(out=y_[0:rows, :], in0=x_[0:rows, :], scalar=fs[0:rows, 2:3], in1=fg[0:rows, :], op0=ALU.mult, op1=ALU.mult),
                  reads=[bx_, bfs, bfg], writes=[by_])
            S.dma(lambda e, y_=y_, t0=t0, rows=rows: e.dma_start(out=y_out[t0:t0 + rows, :], in_=y_[0:rows, :]), reads=[by_])


_PROG = None


def kernel(x_prompt, x_sample, cache_k, cache_v, state_conv, state_ssm,
           norm_g, w_in, conv_w, conv_b, dt_bias, a_log, d_skip, ssm_norm,
           w_branch_a, w_branch_b, w_out, rel_bias, final_norm):
    global _PROG
    f = np.float32
    asf = lambda a: np.ascontiguousarray(np.asarray(a, dtype=f))
    x_prompt, x_sample = asf(x_prompt), asf(x_sample)
    cache_k, cache_v = np.asarray(cache_k, dtype=f), np.asarray(cache_v, dtype=f)
    state_conv, state_ssm = asf(state_conv), asf(state_ssm)
    rel_bias = asf(rel_bias)
    tb_i, fb_i, fn_i = _static_index_tables()
    rb_ext = np.concatenate([rel_bias, np.full((1, 12), NEG, f)], axis=0)
    tbraw = np.ascontiguousarray(rb_ext[tb_i].transpose(0, 2, 1))
    fbraw = np.ascontiguousarray(rb_ext[fb_i])
    fnraw = np.ascontiguousarray(rb_ext[fn_i])
    cst = _const_pack()
    cwT = np.ascontiguousarray(asf(conv_w)[0].reshape(4, 32, 128).transpose(2, 1, 0))
    cbT = np.ascontiguousarray(asf(conv_b)[0].reshape(32, 128).T)
    hpar = np.concatenate([asf(dt_bias)[0], asf(a_log)[0], asf(d_skip)[0]])[None, :]
    common = dict(w_in=asf(w_in)[0], w_a=asf(w_branch_a)[0], w_b=asf(w_branch_b)[0], w_o=asf(w_out)[0],
                  norm_g=asf(norm_g), final_g=asf(final_norm)[None, :], ssm_norm=asf(ssm_norm),
                  cwT=cwT, cbT=cbT, cb_row=asf(conv_b), hpar=np.ascontiguousarray(hpar), tbraw=tbraw, fbraw=fbraw, fnraw=fnraw, cst=cst)
    in_maps = []
    for c in range(NCORES):
        sl = slice(4 * c, 4 * c + 4)
        m = dict(common)
        m["x_all"] = np.ascontiguousarray(np.concatenate([x_prompt[c], x_sample[sl].reshape(NS, D)], axis=0))
        m["cache_k"] = np.ascontiguousarray(cache_k[0, sl].reshape(4, 2048, 768))
        m["cache_v"] = np.ascontiguousarray(cache_v[0, sl].reshape(4, 2048, 768))
        sc = state_conv[0, sl]
        m["scT"] = np.ascontiguousarray(sc.reshape(4, 3, 32, 128).transpose(3, 2, 0, 1))
        st = state_ssm[0, sl]
        m["st_nat"] = np.ascontiguousarray(st)
        m["st_T"] = np.ascontiguousarray(st.reshape(4, 2048, 128).transpose(2, 0, 1))
        in_maps.append(m)
    if _PROG is None:
        _PROG = build_program()
    res = run_bass_kernel_spmd(_PROG, in_maps, core_ids=list(range(NCORES)))
    R = res.results
    y_p = np.stack([R[c]["y_out"][:LP] for c in range(NCORES)])
    y_s = np.concatenate([R[c]["y_out"][LP:].reshape(4, 8, D) for c in range(NCORES)])
    k_p = np.stack([R[c]["k_out"][:LP].reshape(LP, 12, 64) for c in range(NCORES)])[None]
    v_p = np.stack([R[c]["v_out"][:LP].reshape(LP, 12, 64) for c in range(NCORES)])[None]
    k_s = np.concatenate([R[c]["k_out"][LP:].reshape(4, 8, 12, 64) for c in range(NCORES)])[None]
    v_s = np.concatenate([R[c]["v_out"][LP:].reshape(4, 8, 12, 64) for c in range(NCORES)])[None]
    c_p = np.stack([R[c]["conv_out"][0:3] for c in range(NCORES)])[None]
    c_s = np.concatenate([R[c]["conv_out"][3:15].reshape(4, 3, 4096) for c in range(NCORES)])[None]
    s_p = np.stack([R[c]["ssm_p"] for c in range(NCORES)])[None]
    s_s = np.concatenate([R[c]["ssm_s"] for c in range(NCORES)])[None]
    outs = (y_p, y_s, k_p, v_p, c_p, s_p, k_s, v_s, c_s, s_s)
    return tuple(np.ascontiguousarray(o.astype(np.float32)) for o in outs)
```

```python
import math
import numpy as np
from contextlib import ExitStack
import concourse.bass as bass
import concourse.mybir as mybir
from concourse.bass_utils import run_bass_kernel_spmd

F32 = mybir.dt.float32
BF16 = mybir.dt.bfloat16
ALU = mybir.AluOpType
AF = mybir.ActivationFunctionType

NCORES = 8
D = 1024
LP = 2048
NS = 32
NT = LP + NS
NIN = 11296
NEG = -30000.0


class Buf:
    __slots__ = ("name", "w", "r", "excl")

    def __init__(self, name="", excl=False):
        self.name = name
        self.w = None
        self.r = []
        self.excl = excl


class Op:
    __slots__ = ("eng", "fn", "deps", "hard", "idx", "signal", "token", "is_dma", "prev_token")

    def __init__(self, eng, fn, idx, is_dma):
        self.eng = eng
        self.fn = fn
        self.idx = idx
        self.deps = set()
        self.hard = set()
        self.signal = False
        self.token = None
        self.is_dma = is_dma
        self.prev_token = None


class Sched:
    ENGS = ("pe", "act", "dve", "pool", "sp")

    def __init__(self, nc, n_dma_sems=40):
        self.nc = nc
        self.ops = []
        self.n_dma_sems = n_dma_sems
        self.dma_since_barrier = []

    def add(self, eng, fn, reads=(), writes=(), is_dma=False):
        op = Op(eng, fn, len(self.ops), is_dma)
        self.ops.append(op)
        writes = list(writes) + [b for b in reads if b.excl]
        wset = set(id(b) for b in writes)
        for b in reads:
            if id(b) in wset:
                continue
            if b.w is not None:
                op.deps.add(b.w)
                op.hard.add(b.w)
            b.r.append(op.idx)
        done = set()
        for b in writes:
            if id(b) in done:
                continue
            done.add(id(b))
            if b.w is not None:
                op.deps.add(b.w)
                op.hard.add(b.w)
            op.deps.update(b.r)
            b.w = op.idx
            b.r = []
        op.deps.discard(op.idx)
        op.hard.discard(op.idx)
        if is_dma:
            self.dma_since_barrier.append(op.idx)
        return op

    def dma(self, fn, reads=(), writes=(), eng="sp"):
        return self.add(eng, fn, reads, writes, is_dma=True)

    def barrier(self, scratch):
        bars = {}
        firsts = []
        for i, e in enumerate(self.ENGS):
            b = Buf("bar")
            bars[e] = b
            if e in ("pe", "sp"):
                fn = lambda en: en.nop()
            elif e == "act":
                fn = (lambda en, i=i: en.copy(out=scratch[0:1, i:i + 1], in_=scratch[0:1, i:i + 1]))
            else:
                fn = (lambda en, i=i: en.memset(scratch[0:1, i:i + 1], 0.0))
            op = self.add(e, fn, writes=[b])
            firsts.append(op)
        for op in firsts:
            op.deps.update(self.dma_since_barrier)
            op.deps.discard(op.idx)
        self.dma_since_barrier = []
        for i, e in enumerate(self.ENGS):
            if e in ("pe", "sp"):
                fn = lambda en: en.nop()
            elif e == "act":
                fn = (lambda en, i=i: en.copy(out=scratch[0:1, 8 + i:9 + i], in_=scratch[0:1, 8 + i:9 + i]))
            else:
                fn = (lambda en, i=i: en.memset(scratch[0:1, 8 + i:9 + i], 0.0))
            self.add(e, fn, reads=list(bars.values()))

    def emit(self, stack):
        nc = self.nc
        ops = self.ops
        def needs(op, dop):
            if dop.is_dma or dop.eng != op.eng:
                return True
            if op.eng in ("pe", "sp"):
                return False
            return dop.idx in op.hard

        for op in ops:
            for d in op.deps:
                dop = ops[d]
                if needs(op, dop):
                    dop.signal = True
        esem = {e: stack.enter_context(nc.semaphore("s_" + e)) for e in self.ENGS}
        dsem = [stack.enter_context(nc.semaphore("d%d" % i)) for i in range(self.n_dma_sems)]
        cnt = {e: 0 for e in self.ENGS}
        duse = [0] * self.n_dma_sems
        ndma = 0
        for op in ops:
            if op.is_dma:
                k = ndma % self.n_dma_sems
                ndma += 1
                if duse[k] > 0:
                    op.prev_token = (dsem[k], 16 * duse[k])
                duse[k] += 1
                op.token = (dsem[k], 16 * duse[k])
            elif op.signal:
                cnt[op.eng] += 1
                op.token = (esem[op.eng], cnt[op.eng])
        per_eng = {e: [op for op in ops if op.eng == e] for e in self.ENGS}
        final_dma = [(dsem[k], 16 * duse[k]) for k in range(self.n_dma_sems) if duse[k] > 0]

        def run(ename, e):
            waited = {}
            for op in per_eng[ename]:
                need = {}
                for d in op.deps:
                    dop = ops[d]
                    if not needs(op, dop):
                        continue
                    s, v = dop.token
                    if need.get(s.num, (None, 0))[1] < v:
                        need[s.num] = (s, v)
                if op.prev_token is not None:
                    s, v = op.prev_token
                    if need.get(s.num, (None, 0))[1] < v:
                        need[s.num] = (s, v)
                for sn, (s, v) in need.items():
                    if waited.get(sn, 0) < v:
                        e.wait_ge(s, v)
                        waited[sn] = v
                ins = op.fn(e)
                if op.is_dma:
                    ins.then_inc(op.token[0], 16)
                elif op.signal:
                    ins.then_inc(op.token[0], 1)
            if ename == "sp":
                for s, v in final_dma:
                    if waited.get(s.num, 0) < v:
                        e.wait_ge(s, v)

        block = stack.enter_context(nc.Block())

        @block.tensor
        def _(e):
            run("pe", e)

        @block.scalar
        def _(e):
            run("act", e)

        @block.vector
        def _(e):
            run("dve", e)

        @block.gpsimd
        def _(e):
            run("pool", e)

        @block.sync
        def _(e):
            run("sp", e)


def pipeline(stage_lists):
    n = len(stage_lists)
    ns = max(len(x) for x in stage_lists) if n else 0
    for t in range(n + ns - 1):
        for k in reversed(range(ns)):
            i = t - k
            if 0 <= i < n and k < len(stage_lists[i]) and stage_lists[i][k] is not None:
                stage_lists[i][k]()


def _t5_bucket(dist):
    n_buckets, max_distance = 32, 2048
    max_exact = n_buckets // 2
    d = np.maximum(dist, 1).astype(np.float32)
    large = max_exact + (np.log(d / max_exact) / math.log(max_distance / max_exact)
                         * (n_buckets - max_exact)).astype(np.int32)
    large = np.minimum(large, n_buckets - 1)
    return np.where(dist < max_exact, dist, large).astype(np.int32)


GROUP_D = (1, 4, 16)


def _static_index_tables():
    bk = [_t5_bucket(np.arange(0, 129, dtype=np.int32) * d) for d in GROUP_D]
    k = np.arange(128)[:, None]
    q = np.arange(128)[None, :]
    tb = np.full((128, 640), 32, np.int32)
    for g in range(2):
        dist = q + 128 - k
        tb[:, 256 * g:256 * g + 128] = np.where(dist <= 128, bk[g][np.clip(dist, 0, 128)], 32)
        dist = q - k
        tb[:, 256 * g + 128:256 * g + 256] = np.where(dist >= 0, bk[g][np.clip(dist, 0, 128)], 32)
    dist = q - k
    tb[:, 512:640] = np.where(dist >= 0, bk[2][np.clip(dist, 0, 128)], 32)
    fb = np.full((128, 8, 3), 32, np.int32)
    m = np.arange(128)
    for i in range(8):
        fb[:, i, 2] = bk[2][128 - m]
        j = 128 + (i // 4) - m
        fb[:, i, 1] = np.where((j >= 1) & (j <= 128), bk[1][np.clip(j, 0, 128)], 32)
        j = 128 + i - m
        fb[:, i, 0] = np.where((j >= 1) & (j <= 128), bk[0][np.clip(j, 0, 128)], 32)
    fn = np.full((32, 32, 3), 32, np.int32)
    for s in range(4):
        for i in range(8):
            for ip in range(i + 1):
                dlt = i - ip
                p = s * 8 + ip
                fn[p, s * 8 + i, 0] = bk[0][dlt]
                if dlt % 4 == 0:
                    fn[p, s * 8 + i, 1] = bk[1][dlt // 4]
                if dlt == 0:
                    fn[p, s * 8 + i, 2] = bk[2][0]
    return tb, fb, fn


def _const_pack():
    c = np.zeros((128, 776), np.float32)
    c[:, 0:128] = np.eye(128)
    s = np.arange(128)[:, None]
    l = np.arange(128)[None, :]
    c[:, 128:256] = (s <= l)
    same = (s // 8 == l // 8) & (s < 32) & (l < 32)
    c[:, 256:384] = same & (s <= l)
    c[:, 384:512] = 1.0
    c[:, 512:640] = same
    for sq in range(4):
        c[sq * 8:(sq + 1) * 8, 640 + sq] = 1.0
    for sq in range(4):
        c[:, 648 + sq * 32 + sq * 8: 648 + sq * 32 + sq * 8 + 8] = 1.0
    return c


def build_program():
    nc = bass.Bass("TRN2", target_bir_lowering=False)

    def din(name, shape, dt=F32):
        return nc.dram_tensor(name, list(shape), dt, kind="ExternalInput").ap()

    def dout(name, shape):
        return nc.dram_tensor(name, list(shape), F32, kind="ExternalOutput").ap()

    x_all = din("x_all", [NT, D])
    w_in = din("w_in", [D, NIN])
    w_a = din("w_a", [768, D])
    w_b = din("w_b", [2048, D])
    w_o = din("w_o", [D, D])
    norm_g = din("norm_g", [1, D])
    final_g = din("final_g", [1, D])
    ssm_norm = din("ssm_norm", [1, 2048])
    cwT = din("cwT", [128, 32, 4])
    cbT = din("cbT", [128, 32])
    cb_row = din("cb_row", [1, 4096])
    hpar = din("hpar", [1, 96])
    tbraw = din("tbraw", [128, 12, 640])
    fbraw = din("fbraw", [128, 8, 3, 12])
    fnraw = din("fnraw", [32, 32, 3, 12])
    cst = din("cst", [128, 776])
    cache_k = din("cache_k", [4, 2048, 768])
    cache_v = din("cache_v", [4, 2048, 768])
    scT = din("scT", [128, 32, 4, 3])
    st_nat = din("st_nat", [4, 32, 64, 128])
    st_T = din("st_T", [128, 4, 2048])

    y_out = dout("y_out", [NT, D])
    k_out = dout("k_out", [NT, 768])
    v_out = dout("v_out", [NT, 768])
    conv_out = dout("conv_out", [15, 4096])
    ssm_p = dout("ssm_p", [32, 64, 128])
    ssm_s = dout("ssm_s", [4, 32, 64, 128])
    acT_d = nc.dram_tensor("acT_d", [32, NT], F32, kind="Internal").ap()

    w_in_v = w_in.rearrange("(kc p) n -> p kc n", p=128)

    TT = [(i * 128, 128) for i in range(16)] + [(LP, NS)]
    TB = [(i * 512, 512) for i in range(4)] + [(LP, NS)]

    with ExitStack() as top:
        S = Sched(nc)

        def SB(st, name, shape, dt):
            return st.enter_context(nc.sbuf_tensor(name, list(shape), dt))

        PS = [top.enter_context(nc.psum_tensor("ps%d" % i, [128, 512], F32)) for i in range(8)]
        PB = [Buf("ps%d" % i, excl=True) for i in range(8)]

        XN = SB(top, "XN", [128, 8, NT], BF16); bXN = Buf()
        ysT = SB(top, "ysT", [128, 16, NT], BF16); bys = Buf()
        tabs = ExitStack()
        wbuf = [SB(top, "wbuf0", [128, 8, 512], BF16)]
        bw = [Buf(), Buf()]
        nwb = [2]
        C = SB(top, "C", [128, 776], F32); bC = Buf()
        Cb = SB(top, "Cb", [128, 776], BF16); bCb = Buf()
        scr = SB(top, "scr", [128, 16], F32)
        hp_bc = SB(top, "hp_bc", [128, 96], F32); bhp = Buf()
        IDf = C[:, 0:128]; TRIp = C[:, 128:256]; TRIs = C[:, 256:384]; ONESf = C[:, 384:512]; ONESs = C[:, 512:640]
        SEG = C[0:32, 640:644]
        IDb = Cb[:, 0:128]; TRIpb = Cb[:, 128:256]; TRIsb = Cb[:, 256:384]; ONESb = Cb[:, 384:512]
        SEGXb = Cb[:, 648:776]

        S.dma(lambda e: e.dma_start(out=C[:], in_=cst), writes=[bC])
        S.add("dve", lambda e: e.tensor_copy(out=Cb[:], in_=C[:]), reads=[bC], writes=[bCb])
        S.dma(lambda e: e.dma_start(out=hp_bc[:], in_=hpar.to_broadcast([128, 96])), writes=[bhp])

        wcount = [0]

        def load_w(segs):
            i = wcount[0] % nwb[0]
            wcount[0] += 1
            t, b = wbuf[i], bw[i]
            for (src, nk, c0, n) in segs:
                S.dma(lambda e, src=src, nk=nk, c0=c0, n=n: e.dma_start(out=t[:, 0:nk, c0:c0 + n], in_=src),
                      writes=[b], eng="pool")
            return t, b

        def win_seg(col0, n, c0):
            return (w_in_v[:, :, col0:col0 + n], 8, c0, n)

        with ExitStack() as p0:
            gbc = SB(p0, "gbc", [128, D], F32); bg = Buf()
            xin = [SB(p0, "xin%d" % i, [128, D], F32) for i in range(2)]; bxin = [Buf(), Buf()]
            junk = SB(p0, "junk", [128, D], BF16); bjunk = Buf()
            hb = [SB(p0, "hb%d" % i, [128, D], BF16) for i in range(2)]; bhb = [Buf(), Buf()]
            ssq = SB(p0, "ssq", [128, 51], F32); bss = Buf()
            S.dma(lambda e: e.dma_start(out=gbc[:], in_=norm_g.to_broadcast([128, D])), writes=[bg])
            S.add("dve", lambda e: e.memset(ssq[:], 0.0), writes=[bss])
            for ti, (t0, rows) in enumerate(TT):
                xi, bxi = xin[ti % 2], bxin[ti % 2]
                S.dma(lambda e, xi=xi, t0=t0, rows=rows: e.dma_start(out=xi[0:rows, :], in_=x_all[t0:t0 + rows, :]), writes=[bxi])
                S.add("act", lambda e, xi=xi, rows=rows, ti=ti: e.activation(out=junk[0:rows, :], in_=xi[0:rows, :], func=AF.Square,
                                                                             accum_out=ssq[0:rows, ti:ti + 1]), reads=[bxi], writes=[bjunk, bss])
            S.add("act", lambda e: e.activation(out=ssq[:, 17:34], in_=ssq[:, 0:17], func=AF.Sqrt, scale=1.0 / D, bias=1e-6), reads=[bss], writes=[bss])
            S.add("dve", lambda e: e.reciprocal(out=ssq[:, 34:51], in_=ssq[:, 17:34]), reads=[bss], writes=[bss])
            for ti, (t0, rows) in enumerate(TT):
                xi, bxi = xin[ti % 2], bxin[ti % 2]
                h_, bh_ = hb[ti % 2], bhb[ti % 2]
                pb, bpb = PS[ti % 2], PB[ti % 2]
                S.dma(lambda e, xi=xi, t0=t0, rows=rows: e.dma_start(out=xi[0:rows, :], in_=x_all[t0:t0 + rows, :]), writes=[bxi])
                S.add("dve", lambda e, xi=xi, h_=h_, rows=rows, ti=ti: e.scalar_tensor_tensor(
                    out=h_[0:rows, :], in0=xi[0:rows, :], scalar=ssq[0:rows, 34 + ti:35 + ti], in1=gbc[0:rows, :],
                    op0=ALU.mult, op1=ALU.mult), reads=[bxi, bss, bg], writes=[bh_])
                pv = pb[:].bitcast(BF16)
                for kc in range(8):
                    S.add("pe", lambda e, pv=pv, h_=h_, kc=kc, rows=rows: e.transpose(
                        out=pv[:, kc * 128:kc * 128 + rows], in_=h_[0:rows, kc * 128:(kc + 1) * 128],
                        identity=IDb[0:rows, 0:rows]), reads=[bh_, bCb], writes=[bpb])
                S.add("act", lambda e, pv=pv, t0=t0, rows=rows: e.copy(
                    out=XN[:, :, t0:t0 + rows], in_=pv.rearrange("p (k t) -> p k t", k=8)[:, :, 0:rows]),
                    reads=[bpb], writes=[bXN, bpb])
        S.barrier(scr)

        wbuf.append(SB(tabs, "wbuf1", [128, 8, 512], BF16))
        dtt = SB(tabs, "dtt", [128, 17, 32], F32)
        acum = SB(tabs, "acum", [128, 17, 32], F32)
        eacum = SB(tabs, "eacum", [128, 17, 32], F32)
        dtw = SB(tabs, "dtw", [128, 17, 32], F32)
        decay = SB(tabs, "decay", [128, 17, 32], F32)
        totb = SB(tabs, "totb", [128, 17, 32], F32)
        nacum = SB(tabs, "nacum", [128, 17, 32], F32)
        bT = Buf()
        with ExitStack() as p1:
            dta = SB(p1, "dta", [128, 17, 32], F32)
            acTs = SB(p1, "acTs", [32, NT], F32); bacT = Buf()
            wt, bwt = load_w([win_seg(9216, 32, 0)])
            S.add("dve", lambda e: e.memset(dtt[:], 0.0), writes=[bT])
            for ti, (t0, rows) in enumerate(TT):
                pb, bpb = (PS[2], PB[2]) if ti < 16 else (PS[3], PB[3])
                c0 = (ti % 16) * 32
                for kc in range(8):
                    S.add("pe", lambda e, pb=pb, kc=kc, t0=t0, rows=rows, c0=c0: e.matmul(
                        pb[0:rows, c0:c0 + 32], lhsT=XN[:, kc, t0:t0 + rows], rhs=wt[:, kc, 0:32],
                        start=(kc == 0), stop=(kc == 7)), reads=[bXN, bwt], writes=[bpb])
            S.add("dve", lambda e: e.tensor_tensor(out=dtt[:, 0:16, :], in0=PS[2][:].rearrange("p (c h) -> p c h", h=32),
                                                   in1=hp_bc[:, 0:32].unsqueeze(1).to_broadcast([128, 16, 32]), op=ALU.add),
                  reads=[PB[2], bhp], writes=[bT, PB[2]])
            S.add("dve", lambda e: e.tensor_tensor(out=dtt[0:32, 16, :], in0=PS[3][0:32, 0:32], in1=hp_bc[0:32, 0:32], op=ALU.add),
                  reads=[PB[3], bhp], writes=[bT, PB[3]])
            S.add("act", lambda e: e.activation(out=dtt[:], in_=dtt[:], func=AF.Exp), reads=[bT], writes=[bT])
            S.add("act", lambda e: e.activation(out=dtt[:], in_=dtt[:], func=AF.Ln, bias=1.0), reads=[bT], writes=[bT])
            S.add("dve", lambda e: e.memset(dtt[32:64, 16, :], 0.0), writes=[bT])
            S.add("dve", lambda e: e.memset(dtt[64:128, 16, :], 0.0), writes=[bT])
            S.add("act", lambda e: e.activation(out=hp_bc[:, 32:64], in_=hp_bc[:, 32:64], func=AF.Exp), reads=[bhp], writes=[bhp])
            S.add("dve", lambda e: e.scalar_tensor_tensor(out=dta[:], in0=dtt[:], scalar=-1.0,
                                                          in1=hp_bc[:, 32:64].unsqueeze(1).to_broadcast([128, 17, 32]),
                                                          op0=ALU.mult, op1=ALU.mult), reads=[bT, bhp], writes=[bT])
            for c in range(17):
                tri = TRIp if c < 16 else TRIs
                ones = ONESf if c < 16 else ONESs
                pa, bpa = (PS[4], PB[4]) if c < 16 else (PS[5], PB[5])
                pt, bpt = (PS[6], PB[6]) if c < 16 else (PS[7], PB[7])
                c0 = (c % 16) * 32
                S.add("pe", lambda e, pa=pa, tri=tri, c=c, c0=c0: e.matmul(pa[:, c0:c0 + 32], lhsT=tri, rhs=dta[:, c, :],
                                                                         start=True, stop=True), reads=[bT, bC], writes=[bpa])
                S.add("pe", lambda e, pt=pt, ones=ones, c=c, c0=c0: e.matmul(pt[:, c0:c0 + 32], lhsT=ones, rhs=dta[:, c, :],
                                                                           start=True, stop=True), reads=[bT, bC], writes=[bpt])
            S.add("dve", lambda e: e.tensor_copy(out=acum[:, 0:16, :], in_=PS[4][:].rearrange("p (c h) -> p c h", h=32)),
                  reads=[PB[4]], writes=[bT, PB[4]])
            S.add("dve", lambda e: e.tensor_copy(out=acum[:, 16, :], in_=PS[5][:, 0:32]), reads=[PB[5]], writes=[bT, PB[5]])
            S.add("dve", lambda e: e.tensor_copy(out=totb[:, 0:16, :], in_=PS[6][:].rearrange("p (c h) -> p c h", h=32)),
                  reads=[PB[6]], writes=[bT, PB[6]])
            S.add("dve", lambda e: e.tensor_copy(out=totb[:, 16, :], in_=PS[7][:, 0:32]), reads=[PB[7]], writes=[bT, PB[7]])
            S.add("act", lambda e: e.activation(out=eacum[:], in_=acum[:], func=AF.Exp), reads=[bT], writes=[bT])
            S.add("act", lambda e: e.activation(out=decay[:], in_=totb[:], func=AF.Exp), reads=[bT], writes=[bT])
            S.add("dve", lambda e: e.tensor_sub(out=dtw[:], in0=totb[:], in1=acum[:]), reads=[bT], writes=[bT])
            S.add("act", lambda e: e.activation(out=dtw[:], in_=dtw[:], func=AF.Exp), reads=[bT], writes=[bT])
            S.add("dve", lambda e: e.tensor_mul(out=dtw[:], in0=dtw[:], in1=dtt[:]), reads=[bT], writes=[bT])
            S.add("dve", lambda e: e.tensor_scalar(out=nacum[:], in0=acum[:], scalar1=-1.0, scalar2=None, op0=ALU.mult), reads=[bT], writes=[bT])
            for c in range(17):
                tri = TRIp if c < 16 else TRIs
                pb, bpb = PS[c % 2], PB[c % 2]
                rows = 128 if c < 16 else 32
                S.add("pe", lambda e, pb=pb, tri=tri, c=c, rows=rows: e.matmul(pb[0:32, 0:rows], lhsT=dta[:, c, :], rhs=tri[:, 0:rows],
                                                                             start=True, stop=True), reads=[bT, bC], writes=[bpb])
                S.add("dve", lambda e, pb=pb, c=c, rows=rows: e.tensor_copy(out=acTs[:, c * 128:c * 128 + rows], in_=pb[0:32, 0:rows]),
                      reads=[bpb], writes=[bacT, bpb])
            bscr = Buf()
            S.dma(lambda e: e.dma_start(out=acT_d, in_=acTs[:]), reads=[bacT], writes=[bscr])
        S.barrier(scr)

        with ExitStack() as p2:
            cw = SB(p2, "cw", [128, 32, 4], F32); bcw = Buf()
            cb = SB(p2, "cb", [128, 32], F32)
            sct = SB(p2, "sct", [128, 32, 4, 3], F32)
            S.dma(lambda e: e.dma_start(out=cw[:], in_=cwT), writes=[bcw])
            S.dma(lambda e: e.dma_start(out=cb[:], in_=cbT), writes=[bcw])
            S.dma(lambda e: e.dma_start(out=sct[:], in_=scT), writes=[bcw])
            S.add("pool", lambda e: e.tensor_scalar(out=cw[:], in0=cw[:], scalar1=0.5, scalar2=None, op0=ALU.mult), reads=[bcw], writes=[bcw])
            S.add("pool", lambda e: e.tensor_scalar(out=cb[:], in0=cb[:], scalar1=0.5, scalar2=None, op0=ALU.mult), reads=[bcw], writes=[bcw])
            hsel = SB(p2, "hsel", [128, 8, 15], BF16); bhsel = Buf()
            S.add("pool", lambda e: e.tensor_copy(out=hsel[:, :, 0:3], in_=XN[:, :, 2045:2048]), reads=[bXN], writes=[bhsel])
            for s in range(4):
                S.add("pool", lambda e, s=s: e.tensor_copy(out=hsel[:, :, 3 + 3 * s:6 + 3 * s], in_=XN[:, :, LP + 8 * s + 5:LP + 8 * s + 8]),
                      reads=[bXN], writes=[bhsel])
            wz = [SB(p2, "wz%d" % i, [128, 8, 256], BF16) for i in range(1)] * 2; bwz = [Buf()] * 2
            xwb = SB(p2, "xwb", [128, 3 + LP], BF16); bxw = Buf()
            xwsb = SB(p2, "xwsb", [128, 4, 11], BF16); bxws = Buf()
            tnh2 = [SB(p2, "tnh2_%d" % i, [128, 512], BF16) for i in range(2)]; btnh2 = [Buf(), Buf()]
            dg = SB(p2, "dg", [128, 4, 128], BF16); bdg = Buf()
            cbrf = SB(p2, "cbrf", [1, 128], F32); bcbrf = Buf()
            cbrb = SB(p2, "cbrb", [1, 128], BF16); bcbrb = Buf()
            onesrow = SB(p2, "onesrow", [1, 512], BF16); bones = Buf()
            S.add("pool", lambda e: e.memset(onesrow[:], 1.0), writes=[bones])
            xdt2 = [SB(p2, "xdt%d" % i, [128, 256], BF16) for i in range(2)]; bxdt2 = [Buf(), Buf()]
            xdb2 = [SB(p2, "xdb%d" % i, [128, 256], BF16) for i in range(2)]; bxdb2 = [Buf(), Buf()]
            XT = SB(p2, "XT", [128, 4, NT], BF16); bXT = [Buf() for _ in range(4)]
            xtok = SB(p2, "xtok", [128, 17, 256], BF16); bxtk = [Buf() for _ in range(17)]
            btok = SB(p2, "btok", [128, 17, 128], BF16); bbtok = Buf()
            CBm = SB(p2, "CBm", [128, 17, 128], BF16); bCBm = Buf()
            CTs = SB(p2, "CTs", [128, 4, 32], BF16); bCTs = Buf()
            abc = [SB(p2, "abc%d" % i, [128, 4, 128], F32) for i in range(2)]; babc = [Buf() for _ in range(2)]
            Dt = [SB(p2, "Dt%d" % i, [128, 2, 128], F32) for i in range(1)] * 2; bDt = [Buf()] * 2
            Et = [SB(p2, "Et%d" % i, [128, 4, 128], BF16) for i in range(2)]; bEt = [Buf(), Buf()]
            Mt = [SB(p2, "Mt%d" % i, [128, 4, 128], BF16) for i in range(2)]; bMt = [Buf(), Buf()]
            Bw = [SB(p2, "Bw%d" % i, [128, 256], BF16) for i in range(2)]; bBw = [Buf(), Buf()]
            yt = [SB(p2, "yt%d" % i, [128, 256], F32) for i in range(2)]; byt = [Buf(), Buf()]
            STt = SB(p2, "STt", [128, 256], F32); bST = Buf()
            STb = SB(p2, "STb", [128, 256], BF16); bSTb = Buf()
            tz = SB(p2, "tz", [128, 256], F32); btz = Buf()
            uz = tz; buz = btz
            yg = tz; byg = btz
            ynb2 = [SB(p2, "ynb%d" % i, [128, 256], BF16) for i in range(2)]; bynb2 = [Buf(), Buf()]; ynb = ynb2[0]; bynb = bynb2[0]
            gs = SB(p2, "gs", [128, 51], F32); bgs = Buf()
            nrm = SB(p2, "nrm", [128, 256], F32); bnrm = Buf()
            cvo = abc[0][:, :, :].rearrange("p a b -> p (a b)"); bcvo = babc[0]
            h0T = SB(p2, "h0T", [128, 4, 256], BF16); bh0T = Buf()
            h0n = [SB(p2, "h0n%d" % i, [128, 2, 128], F32) for i in range(1)] * 2; bh0n = [Buf()] * 2
            wxm = SB(p2, "wxm", [32, 256], BF16); bwxm = Buf(); xsr = SB(p2, "xsr", [32, 256], BF16); bxsr = Buf()
            dsg = SB(p2, "dsg", [32, 4, 4], F32); bdsg = Buf()
            dtaE = SB(p2, "dtaE", [32, 256], F32); bdtaE = Buf()
            dcol = SB(p2, "dcol", [128, 2, 4], F32); bdcol = Buf()
            nst = [SB(p2, "nst%d" % i, [128, 2, 128], F32) for i in range(1)] * 2; bnst = [Buf()] * 2
            stp = nst[0]; bstp = bnst[0]
            jk = ynb; bjk = bynb

            WT = {}
            wzt, bwzt = wz[0], bwz[0]

            def g_loadw(g):
                WT[g] = load_w([win_seg(5120 + 256 * g, 256, 0), win_seg(7168 + 128 * g, 128, 256), win_seg(8192 + 128 * g, 128, 384)])

            def g_pre(g):
                wt, bwt = WT[g]
                S.dma(lambda e: e.dma_start(out=wzt[:], in_=w_in_v[:, :, 3072 + 256 * g:3072 + 256 * (g + 1)]), writes=[bwzt], eng="pool")
                S.dma(lambda e: e.dma_start(out=nrm[:], in_=ssm_norm[:, 256 * g:256 * (g + 1)].to_broadcast([128, 256])), writes=[bnrm])
                S.dma(lambda e: e.dma_start(out=h0T[:], in_=st_T[:, :, 256 * g:256 * (g + 1)]), writes=[bh0T], eng="pool")
                for kc in range(8):
                    S.add("pe", lambda e, kc=kc, wt=wt: e.matmul(PS[2][0:15, :], lhsT=hsel[:, kc, :], rhs=wt[:, kc, :],
                                                               start=(kc == 0), stop=(kc == 7)), reads=[bhsel, bwt], writes=[PB[2]])
                S.add("act", lambda e: e.copy(out=cvo[0:15, :], in_=PS[2][0:15, :]), reads=[PB[2]], writes=[bcvo, PB[2]])
                S.dma(lambda e, g=g: e.dma_start(out=conv_out[:, 256 * g:256 * (g + 1)], in_=cvo[0:15, 0:256]), reads=[bcvo])
                S.dma(lambda e, g=g: e.dma_start(out=conv_out[:, 2048 + 128 * g:2048 + 128 * (g + 1)], in_=cvo[0:15, 256:384]), reads=[bcvo])
                S.dma(lambda e, g=g: e.dma_start(out=conv_out[:, 3072 + 128 * g:3072 + 128 * (g + 1)], in_=cvo[0:15, 384:512]), reads=[bcvo])

            def g_conv_items(g, tiles):
                wt, bwt = WT[g]
                conv_items = []
                for ct in tiles:
                    ctile = (2 * g + ct) if ct < 2 else (16 + g if ct == 2 else 24 + g)
                    for bi, (t0, n) in enumerate(TB):
                        def c0(ct=ct, ctile=ctile, bi=bi, t0=t0, n=n, wt=wt, bwt=bwt):
                            pb, bpb = PS[bi % 2], PB[bi % 2]
                            if bi == 0:
                                S.add("pool", lambda e: e.memset(xwb[:, 0:3], 0.0), writes=[bxw])
                                S.add("pool", lambda e: e.tensor_copy(out=xwsb[:, :, 0:3], in_=sct[:, ctile, :, :]), reads=[bcw], writes=[bxws])
                                S.add("pool", lambda e: e.tensor_tensor(out=dg[:], in0=IDb.unsqueeze(1).to_broadcast([128, 4, 128]),
                                                                        in1=cw[:, ctile, :].unsqueeze(2).to_broadcast([128, 4, 128]), op=ALU.mult),
                                      reads=[bCb, bcw], writes=[bdg])
                                S.dma(lambda e: e.dma_start(out=cbrf[:], in_=cb_row[:, ctile * 128:(ctile + 1) * 128]), writes=[bcbrf])
                                S.add("pool", lambda e: e.tensor_scalar(out=cbrb[:], in0=cbrf[:], scalar1=0.5, scalar2=None, op0=ALU.mult), reads=[bcbrf], writes=[bcbrb])
                            for kc in range(8):
                                S.add("pe", lambda e, kc=kc: e.matmul(pb[:, 0:n], lhsT=wt[:, kc, ct * 128:(ct + 1) * 128], rhs=XN[:, kc, t0:t0 + n],
                                                                     start=(kc == 0), stop=(kc == 7)), reads=[bXN, bwt], writes=[bpb])
                            if bi < 4:
                                S.add("act", lambda e: e.copy(out=xwb[:, 3 + 512 * bi:3 + 512 * (bi + 1)], in_=pb[:, :]), reads=[bpb], writes=[bxw, bpb])
                            else:
                                S.add("act", lambda e: e.copy(out=xwsb[:, :, 3:11], in_=pb[:, 0:32].rearrange("p (s t) -> p s t", s=4)),
                                      reads=[bpb], writes=[bxws, bpb])

                        def c1(ct=ct, bi=bi):
                            cps, bcps = PS[2 + (bi % 2)], PB[2 + (bi % 2)]
                            tn_, btn_ = tnh2[bi % 2], btnh2[bi % 2]
                            if bi < 4:
                                S.add("pe", lambda e: e.matmul(cps[:, :], lhsT=cbrb[0:1, :], rhs=onesrow[0:1, :], start=True, stop=False),
                                      reads=[bcbrb, bones], writes=[bcps])
                                for tap in range(4):
                                    S.add("pe", lambda e, tap=tap: e.matmul(cps[:, :], lhsT=dg[:, tap, :], rhs=xwb[:, 512 * bi + tap:512 * bi + tap + 512],
                                                                           start=False, stop=(tap == 3)), reads=[bdg, bxw], writes=[bcps])
                                S.add("act", lambda e: e.activation(out=tn_[:], in_=cps[:, :], func=AF.Tanh), reads=[bcps], writes=[btn_])
                                S.add("dve", lambda e: e.scalar_tensor_tensor(out=XT[:, ct, 512 * bi:512 * (bi + 1)], in0=tn_[:], scalar=1.0, in1=cps[:, :],
                                                                              op0=ALU.add, op1=ALU.mult), reads=[btn_, bcps], writes=[bXT[ct]])
                            else:
                                S.add("pe", lambda e: e.matmul(cps[:, 0:32], lhsT=cbrb[0:1, :], rhs=onesrow[0:1, 0:32], start=True, stop=False),
                                      reads=[bcbrb, bones], writes=[bcps])
                                for tap in range(4):
                                    S.add("pe", lambda e, tap=tap: e.matmul(cps[:, 0:32].rearrange("p (s t) -> p s t", s=4), lhsT=dg[:, tap, :],
                                                                           rhs=xwsb[:, :, tap:tap + 8], start=False, stop=(tap == 3)),
                                          reads=[bdg, bxws], writes=[bcps])
                                S.add("act", lambda e: e.activation(out=tn_[:, 0:32], in_=cps[:, 0:32], func=AF.Tanh), reads=[bcps], writes=[btn_])
                                S.add("dve", lambda e: e.scalar_tensor_tensor(out=XT[:, ct, LP:NT], in0=tn_[:, 0:32], scalar=1.0, in1=cps[:, 0:32],
                                                                              op0=ALU.add, op1=ALU.mult), reads=[btn_, bcps], writes=[bXT[ct]])
                        conv_items.append([c0, c1])
                return conv_items

            def g_mid(g):
                for c4 in range(5):
                    chunks = list(range(c4 * 4, min(c4 * 4 + 4, 17)))
                    ix = c4 % 2
                    pvx, bpx = PS[0 + ix][:].bitcast(BF16), PB[0 + ix]
                    pvb, bpb_ = PS[3 + ix][:].bitcast(BF16), PB[3 + ix]
                    pcb, bpc = (PS[2], PB[2]) if ix == 0 else (PS[5], PB[5])
                    for j, c in enumerate(chunks):
                        t0, rows = TT[c]
                        for k in range(2):
                            S.add("pe", lambda e, pvx=pvx, j=j, k=k, t0=t0, rows=rows: e.transpose(
                                out=pvx[0:rows, j * 256 + k * 128:j * 256 + (k + 1) * 128], in_=XT[:, k, t0:t0 + rows], identity=IDb),
                                reads=[bXT[k], bCb], writes=[bpx])
                    nchk = len(chunks)
                    rws = 128 if chunks[-1] < 16 else 32
                    S.add("act", lambda e, pvx=pvx, c4=c4, nchk=nchk, rws=rws: e.copy(
                        out=xtok[0:rws, c4 * 4:c4 * 4 + nchk, :], in_=pvx[0:rws, 0:nchk * 256].rearrange("p (c x) -> p c x", x=256)),
                        reads=[bpx], writes=[bxtk[c_] for c_ in chunks] + [bpx])
                    for j, c in enumerate(chunks):
                        t0, rows = TT[c]
                        S.add("pe", lambda e, pvb=pvb, j=j, t0=t0, rows=rows: e.transpose(
                            out=pvb[0:rows, j * 128:(j + 1) * 128], in_=XT[:, 2, t0:t0 + rows], identity=IDb),
                            reads=[bXT[2], bCb], writes=[bpb_])
                    S.add("act", lambda e, pvb=pvb, c4=c4, nchk=nchk, rws=rws: e.copy(
                        out=btok[0:rws, c4 * 4:c4 * 4 + nchk, :], in_=pvb[0:rws, 0:nchk * 128].rearrange("p (c x) -> p c x", x=128)),
                        reads=[bpb_], writes=[bbtok, bpb_])
                    for j, c in enumerate(chunks):
                        t0, rows = TT[c]
                        S.add("pe", lambda e, pcb=pcb, j=j, t0=t0, rows=rows: e.matmul(
                            pcb[0:rows, j * 128:j * 128 + rows], lhsT=XT[:, 2, t0:t0 + rows], rhs=XT[:, 3, t0:t0 + rows],
                            start=True, stop=True), reads=[bXT[2], bXT[3]], writes=[bpc])
                    if rws == 128:
                        S.add("dve", lambda e, pcb=pcb, c4=c4, nchk=nchk: e.tensor_tensor(
                            out=CBm[:, c4 * 4:c4 * 4 + nchk, :], in0=pcb[:, 0:nchk * 128].rearrange("p (c x) -> p c x", x=128),
                            in1=TRIpb.unsqueeze(1).to_broadcast([128, nchk, 128]), op=ALU.mult),
                            reads=[bpc, bCb], writes=[bCBm, bpc])
                    else:
                        S.add("dve", lambda e, pcb=pcb: e.tensor_tensor(out=CBm[0:32, 16, 0:32], in0=pcb[0:32, 0:32], in1=TRIsb[0:32, 0:32], op=ALU.mult),
                              reads=[bpc, bCb], writes=[bCBm, bpc])
                S.add("pool", lambda e: e.tensor_tensor(out=CTs[:], in0=XT[:, 3, LP:NT].unsqueeze(1).to_broadcast([128, 4, 32]),
                                                        in1=SEGXb.rearrange("p (s t) -> p s t", s=4), op=ALU.mult),
                      reads=[bXT[3], bCb], writes=[bCTs])
                S.add("dve", lambda e, g=g: e.tensor_tensor(out=dsg[:], in0=dtw[0:32, 16, 4 * g:4 * g + 4].unsqueeze(1).to_broadcast([32, 4, 4]),
                                                           in1=SEG.unsqueeze(2).to_broadcast([32, 4, 4]), op=ALU.mult), reads=[bT, bC], writes=[bdsg])
                S.add("pool", lambda e: e.tensor_copy(out=xsr[:], in_=xtok[0:32, 16, :]), reads=[bxtk[16]], writes=[bxsr])
                S.add("dve", lambda e: e.memset(gs[:], 0.0), writes=[bgs])
                S.add("dve", lambda e: e.memset(STt[:], 0.0), writes=[bST])
                S.add("dve", lambda e: e.memset(STb[:], 0.0), writes=[bSTb])

            def g_chunk_items(g):
                chunk_items = []
                for c in range(17):
                    def s0(c=c, g=g):
                        t0, rows = TT[c]
                        a_, ba_ = abc[c % 2], babc[c % 2]
                        d_, bd_ = Dt[c % 2], bDt[c % 2]
                        e_, be_ = Et[c % 2], bEt[c % 2]
                        w_, bw_ = Bw[c % 2], bBw[c % 2]
                        xdt, bxdt = xdt2[c % 2], bxdt2[c % 2]
                        xdb, bxdb = xdb2[c % 2], bxdb2[c % 2]
                        S.dma(lambda e: e.dma_start(out=a_[:, :, 0:rows], in_=acT_d[4 * g:4 * g + 4, t0:t0 + rows].partition_broadcast(128)),
                              reads=[bscr], writes=[ba_])
                        for j in (2, 3):
                            hh = 4 * g + j
                            S.add("dve", lambda e, j=j, hh=hh: e.tensor_scalar(
                                out=d_[0:rows, j - 2, 0:rows], in0=a_[0:rows, j, 0:rows], scalar1=acum[0:rows, c, hh:hh + 1], scalar2=0.0,
                                op0=ALU.subtract, op1=ALU.min), reads=[ba_, bT], writes=[bd_])
                        for j in (0, 1):
                            hh = 4 * g + j
                            S.add("act", lambda e, j=j, hh=hh: e.activation(out=e_[0:rows, j, 0:rows], in_=a_[0:rows, j, 0:rows], func=AF.Exp,
                                                                           bias=nacum[0:rows, c, hh:hh + 1]), reads=[ba_, bT], writes=[be_])
                        S.add("act", lambda e: e.activation(out=e_[0:rows, 2:4, 0:rows], in_=d_[0:rows, 0:2, 0:rows], func=AF.Exp), reads=[bd_], writes=[be_])
                        S.add("pool", lambda e: e.tensor_tensor(
                            out=xdt[0:rows, :].rearrange("p (h x) -> p h x", h=4), in0=xtok[0:rows, c, :].rearrange("p (h x) -> p h x", h=4),
                            in1=dtt[0:rows, c, 4 * g:4 * g + 4].unsqueeze(2).to_broadcast([rows, 4, 64]), op=ALU.mult), reads=[bxtk[c], bT], writes=[bxdt])
                        S.add("pool", lambda e: e.tensor_tensor(
                            out=xdb[0:rows, :].rearrange("p (h x) -> p h x", h=4), in0=xtok[0:rows, c, :].rearrange("p (h x) -> p h x", h=4),
                            in1=hp_bc[0:rows, 64 + 4 * g:68 + 4 * g].unsqueeze(2).to_broadcast([rows, 4, 64]), op=ALU.mult), reads=[bxtk[c], bhp], writes=[bxdb])
                        if c < 16:
                            S.add("pool", lambda e: e.tensor_tensor(
                                out=w_[:, :].rearrange("p (h x) -> p h x", h=4), in0=xtok[:, c, :].rearrange("p (h x) -> p h x", h=4),
                                in1=dtw[:, c, 4 * g:4 * g + 4].unsqueeze(2).to_broadcast([128, 4, 64]), op=ALU.mult), reads=[bxtk[c], bT], writes=[bw_])

                    def s1(c=c, g=g, wzt=wzt, bwzt=bwzt):
                        t0, rows = TT[c]
                        e_, be_ = Et[c % 2], bEt[c % 2]
                        m_, bm_ = Mt[c % 2], bMt[c % 2]
                        w_, bw_ = Bw[c % 2], bBw[c % 2]
                        xdt, bxdt = xdt2[c % 2], bxdt2[c % 2]
                        xdb, bxdb = xdb2[c % 2], bxdb2[c % 2]
                        yb, byb = PS[4 + (c % 2)], PB[4 + (c % 2)]
                        sbk, bsbk = (PS[6], PB[6]) if c % 2 == 0 else (PS[2], PB[2])
                        zb, bzb = (PS[7], PB[7]) if c % 2 == 0 else (PS[3], PB[3])
                        S.add("dve", lambda e: e.scalar_tensor_tensor(
                            out=m_[0:rows, :, 0:rows], in0=e_[0:rows, :, 0:rows], scalar=1.0, in1=CBm[0:rows, c, 0:rows].unsqueeze(1).to_broadcast([rows, 4, rows]),
                            op0=ALU.min, op1=ALU.mult), reads=[be_, bCBm], writes=[bm_])
                        for kc in range(8):
                            S.add("pe", lambda e, kc=kc: e.matmul(zb[0:rows, 0:256], lhsT=XN[:, kc, t0:t0 + rows], rhs=wzt[:, kc, :],
                                                                 start=(kc == 0), stop=(kc == 7)), reads=[bXN, bwzt], writes=[bzb])
                        if c < 16:
                            S.add("pe", lambda e: e.matmul(sbk[:, 0:256], lhsT=btok[:, c, :], rhs=w_[:, :], start=True, stop=True),
                                  reads=[bw_, bbtok], writes=[bsbk])
                            S.add("pe", lambda e: e.matmul(yb[:, 256:512], lhsT=XT[:, 3, t0:t0 + 128], rhs=STb[:],
                                                           start=True, stop=True, skip_group_check=True), reads=[bXT[3], bSTb], writes=[byb])
                        else:
                            for sq in range(4):
                                S.add("pe", lambda e, sq=sq: e.matmul(yb[0:32, 256:512], lhsT=CTs[:, sq, :], rhs=h0T[:, sq, :],
                                                                     start=(sq == 0), stop=(sq == 3), skip_group_check=True), reads=[bCTs, bh0T], writes=[byb])
                        S.add("pe", lambda e: e.matmul(yb[0:rows, 0:256], lhsT=IDb[0:rows, 0:rows], rhs=xdb[0:rows, :],
                                                       start=False, stop=False, skip_group_check=True), reads=[bxdb, bCb], writes=[byb])
                        for j in range(4):
                            S.add("pe", lambda e, j=j: e.matmul(yb[0:rows, 64 * j:64 * j + 64], lhsT=m_[0:rows, j, 0:rows], rhs=xdt[0:rows, 64 * j:64 * j + 64],
                                                               start=False, stop=True, skip_group_check=True), reads=[bm_, bxdt], writes=[byb])

                    def s2(c=c, g=g):
                        t0, rows = TT[c]
                        y_, by_ = yt[c % 2], byt[c % 2]
                        yb, byb = PS[4 + (c % 2)], PB[4 + (c % 2)]
                        sbk, bsbk = (PS[6], PB[6]) if c % 2 == 0 else (PS[2], PB[2])
                        zb, bzb = (PS[7], PB[7]) if c % 2 == 0 else (PS[3], PB[3])
                        if c < 16:
                            S.add("pool", lambda e: e.tensor_tensor(
                                out=STt[:].rearrange("p (h x) -> p h x", h=4), in0=STt[:].rearrange("p (h x) -> p h x", h=4),
                                in1=decay[:, c, 4 * g:4 * g + 4].unsqueeze(2).to_broadcast([128, 4, 64]), op=ALU.mult), reads=[bT, bSTb], writes=[bST])
                            S.add("dve", lambda e: e.tensor_tensor(out=STt[:], in0=STt[:], in1=sbk[:, 0:256], op=ALU.add), reads=[bsbk], writes=[bST, bsbk])
                            S.add("act", lambda e: e.copy(out=STb[:], in_=STt[:]), reads=[bST], writes=[bSTb])
                        S.add("dve", lambda e: e.tensor_tensor(
                            out=y_[0:rows, :].rearrange("p (h x) -> p h x", h=4), in0=yb[0:rows, 256:512].rearrange("p (h x) -> p h x", h=4),
                            in1=eacum[0:rows, c, 4 * g:4 * g + 4].unsqueeze(2).to_broadcast([rows, 4, 64]), op=ALU.mult), reads=[byb, bT], writes=[by_])
                        S.add("dve", lambda e: e.tensor_tensor(out=y_[0:rows, :], in0=y_[0:rows, :], in1=yb[0:rows, 0:256], op=ALU.add), reads=[byb], writes=[by_, byb])
                        S.add("act", lambda e: e.activation(out=tz[0:rows, :], in_=zb[0:rows, 0:256], func=AF.Tanh, scale=0.5), reads=[bzb], writes=[btz])
                        S.add("dve", lambda e: e.scalar_tensor_tensor(out=tz[0:rows, :], in0=tz[0:rows, :], scalar=1.0, in1=zb[0:rows, 0:256],
                                                                      op0=ALU.add, op1=ALU.mult), reads=[bzb], writes=[btz, bzb])
                        S.add("dve", lambda e: e.tensor_tensor(out=tz[0:rows, :], in0=tz[0:rows, :], in1=y_[0:rows, :], op=ALU.mult), reads=[by_], writes=[btz])
                        S.add("act", lambda e: e.activation(out=ynb[0:rows, :], in_=tz[0:rows, :], func=AF.Square, accum_out=gs[0:rows, c:c + 1]),
                              reads=[btz], writes=[bynb, bgs])
                        S.add("act", lambda e: e.copy(out=xtok[0:rows, c, :], in_=tz[0:rows, :]), reads=[btz], writes=[bxtk[c]])
                    chunk_items.append([s0, s1, s2])
                return chunk_items

            def g_post(g):
                S.add("act", lambda e: e.activation(out=gs[:, 17:34], in_=gs[:, 0:17], func=AF.Sqrt, scale=1.0 / 256, bias=4e-5), reads=[bgs], writes=[bgs])
                S.add("dve", lambda e: e.reciprocal(out=gs[:, 34:51], in_=gs[:, 17:34]), reads=[bgs], writes=[bgs])
                norm_items = []
                for c in range(17):
                    def n0(c=c):
                        t0, rows = TT[c]
                        yn_, byn_ = ynb2[c % 2], bynb2[c % 2]
                        S.add("dve", lambda e: e.scalar_tensor_tensor(out=yn_[0:rows, :], in0=xtok[0:rows, c, :], scalar=gs[0:rows, 34 + c:35 + c], in1=nrm[0:rows, :],
                                                                      op0=ALU.mult, op1=ALU.mult), reads=[bxtk[c], bgs, bnrm], writes=[byn_])

                    def n1(c=c, g=g):
                        t0, rows = TT[c]
                        yn_, byn_ = ynb2[c % 2], bynb2[c % 2]
                        nb = 3 if c % 2 == 0 else 2
                        pv = PS[nb][:].bitcast(BF16)
                        for k in range(2):
                            S.add("pe", lambda e, k=k: e.transpose(out=pv[:, k * 128:k * 128 + rows], in_=yn_[0:rows, k * 128:(k + 1) * 128],
                                                                  identity=IDb[0:rows, 0:rows]), reads=[byn_, bCb], writes=[PB[nb]])
                        S.add("act", lambda e: e.copy(out=ysT[:, 2 * g:2 * g + 2, t0:t0 + rows], in_=pv[:, 0:256].rearrange("p (k t) -> p k t", k=2)[:, :, 0:rows]),
                              reads=[PB[nb]], writes=[bys, PB[nb]])
                    norm_items.append([n0, n1])
                pipeline(norm_items)
                for k in range(2):
                    S.add("pe", lambda e, k=k: e.transpose(out=PS[2][:, k * 128:(k + 1) * 128], in_=STt[:, k * 128:(k + 1) * 128], identity=IDf),
                          reads=[bST, bC], writes=[PB[2]])
                S.add("act", lambda e: e.copy(out=stp[:], in_=PS[2][:, 0:256].rearrange("p (k n) -> p k n", k=2)), reads=[PB[2]], writes=[bstp, PB[2]])
                S.dma(lambda e, g=g: e.dma_start(out=ssm_p[4 * g:4 * g + 4, :, :].rearrange("(a b) p n -> (b p) a n", b=2), in_=stp[:]), reads=[bstp])
                S.add("dve", lambda e, g=g: e.tensor_scalar(out=dtaE[:].rearrange("p (h x) -> p h x", h=4),
                                                           in0=totb[0:32, 16, 4 * g:4 * g + 4].unsqueeze(2).to_broadcast([32, 4, 64]),
                                                           scalar1=0.125, scalar2=None, op0=ALU.mult), reads=[bT], writes=[bdtaE])
                for k in range(2):
                    S.add("pe", lambda e, k=k: e.matmul(PS[2][:, 256 + 4 * k:260 + 4 * k], lhsT=dtaE[:, k * 128:(k + 1) * 128], rhs=SEG,
                                                       start=True, stop=True), reads=[bdtaE, bC], writes=[PB[2]])
                S.add("act", lambda e: e.activation(out=dcol[:], in_=PS[2][:, 256:264].rearrange("p (k s) -> p k s", k=2), func=AF.Exp),
                      reads=[PB[2]], writes=[bdcol, PB[2]])
                for s in range(4):
                    S.dma(lambda e, g=g, s=s: e.dma_start(out=h0n[s % 2][:], in_=st_nat[s, 4 * g:4 * g + 4, :, :].rearrange("(a b) p n -> (b p) a n", b=2)),
                          writes=[bh0n[s % 2]])
                    S.add("dve", lambda e, s=s: e.tensor_tensor(out=wxm[:, :].rearrange("p (h x) -> p h x", h=4),
                                                               in0=xsr[:, :].rearrange("p (h x) -> p h x", h=4),
                                                               in1=dsg[:, s, :].unsqueeze(2).to_broadcast([32, 4, 64]), op=ALU.mult),
                          reads=[bxsr, bdsg], writes=[bwxm])
                    for k in range(2):
                        S.add("pe", lambda e, s=s, k=k: e.matmul(PS[6 + (s % 2)][:, k * 128:(k + 1) * 128],
                                                                lhsT=wxm[:, k * 128:(k + 1) * 128], rhs=btok[0:32, 16, :], start=True, stop=True),
                              reads=[bwxm, bbtok], writes=[PB[6 + (s % 2)]])
                    for k in range(2):
                        S.add("dve", lambda e, s=s, k=k: e.scalar_tensor_tensor(out=nst[s % 2][:, k, :], in0=h0n[s % 2][:, k, :], scalar=dcol[:, k, s:s + 1],
                                                                              in1=PS[6 + (s % 2)][:, k * 128:(k + 1) * 128], op0=ALU.mult, op1=ALU.add),
                              reads=[bh0n[s % 2], bdcol, PB[6 + (s % 2)]], writes=[bnst[s % 2], PB[6 + (s % 2)]])
                    S.dma(lambda e, g=g, s=s: e.dma_start(out=ssm_s[s, 4 * g:4 * g + 4, :, :].rearrange("(a b) p n -> (b p) a n", b=2), in_=nst[s % 2][:]), reads=[bnst[s % 2]])

            def interleave(a, b):
                out = []
                nb = len(b)
                for i, x in enumerate(a):
                    out.append(x)
                    if i < nb:
                        out.append(b[i])
                out.extend(b[len(a):])
                return out

            g_loadw(0)
            for g in range(8):
                g_pre(g)
                pipeline(g_conv_items(g, range(4)))
                g_mid(g)
                if g + 1 < 8:
                    g_loadw(g + 1)
                pipeline(g_chunk_items(g))
                g_post(g)
        S.barrier(scr)
        tabs.close()
        nwb[0] = 1

        build_attention_and_tail(nc, S, top, SB, PS, PB, XN, bXN, ysT, bys, load_w, win_seg, w_in_v, C, Cb, bC, bCb, scr, hp_bc, bhp,
                                 dict(x_all=x_all, w_a=w_a, w_b=w_b, w_o=w_o, final_g=final_g, tbraw=tbraw, fbraw=fbraw, fnraw=fnraw,
                                      cache_k=cache_k, cache_v=cache_v, y_out=y_out, k_out=k_out, v_out=v_out), TT, TB)
        S.emit(top)
    return nc


def build_attention_and_tail(nc, S, top, SB, PS, PB, XN, bXN, ysT, bys, load_w, win_seg, w_in_v, C, Cb, bC, bCb, scr, hp_bc, bhp, T, TT, TB):
    IDf = C[:, 0:128]; IDb = Cb[:, 0:128]; ONESb = Cb[:, 384:512]
    x_all, w_a, w_b, w_o, final_g = T["x_all"], T["w_a"], T["w_b"], T["w_o"], T["final_g"]
    k_out, v_out, y_out = T["k_out"], T["v_out"], T["y_out"]
    yaT = SB(top, "yaT", [128, 6, NT], BF16); bya = Buf()
    Qs = SB(top, "Qs", [32, 768], BF16); bQs = Buf()
    Ks = SB(top, "Ks", [32, 768], F32); bKs = Buf()
    Vs = SB(top, "Vs", [32, 768], BF16); bVs = Buf()

    with ExitStack() as p3:
        TM = SB(p3, "TM", [128, 2, 640], BF16); bTM = Buf()
        tstage = SB(p3, "tstage", [128, 2, 640], F32); bts = Buf()
        QT = SB(p3, "QT", [128, NT], BF16); bQT = Buf()
        KT = SB(p3, "KT", [128, NT], BF16); bKT = Buf()
        Kf = SB(p3, "Kf", [128, NT], F32); bKf = Buf()
        Vf = Kf; bVf = bKf
        VTb = SB(p3, "VTb", [128, NT], BF16); bVTb = Buf()
        SG = SB(p3, "SG", [128, NT], BF16); bSG = Buf()
        tg = SB(p3, "tg", [128, 512], F32); btg = Buf()
        Vx = [SB(p3, "Vx%d" % i, [128, 16, 2, 128], BF16) for i in range(3)]; bVx = [Buf(), Buf(), Buf()]
        ost = [SB(p3, "ost%d" % i, [128, 4, 128], F32) for i in range(1)] * 2; bost = [Buf()] * 2
        Pt = [SB(p3, "Pt%d" % i, [128, 512], BF16) for i in range(3)]; bPt = [Buf(), Buf(), Buf()]
        rec = SB(p3, "rec", [128, 512], F32); brec = Buf()
        tmpo = rec; btmpo = Buf()
        for i in range(3):
            S.add("pool", lambda e, i=i: e.memset(Vx[i][:], 1.0), writes=[bVx[i]])

        for hp in range(6):
            S.dma(lambda e, hp=hp: e.dma_start(out=tstage[:], in_=T["tbraw"][:, 2 * hp:2 * hp + 2, :]), writes=[bts])
            S.add("act", lambda e: e.copy(out=TM[:], in_=tstage[:]), reads=[bts], writes=[bTM])
            wt, bwt = load_w([win_seg(128 * hp, 128, 0), win_seg(768 + 128 * hp, 128, 128),
                              win_seg(1536 + 128 * hp, 128, 256), win_seg(2304 + 128 * hp, 128, 384)])
            def emit_kv_out(which, src, bsrc, dst, hp=hp):
                def rnd(c4):
                    kb = 2 if (which * 5 + c4) % 2 == 0 else 4
                    pk, bpk = PS[kb], PB[kb]
                    tiles = list(range(c4 * 4, min(c4 * 4 + 4, 17)))
                    o_, bo_ = ost[(which * 5 + c4) % 2], bost[(which * 5 + c4) % 2]
                    for j, ti in enumerate(tiles):
                        t0, rows = TT[ti]
                        S.add("pe", lambda e, j=j, t0=t0, rows=rows, src=src: e.transpose(out=pk[0:rows, j * 128:(j + 1) * 128], in_=src[:, t0:t0 + rows], identity=IDf),
                              reads=[bsrc, bC], writes=[bpk])
                    nt_ = len(tiles)
                    rws = 128 if tiles[-1] < 16 else 32
                    S.add("act", lambda e, o_=o_, nt_=nt_, rws=rws: e.copy(out=o_[0:rws, 0:nt_, :], in_=pk[0:rws, 0:nt_ * 128].rearrange("p (c x) -> p c x", x=128)),
                          reads=[bpk], writes=[bo_])
                    if which == 1 and rws == 128:
                        S.add("dve", lambda e, c4=c4: e.tensor_copy(out=Vx[0][:, c4 * 4:c4 * 4 + 4, 0, 0:64], in_=pk[:, :].rearrange("p (c x) -> p c x", x=128)[:, :, 0:64]),
                              reads=[bpk], writes=[bVx[0]])
                        S.add("dve", lambda e, c4=c4: e.tensor_copy(out=Vx[0][:, c4 * 4:c4 * 4 + 4, 1, 64:128], in_=pk[:, :].rearrange("p (c x) -> p c x", x=128)[:, :, 64:128]),
                              reads=[bpk], writes=[bVx[0], bpk])
                    if rws == 32:
                        tgt, btgt = (Ks, bKs) if which == 0 else (Vs, bVs)
                        S.add("dve", lambda e, tgt=tgt, hp=hp: e.tensor_copy(out=tgt[:, 128 * hp:128 * (hp + 1)], in_=pk[0:32, 0:128]), reads=[bpk], writes=[btgt, bpk])
                    S.add("dve", lambda e: e.memset(scr[0:1, 15:16], 0.0), reads=[bpk], writes=[bpk])
                    if rws == 128:
                        S.dma(lambda e, o_=o_, c4=c4, hp=hp, dst=dst: e.dma_start(
                            out=dst[c4 * 512:(c4 + 1) * 512, 128 * hp:128 * (hp + 1)].rearrange("(c p) x -> p c x", p=128), in_=o_[:, :, :]), reads=[bo_])
                    else:
                        S.dma(lambda e, o_=o_, hp=hp, dst=dst: e.dma_start(out=dst[LP:NT, 128 * hp:128 * (hp + 1)], in_=o_[0:32, 0, :]), reads=[bo_])
                for c4 in range(5):
                    rnd(c4)
            for seg in range(4):
                for bi, (t0, n) in enumerate(TB):
                    pb, bpb = PS[bi % 2], PB[bi % 2]
                    for kc in range(8):
                        S.add("pe", lambda e, pb=pb, kc=kc, t0=t0, n=n, seg=seg, wt=wt: e.matmul(
                            pb[:, 0:n], lhsT=wt[:, kc, seg * 128:(seg + 1) * 128], rhs=XN[:, kc, t0:t0 + n],
                            start=(kc == 0), stop=(kc == 7)), reads=[bXN, bwt], writes=[bpb])
                    if seg == 0:
                        S.add("act", lambda e, pb=pb, t0=t0, n=n: e.activation(out=QT[:, t0:t0 + n], in_=pb[:, 0:n], func=AF.Identity, scale=0.125),
                              reads=[bpb], writes=[bQT, bpb])
                    elif seg == 1:
                        S.add("act", lambda e, pb=pb, t0=t0, n=n: e.copy(out=KT[:, t0:t0 + n], in_=pb[:, 0:n]), reads=[bpb], writes=[bKT])
                        S.add("dve", lambda e, pb=pb, t0=t0, n=n: e.tensor_copy(out=Kf[:, t0:t0 + n], in_=pb[:, 0:n]), reads=[bpb], writes=[bKf, bpb])
                    elif seg == 2:
                        S.add("act", lambda e, pb=pb, t0=t0, n=n: e.copy(out=VTb[:, t0:t0 + n], in_=pb[:, 0:n]), reads=[bpb], writes=[bVTb])
                        S.add("dve", lambda e, pb=pb, t0=t0, n=n: e.tensor_copy(out=Vf[:, t0:t0 + n], in_=pb[:, 0:n]), reads=[bpb], writes=[bVf, bpb])
                    else:
                        S.add("act", lambda e, pb=pb, n=n: e.activation(out=tg[:, 0:n], in_=pb[:, 0:n], func=AF.Tanh, scale=0.5), reads=[bpb], writes=[btg])
                        S.add("dve", lambda e, pb=pb, t0=t0, n=n: e.scalar_tensor_tensor(out=SG[:, t0:t0 + n], in0=tg[:, 0:n], scalar=1.0, in1=pb[:, 0:n],
                                                                                       op0=ALU.add, op1=ALU.mult), reads=[btg, bpb], writes=[bSG, bpb])
                if seg == 1:
                    emit_kv_out(0, Kf, bKf, k_out)
                elif seg == 2:
                    emit_kv_out(1, Vf, bVf, v_out)
            pv = PS[3][:].bitcast(BF16)
            S.add("pe", lambda e, pv=pv: e.transpose(out=pv[0:32, 0:128], in_=QT[:, LP:NT], identity=IDb), reads=[bQT, bCb], writes=[PB[3]])
            S.add("act", lambda e, pv=pv, hp=hp: e.copy(out=Qs[:, 128 * hp:128 * (hp + 1)], in_=pv[0:32, 0:128]), reads=[PB[3]], writes=[bQs, PB[3]])
            for pi, (mod, vx, bvx) in enumerate(((4, Vx[1], bVx[1]), (16, Vx[2], bVx[2]))):
                for half in range(2):
                    vb = 3 if (pi * 2 + half) % 2 == 0 else 5
                    pv = PS[vb][:].bitcast(BF16)
                    bpv = PB[vb]
                    for j in range(8):
                        tl = half * 8 + j
                        if mod == 4:
                            r, cbk = tl // 4, tl % 4
                            src = VTb[:, r + 512 * cbk:r + 512 * cbk + 512:4]
                        else:
                            src = VTb[:, tl:LP:16]
                        S.add("pe", lambda e, pv=pv, j=j, src=src: e.transpose(out=pv[:, j * 128:(j + 1) * 128], in_=src, identity=IDb),
                              reads=[bVTb, bCb], writes=[bpv])
                    S.add("act", lambda e, pv=pv, vx=vx, half=half: e.copy(out=vx[:, half * 8:half * 8 + 8, 0, 0:64],
                                                                        in_=pv[:, :].rearrange("p (c x) -> p c x", x=128)[:, :, 0:64]), reads=[bpv], writes=[bvx])
                    S.add("dve", lambda e, pv=pv, vx=vx, half=half: e.tensor_copy(out=vx[:, half * 8:half * 8 + 8, 1, 64:128],
                                                                               in_=pv[:, :].rearrange("p (c x) -> p c x", x=128)[:, :, 64:128]),
                          reads=[bpv], writes=[bvx, bpv])
            bank_items = []
            bctr = [0]
            for hd in range(2):
                hs = slice(64 * hd, 64 * hd + 64)
                for R in range(4):
                    accb, baccb = PS[6 + ((hd * 4 + R) % 2)], PB[6 + ((hd * 4 + R) % 2)]
                    banks = []
                    for half in range(2):
                        s_mm, pv_mm = [], []
                        qb0 = 4 * R + 2 * half
                        lc = (qb0 - 4 * R) * 128
                        if qb0 > 0:
                            s_mm.append((0, 128, KT[hs, (qb0 - 1) * 128:qb0 * 128], QT[hs, qb0 * 128:(qb0 + 1) * 128]))
                            pv_mm.append((accb[:, lc:lc + 128], 0, 128, Vx[0][:, qb0 - 1, hd, :], bVx[0]))
                        s_mm.append((128, 256, KT[hs, qb0 * 128:(qb0 + 1) * 128], QT[hs, qb0 * 128:(qb0 + 2) * 128]))
                        pv_mm.append((accb[:, lc:lc + 256], 128, 256, Vx[0][:, qb0, hd, :], bVx[0]))
                        s_mm.append((384, 128, KT[hs, (qb0 + 1) * 128:(qb0 + 2) * 128], QT[hs, (qb0 + 1) * 128:(qb0 + 2) * 128]))
                        pv_mm.append((accb[:, lc + 128:lc + 256], 384, 128, Vx[0][:, qb0 + 1, hd, :], bVx[0]))
                        banks.append((s_mm, TM[:, hd, 0:256].unsqueeze(1).to_broadcast([128, 2, 256]), pv_mm, 256))
                    for half in range(2):
                        s_mm, pv_mm = [], []
                        for rl in range(2):
                            r = 2 * half + rl
                            qcols = QT[hs, r + 512 * R:r + 512 * R + 512:4]
                            oc = accb[:, r:512:4]
                            if R > 0:
                                s_mm.append((rl * 256, 128, KT[hs, r + 512 * (R - 1):r + 512 * R:4], qcols))
                                pv_mm.append((oc, rl * 256, 128, Vx[1][:, r * 4 + R - 1, hd, :], bVx[1]))
                            s_mm.append((rl * 256 + 128, 128, KT[hs, r + 512 * R:r + 512 * R + 512:4], qcols))
                            pv_mm.append((oc, rl * 256 + 128, 128, Vx[1][:, r * 4 + R, hd, :], bVx[1]))
                        banks.append((s_mm, TM[:, hd, 256:512].unsqueeze(1).to_broadcast([128, 2, 256]), pv_mm, 256))
                    s_mm, pv_mm = [], []
                    for r in range(16):
                        s_mm.append((r * 32, 32, KT[hs, r:LP:16], QT[hs, r + 512 * R:r + 512 * R + 512:16]))
                        pv_mm.append((accb[:, r:512:16], r * 32, 32, Vx[2][:, r, hd, :], bVx[2]))
                    banks.append((s_mm, TM[:, hd, 512 + 32 * R:544 + 32 * R].unsqueeze(1).to_broadcast([128, 16, 32]), pv_mm, 32))

                    for bk, (s_mm, mask_ap, pv_mm, bw_) in enumerate(banks):
                        i = bctr[0] % 3
                        bctr[0] += 1
                        meng = "dve" if bk == 4 else "pool"

                        def sA(i=i, s_mm=s_mm, mask_ap=mask_ap, bw_=bw_, meng=meng, bk=bk, hd=hd):
                            sb_, bsb_ = PS[3 + i], PB[3 + i]
                            p_, bp_ = Pt[i], bPt[i]
                            if bk < 4:
                                c0g = 0 if bk < 2 else 256
                                for half in range(2):
                                    S.add("pe", lambda e, half=half, c0g=c0g: e.matmul(sb_[:, 256 * half:256 * (half + 1)], lhsT=IDb, rhs=TM[:, hd, c0g:c0g + 256],
                                                                                    start=(half == 0), stop=False, skip_group_check=True),
                                          reads=[bTM, bCb], writes=[bsb_])
                                for (c0, n, lhsT, rhs) in s_mm:
                                    S.add("pe", lambda e, c0=c0, n=n, lhsT=lhsT, rhs=rhs: e.matmul(sb_[:, c0:c0 + n], lhsT=lhsT, rhs=rhs, start=False, stop=True,
                                                                                                skip_group_check=True), reads=[bKT, bQT], writes=[bsb_])
                                S.add("act", lambda e: e.activation(out=p_[:], in_=sb_[:], func=AF.Exp), reads=[bsb_], writes=[bp_, bsb_])
                            else:
                                S.add("pe", lambda e: e.matmul(sb_[:].rearrange("p (a b) -> p a b", b=bw_), lhsT=IDb, rhs=mask_ap,
                                                               start=True, stop=False, skip_group_check=True), reads=[bTM, bCb], writes=[bsb_])
                                for (c0, n, lhsT, rhs) in s_mm:
                                    S.add("pe", lambda e, c0=c0, n=n, lhsT=lhsT, rhs=rhs: e.matmul(sb_[:, c0:c0 + n], lhsT=lhsT, rhs=rhs, start=False, stop=True,
                                                                                                skip_group_check=True), reads=[bKT, bQT], writes=[bsb_])
                                S.add("act", lambda e: e.activation(out=p_[:], in_=sb_[:], func=AF.Exp), reads=[bsb_], writes=[bp_, bsb_])

                        def sB(i=i, pv_mm=pv_mm, bk=bk, accb=accb, baccb=baccb, hd=hd, R=R, hp=hp, last=(bk == len(banks) - 1)):
                            p_, bp_ = Pt[i], bPt[i]
                            for k, (oc, c0, n, lhsT, bl) in enumerate(pv_mm):
                                st = (bk == 0 and k == 0)
                                S.add("pe", lambda e, oc=oc, c0=c0, n=n, lhsT=lhsT, st=st: e.matmul(
                                    oc, lhsT=lhsT, rhs=p_[:, c0:c0 + n], start=st, stop=True, skip_group_check=True),
                                    reads=[bp_, bl], writes=[baccb])
                            if last:
                                ns = slice(64 * hd, 64 * hd + 64)
                                ds_ = slice(64 * (1 - hd), 64 * (1 - hd) + 64)
                                S.add("dve", lambda e: e.reciprocal(out=rec[ds_, :], in_=accb[ds_, :]), reads=[baccb], writes=[brec])
                                S.add("dve", lambda e: e.scalar_tensor_tensor(out=tmpo[ns, :], in0=accb[ns, :], scalar=0.5, in1=rec[ds_, :],
                                                                              op0=ALU.mult, op1=ALU.mult), reads=[baccb, brec], writes=[btmpo, baccb])
                                S.add("dve", lambda e: e.tensor_tensor(out=yaT[ns, hp, 512 * R:512 * (R + 1)], in0=tmpo[ns, :], in1=SG[ns, 512 * R:512 * (R + 1)], op=ALU.mult),
                                      reads=[btmpo, bSG], writes=[bya])
                        bank_items.append([sA, None, sB])
            pipeline(bank_items)
            S.add("pool", lambda e, hp=hp: e.tensor_copy(out=yaT[:, hp, LP:NT], in_=SG[:, LP:NT]), reads=[bSG], writes=[bya])
    S.barrier(scr)

    with ExitStack() as p4:
        FB = SB(p4, "FB", [128, 8, 36], F32); bFB = Buf()
        FN = SB(p4, "FN", [32, 32, 36], F32); bFN = Buf()
        S.dma(lambda e: e.dma_start(out=FB[:], in_=T["fbraw"].rearrange("p i g h -> p i (g h)")), writes=[bFB])
        S.dma(lambda e: e.dma_start(out=FN[:], in_=T["fnraw"].rearrange("p q g h -> p q (g h)")), writes=[bFN])
        S.add("act", lambda e: e.activation(out=FB[:], in_=FB[:], func=AF.Exp), reads=[bFB], writes=[bFB])
        S.add("act", lambda e: e.activation(out=FN[:], in_=FN[:], func=AF.Exp), reads=[bFN], writes=[bFN])
        SEL = SB(p4, "SEL", [32, 32, 128], BF16); bSEL = Buf()
        S.add("pool", lambda e: e.tensor_copy(out=SEL[:], in_=Cb[0:32, 0:32].unsqueeze(2).to_broadcast([32, 32, 128])), reads=[bCb], writes=[bSEL])
        K1 = [SB(p4, "K1_%d" % i, [128, 768], F32) for i in range(2)]; bK1 = [Buf(), Buf()]
        V1 = [SB(p4, "V1_%d" % i, [128, 768], BF16) for i in range(2)]; bV1 = [Buf(), Buf()]
        K2 = [SB(p4, "K2_%d" % i, [128, 768], F32) for i in range(2)]; bK2 = [Buf(), Buf()]
        V2 = [SB(p4, "V2_%d" % i, [128, 768], BF16) for i in range(2)]; bV2 = [Buf(), Buf()]
        K3 = [SB(p4, "K3_%d" % i, [128, 768], F32) for i in range(2)]; bK3 = [Buf(), Buf()]
        V3 = [SB(p4, "V3_%d" % i, [128, 768], BF16) for i in range(2)]; bV3 = [Buf(), Buf()]
        prod = SB(p4, "prod", [128, 3, 768], F32); bprod = Buf()
        Sc = [SB(p4, "Sc%d" % i, [128, 48], F32) for i in range(2)]; bSc = [Buf(), Buf()]
        Pc = [SB(p4, "Pc%d" % i, [128, 36], BF16) for i in range(2)]; bPc = [Buf(), Buf()]
        Pn = SB(p4, "Pn", [32, 36], F32); bPn = Buf()
        Pnb = [SB(p4, "Pnb%d" % i, [32, 12], BF16) for i in range(2)]; bPnb = [Buf(), Buf()]
        osb = SB(p4, "osb", [128, 12], F32); bosb = Buf()
        rcs = SB(p4, "rcs", [128, 12], F32); brcs = Buf()
        it = 0
        n3 = 0
        n2 = 0
        for s in range(4):
            k1, bk1, v1, bv1 = K1[s % 2], bK1[s % 2], V1[s % 2], bV1[s % 2]
            S.dma(lambda e, k1=k1, s=s: e.dma_start(out=k1[:], in_=T["cache_k"][s, 1920:2048, :]), writes=[bk1])
            S.dma(lambda e, v1=v1, s=s: e.dma_start(out=v1[:], in_=T["cache_v"][s, 1920:2048, :]), writes=[bv1], eng="pool")
            for r in range(4):
                k2, bk2, v2, bv2 = K2[n2 % 2], bK2[n2 % 2], V2[n2 % 2], bV2[n2 % 2]
                n2 += 1
                S.dma(lambda e, k2=k2, s=s, r=r: e.dma_start(out=k2[:], in_=T["cache_k"][s, 1536 + r:2048:4, :]), writes=[bk2])
                S.dma(lambda e, v2=v2, s=s, r=r: e.dma_start(out=v2[:], in_=T["cache_v"][s, 1536 + r:2048:4, :]), writes=[bv2], eng="pool")
                for i in (r, r + 4):
                    k3, bk3, v3, bv3 = K3[n3 % 2], bK3[n3 % 2], V3[n3 % 2], bV3[n3 % 2]
                    n3 += 1
                    S.dma(lambda e, k3=k3, s=s, i=i: e.dma_start(out=k3[:], in_=T["cache_k"][s, i:2048:16, :]), writes=[bk3])
                    S.dma(lambda e, v3=v3, s=s, i=i: e.dma_start(out=v3[:], in_=T["cache_v"][s, i:2048:16, :]), writes=[bv3], eng="pool")
                    col = s * 8 + i
                    sc_, bsc_ = Sc[it % 2], bSc[it % 2]
                    pc_, bpc_ = Pc[it % 2], bPc[it % 2]
                    pnb_, bpnb_ = Pnb[it % 2], bPnb[it % 2]
                    ob, bob = PS[2 + 2 * (it % 2)], PB[2 + 2 * (it % 2)]
                    db, bdb = PS[3 + 2 * (it % 2)], PB[3 + 2 * (it % 2)]
                    it += 1
                    ktiles = ((0, k1, bk1), (1, k2, bk2), (2, k3, bk3))
                    vtiles = ((0, v1, bv1), (1, v2, bv2), (2, v3, bv3))
                    S.add("pe", lambda e, col=col: e.matmul(PS[0][:, 0:512], lhsT=SEL[:, col, :], rhs=Qs[:, 0:512], start=True, stop=True), reads=[bSEL, bQs], writes=[PB[0]])
                    S.add("pe", lambda e, col=col: e.matmul(PS[1][:, 0:256], lhsT=SEL[:, col, :], rhs=Qs[:, 512:768], start=True, stop=True), reads=[bSEL, bQs], writes=[PB[1]])
                    for (gi, kt_, bkt_) in ktiles:
                        S.add("dve", lambda e, gi=gi, kt_=kt_: e.tensor_tensor(out=prod[:, gi, 0:512], in0=kt_[:, 0:512], in1=PS[0][:, 0:512], op=ALU.mult),
                              reads=[bkt_, PB[0]], writes=[bprod])
                        S.add("dve", lambda e, gi=gi, kt_=kt_: e.tensor_tensor(out=prod[:, gi, 512:768], in0=kt_[:, 512:768], in1=PS[1][:, 0:256], op=ALU.mult),
                              reads=[bkt_, PB[1]], writes=[bprod])
                    S.add("dve", lambda e, sc_=sc_: e.tensor_reduce(out=sc_[:, 0:36], in_=prod[:].rearrange("p g (h x) -> p (g h) x", x=64), axis=mybir.AxisListType.X, op=ALU.add),
                          reads=[bprod], writes=[bsc_])
                    S.add("dve", lambda e: e.tensor_tensor(out=prod[0:32, 0, 0:512], in0=Ks[:, 0:512], in1=PS[0][0:32, 0:512], op=ALU.mult), reads=[bKs, PB[0]], writes=[bprod, PB[0]])
                    S.add("dve", lambda e: e.tensor_tensor(out=prod[0:32, 0, 512:768], in0=Ks[:, 512:768], in1=PS[1][0:32, 0:256], op=ALU.mult), reads=[bKs, PB[1]], writes=[bprod, PB[1]])
                    S.add("dve", lambda e, sc_=sc_: e.tensor_reduce(out=sc_[0:32, 36:48], in_=prod[0:32, 0, :].rearrange("p (h x) -> p h x", x=64), axis=mybir.AxisListType.X, op=ALU.add),
                          reads=[bprod], writes=[bsc_])
                    S.add("act", lambda e, sc_=sc_: e.activation(out=sc_[:], in_=sc_[:], func=AF.Exp), reads=[bsc_], writes=[bsc_])
                    S.add("pool", lambda e, sc_=sc_, pc_=pc_, i=i: e.tensor_tensor(out=pc_[:], in0=sc_[:, 0:36], in1=FB[:, i, :], op=ALU.mult), reads=[bsc_, bFB], writes=[bpc_])
                    S.add("pool", lambda e, sc_=sc_, col=col: e.tensor_tensor(out=Pn[:].rearrange("p (g h) -> p g h", g=3), in0=sc_[0:32, 36:48].unsqueeze(1).to_broadcast([32, 3, 12]),
                                                                            in1=FN[:, col, :].rearrange("p (g h) -> p g h", g=3), op=ALU.mult), reads=[bsc_, bFN], writes=[bPn])
                    S.add("pool", lambda e: e.tensor_tensor(out=Pn[:, 0:12], in0=Pn[:, 0:12], in1=Pn[:, 12:24], op=ALU.add), reads=[], writes=[bPn])
                    S.add("pool", lambda e, pnb_=pnb_: e.tensor_tensor(out=pnb_[:], in0=Pn[:, 0:12], in1=Pn[:, 24:36], op=ALU.add), reads=[bPn], writes=[bpnb_])
                    for hp in range(6):
                        for (gi, vt_, bvt_) in vtiles:
                            S.add("pe", lambda e, ob=ob, vt_=vt_, pc_=pc_, hp=hp, gi=gi: e.matmul(ob[:, 2 * hp:2 * hp + 2], lhsT=vt_[:, 128 * hp:128 * (hp + 1)],
                                                                                          rhs=pc_[:, gi * 12 + 2 * hp:gi * 12 + 2 * hp + 2], start=(gi == 0), stop=False),
                                  reads=[bvt_, bpc_], writes=[bob])
                        S.add("pe", lambda e, ob=ob, pnb_=pnb_, hp=hp: e.matmul(ob[:, 2 * hp:2 * hp + 2], lhsT=Vs[:, 128 * hp:128 * (hp + 1)], rhs=pnb_[:, 2 * hp:2 * hp + 2],
                                                                             start=False, stop=True), reads=[bVs, bpnb_], writes=[bob])
                    for gi in range(3):
                        S.add("pe", lambda e, db=db, pc_=pc_, gi=gi: e.matmul(db[:, 0:12], lhsT=ONESb, rhs=pc_[:, gi * 12:gi * 12 + 12], start=(gi == 0), stop=False),
                              reads=[bCb, bpc_], writes=[bdb])
                    S.add("pe", lambda e, db=db, pnb_=pnb_: e.matmul(db[:, 0:12], lhsT=ONESb[0:32, :], rhs=pnb_[:], start=False, stop=True), reads=[bCb, bpnb_], writes=[bdb])
                    S.add("dve", lambda e, db=db: e.reciprocal(out=rcs[:], in_=db[:, 0:12]), reads=[bdb], writes=[brcs, bdb])
                    S.add("dve", lambda e, ob=ob: e.scalar_tensor_tensor(out=osb[:], in0=ob[:, 0:12], scalar=0.5, in1=rcs[:], op0=ALU.mult, op1=ALU.mult),
                          reads=[bob, brcs], writes=[bosb, bob])
                    cc = LP + col
                    S.add("dve", lambda e, cc=cc: e.tensor_tensor(out=yaT[0:64, :, cc:cc + 1], in0=yaT[0:64, :, cc:cc + 1], in1=osb[0:64, 0:12:2].unsqueeze(2), op=ALU.mult),
                          reads=[bosb], writes=[bya])
                    S.add("dve", lambda e, cc=cc: e.tensor_tensor(out=yaT[64:128, :, cc:cc + 1], in0=yaT[64:128, :, cc:cc + 1], in1=osb[64:128, 1:12:2].unsqueeze(2), op=ALU.mult),
                          reads=[bosb], writes=[bya])
    S.barrier(scr)

    with ExitStack() as p5:
        mT = SB(p5, "mT", [128, 8, NT], BF16); bmT = Buf()
        p5a = ExitStack()
        wd = [SB(p5a, "wd%d" % i, [128, 38, 128], BF16) for i in range(2)]; bwd = [Buf(), Buf()]
        sa = SB(p5a, "sa", [128, 512], F32); bsa = Buf()
        sbb = SB(p5a, "sbb", [128, 512], F32); bsbb = Buf()
        m1 = SB(p5a, "m1", [128, 512], F32); bm1 = Buf()
        m2 = SB(p5a, "m2", [128, 512], F32); bm2 = Buf()
        w_a_v = w_a.rearrange("(kc p) n -> p kc n", p=128)
        w_b_v = w_b.rearrange("(kc p) n -> p kc n", p=128)
        w_o_v = w_o.rearrange("(kc p) n -> p kc n", p=128)
        for dti in range(8):
            w_, bw_ = wd[dti % 2], bwd[dti % 2]
            cs = slice(128 * dti, 128 * (dti + 1))
            S.dma(lambda e, w_=w_, cs=cs: e.dma_start(out=w_[:, 0:6, :], in_=w_a_v[:, :, cs]), writes=[bw_], eng="pool")
            S.dma(lambda e, w_=w_, cs=cs: e.dma_start(out=w_[:, 6:22, :], in_=w_b_v[:, :, cs]), writes=[bw_], eng="pool")
            S.dma(lambda e, w_=w_, dti=dti: e.dma_start(out=w_[:, 22:30, :], in_=w_in_v[:, :, 9248 + 128 * dti:9248 + 128 * (dti + 1)]), writes=[bw_], eng="pool")
            S.dma(lambda e, w_=w_, dti=dti: e.dma_start(out=w_[:, 30:38, :], in_=w_in_v[:, :, 10272 + 128 * dti:10272 + 128 * (dti + 1)]), writes=[bw_], eng="pool")
            for bi, (t0, n) in enumerate(TB):
                o4 = 4 * (bi % 2)
                pA, pBk, pga, pgb = PS[o4], PS[o4 + 1], PS[o4 + 2], PS[o4 + 3]
                bA, bBk, bga, bgb = PB[o4], PB[o4 + 1], PB[o4 + 2], PB[o4 + 3]
                for kc in range(6):
                    S.add("pe", lambda e, pA=pA, w_=w_, kc=kc, t0=t0, n=n: e.matmul(pA[:, 0:n], lhsT=w_[:, kc, :], rhs=yaT[:, kc, t0:t0 + n], start=(kc == 0), stop=(kc == 5)),
                          reads=[bw_, bya], writes=[bA])
                for kc in range(16):
                    S.add("pe", lambda e, pBk=pBk, w_=w_, kc=kc, t0=t0, n=n: e.matmul(pBk[:, 0:n], lhsT=w_[:, 6 + kc, :], rhs=ysT[:, kc, t0:t0 + n], start=(kc == 0), stop=(kc == 15)),
                          reads=[bw_, bys], writes=[bBk])
                for kc in range(8):
                    S.add("pe", lambda e, pga=pga, w_=w_, kc=kc, t0=t0, n=n: e.matmul(pga[:, 0:n], lhsT=w_[:, 22 + kc, :], rhs=XN[:, kc, t0:t0 + n], start=(kc == 0), stop=(kc == 7)),
                          reads=[bw_, bXN], writes=[bga])
                for kc in range(8):
                    S.add("pe", lambda e, pgb=pgb, w_=w_, kc=kc, t0=t0, n=n: e.matmul(pgb[:, 0:n], lhsT=w_[:, 30 + kc, :], rhs=XN[:, kc, t0:t0 + n], start=(kc == 0), stop=(kc == 7)),
                          reads=[bw_, bXN], writes=[bgb])
                S.add("act", lambda e, pga=pga, n=n: e.activation(out=sa[:, 0:n], in_=pga[:, 0:n], func=AF.Tanh, scale=0.5), reads=[bga], writes=[bsa, bga])
                S.add("act", lambda e, pgb=pgb, n=n: e.activation(out=sbb[:, 0:n], in_=pgb[:, 0:n], func=AF.Tanh, scale=0.5), reads=[bgb], writes=[bsbb, bgb])
                S.add("dve", lambda e, pA=pA, n=n: e.scalar_tensor_tensor(out=m1[:, 0:n], in0=sa[:, 0:n], scalar=1.0, in1=pA[:, 0:n], op0=ALU.add, op1=ALU.mult),
                      reads=[bsa, bA], writes=[bm1, bA])
                S.add("dve", lambda e, pBk=pBk, n=n: e.scalar_tensor_tensor(out=m2[:, 0:n], in0=sbb[:, 0:n], scalar=1.0, in1=pBk[:, 0:n], op0=ALU.add, op1=ALU.mult),
                      reads=[bsbb, bBk], writes=[bm2, bBk])
                S.add("dve", lambda e, n=n: e.tensor_tensor(out=m1[:, 0:n], in0=m1[:, 0:n], in1=m2[:, 0:n], op=ALU.add), reads=[bm2], writes=[bm1])
                S.add("act", lambda e, dti=dti, t0=t0, n=n: e.activation(out=mT[:, dti, t0:t0 + n], in_=m1[:, 0:n], func=AF.Identity, scale=0.5),
                      reads=[bm1], writes=[bmT])
        S.barrier(scr)
        p5a.close()
        wo = SB(p5, "wo", [128, 8, D], BF16); bwo = Buf()
        S.dma(lambda e: e.dma_start(out=wo[:, :, 0:512], in_=w_o_v[:, :, 0:512]), writes=[bwo], eng="pool")
        S.dma(lambda e: e.dma_start(out=wo[:, :, 512:1024], in_=w_o_v[:, :, 512:1024]), writes=[bwo], eng="pool")
        fg = SB(p5, "fg", [128, D], F32); bfg = Buf()
        S.dma(lambda e: e.dma_start(out=fg[:], in_=final_g.to_broadcast([128, D])), writes=[bfg])
        xr = [SB(p5, "xr%d" % i, [128, D], F32) for i in range(2)]; bxr = [Buf(), Buf()]
        yo = xr; byo = bxr
        jk2 = SB(p5, "jk2", [128, D], BF16); bjk2 = Buf()
        fs = SB(p5, "fs", [128, 4], F32); bfs = Buf()
        for ti, (t0, rows) in enumerate(TT):
            x_, bx_ = xr[ti % 2], bxr[ti % 2]
            y_, by_ = yo[ti % 2], byo[ti % 2]
            S.dma(lambda e, x_=x_, t0=t0, rows=rows: e.dma_start(out=x_[0:rows, :], in_=x_all[t0:t0 + rows, :]), writes=[bx_])
            for hf in range(2):
                pb, bpb = PS[2 * (ti % 2) + hf], PB[2 * (ti % 2) + hf]
                for kc in range(8):
                    S.add("pe", lambda e, pb=pb, kc=kc, t0=t0, rows=rows, hf=hf: e.matmul(pb[0:rows, :], lhsT=mT[:, kc, t0:t0 + rows], rhs=wo[:, kc, 512 * hf:512 * (hf + 1)],
                                                                                      start=(kc == 0), stop=(kc == 7)), reads=[bmT, bwo], writes=[bpb])
                S.add("dve", lambda e, pb=pb, x_=x_, hf=hf, rows=rows: e.tensor_tensor(out=x_[0:rows, 512 * hf:512 * (hf + 1)], in0=x_[0:rows, 512 * hf:512 * (hf + 1)], in1=pb[0:rows, :], op=ALU.add),
                      reads=[bpb], writes=[bx_, bpb])
            S.add("dve", lambda e: e.memset(fs[:, 0:1], 0.0), writes=[bfs])
            S.add("act", lambda e, x_=x_, rows=rows: e.activation(out=jk2[0:rows, :], in_=x_[0:rows, :], func=AF.Square, accum_out=fs[0:rows, 0:1]), reads=[bx_], writes=[bjk2, bfs])
            S.add("act", lambda e, rows=rows: e.activation(out=fs[0:rows, 1:2], in_=fs[0:rows, 0:1], func=AF.Sqrt, scale=1.0 / D, bias=1e-6), reads=[bfs], writes=[bfs])
            S.add("dve", lambda e, rows=rows: e.reciprocal(out=fs[0:rows, 2:3], in_=fs[0:rows, 1:2]), reads=[bfs], writes=[bfs])
            S.add("dve", lambda e, x_=x_, y_=y_, rows=rows: e.scalar_tensor_tensor(out=y_[0:rows, :], in0=x_[0:rows, :], scalar=fs[0:rows, 2:3], in1=fg[0:rows, :], op0=ALU.mult, op1=ALU.mult),
                  reads=[bx_, bfs, bfg], writes=[by_])
            S.dma(lambda e, y_=y_, t0=t0, rows=rows: e.dma_start(out=y_out[t0:t0 + rows, :], in_=y_[0:rows, :]), reads=[by_])


_PROG = None


def kernel(x_prompt, x_sample, cache_k, cache_v, state_conv, state_ssm,
           norm_g, w_in, conv_w, conv_b, dt_bias, a_log, d_skip, ssm_norm,
           w_branch_a, w_branch_b, w_out, rel_bias, final_norm):
    global _PROG
    f = np.float32
    asf = lambda a: np.ascontiguousarray(np.asarray(a, dtype=f))
    x_prompt, x_sample = asf(x_prompt), asf(x_sample)
    cache_k, cache_v = np.asarray(cache_k, dtype=f), np.asarray(cache_v, dtype=f)
    state_conv, state_ssm = asf(state_conv), asf(state_ssm)
    rel_bias = asf(rel_bias)
    tb_i, fb_i, fn_i = _static_index_tables()
    rb_ext = np.concatenate([rel_bias, np.full((1, 12), NEG, f)], axis=0)
    tbraw = np.ascontiguousarray(rb_ext[tb_i].transpose(0, 2, 1))
    fbraw = np.ascontiguousarray(rb_ext[fb_i])
    fnraw = np.ascontiguousarray(rb_ext[fn_i])
    cst = _const_pack()
    cwT = np.ascontiguousarray(asf(conv_w)[0].reshape(4, 32, 128).transpose(2, 1, 0))
    cbT = np.ascontiguousarray(asf(conv_b)[0].reshape(32, 128).T)
    hpar = np.concatenate([asf(dt_bias)[0], asf(a_log)[0], asf(d_skip)[0]])[None, :]
    common = dict(w_in=asf(w_in)[0], w_a=asf(w_branch_a)[0], w_b=asf(w_branch_b)[0], w_o=asf(w_out)[0],
                  norm_g=asf(norm_g), final_g=asf(final_norm)[None, :], ssm_norm=asf(ssm_norm),
                  cwT=cwT, cbT=cbT, cb_row=asf(conv_b), hpar=np.ascontiguousarray(hpar), tbraw=tbraw, fbraw=fbraw, fnraw=fnraw, cst=cst)
    in_maps = []
    for c in range(NCORES):
        sl = slice(4 * c, 4 * c + 4)
        m = dict(common)
        m["x_all"] = np.ascontiguousarray(np.concatenate([x_prompt[c], x_sample[sl].reshape(NS, D)], axis=0))
        m["cache_k"] = np.ascontiguousarray(cache_k[0, sl].reshape(4, 2048, 768))
        m["cache_v"] = np.ascontiguousarray(cache_v[0, sl].reshape(4, 2048, 768))
        sc = state_conv[0, sl]
        m["scT"] = np.ascontiguousarray(sc.reshape(4, 3, 32, 128).transpose(3, 2, 0, 1))
        st = state_ssm[0, sl]
        m["st_nat"] = np.ascontiguousarray(st)
        m["st_T"] = np.ascontiguousarray(st.reshape(4, 2048, 128).transpose(2, 0, 1))
        in_maps.append(m)
    if _PROG is None:
        _PROG = build_program()
    res = run_bass_kernel_spmd(_PROG, in_maps, core_ids=list(range(NCORES)))
    R = res.results
    y_p = np.stack([R[c]["y_out"][:LP] for c in range(NCORES)])
    y_s = np.concatenate([R[c]["y_out"][LP:].reshape(4, 8, D) for c in range(NCORES)])
    k_p = np.stack([R[c]["k_out"][:LP].reshape(LP, 12, 64) for c in range(NCORES)])[None]
    v_p = np.stack([R[c]["v_out"][:LP].reshape(LP, 12, 64) for c in range(NCORES)])[None]
    k_s = np.concatenate([R[c]["k_out"][LP:].reshape(4, 8, 12, 64) for c in range(NCORES)])[None]
    v_s = np.concatenate([R[c]["v_out"][LP:].reshape(4, 8, 12, 64) for c in range(NCORES)])[None]
    c_p = np.stack([R[c]["conv_out"][0:3] for c in range(NCORES)])[None]
    c_s = np.concatenate([R[c]["conv_out"][3:15].reshape(4, 3, 4096) for c in range(NCORES)])[None]
    s_p = np.stack([R[c]["ssm_p"] for c in range(NCORES)])[None]
    s_s = np.concatenate([R[c]["ssm_s"] for c in range(NCORES)])[None]
    outs = (y_p, y_s, k_p, v_p, c_p, s_p, k_s, v_s, c_s, s_s)
    return tuple(np.ascontiguousarray(o.astype(np.float32)) for o in outs)
```
